# Optimizing a Trainium2 kernel written in Bass

```python
import math
import jax, jax.numpy as jnp
from jax import lax
import numpy as np

D_MODEL = 1024
BATCH = 2
SEQ = 8192
DEPTH = 1
DEC_BATCH = 32
DEC_SEQ = 1
PAST_LEN = 16384
PAGE_SIZE = 128

HEAD_DIM = 64
HEADS_PER_GROUP = 4
ATTN_GROUPS = ((128, 1), (512, 4), (2048, 16))
N_ATTN_GROUPS = len(ATTN_GROUPS)
ATTN_WIDTH = N_ATTN_GROUPS * HEADS_PER_GROUP * HEAD_DIM
ATTN_OUT_WIDTH = HEADS_PER_GROUP * HEAD_DIM
SSM_CH_PER_GROUP = 16
SSM_STATE = 64
SSM_WIDTH = D_MODEL // 2
SSM_GROUPS = SSM_WIDTH // SSM_CH_PER_GROUP
D_FF = 2816
DT_MIN = 0.01
DT_MAX = 0.1
RMS_EPS = 1e-6
Q_BLOCK = 128
COL_Q = SSM_WIDTH
COL_K = COL_Q + ATTN_WIDTH
COL_V = COL_K + ATTN_WIDTH
COL_GATE_SSM = COL_V + ATTN_WIDTH
COL_GATE_ATTN = COL_GATE_SSM + D_MODEL
IN_WIDTH = COL_GATE_ATTN + D_MODEL

kernel_name = "gated_s5_dilated_attn_macaron_step"


def rms_norm(x, g):
    xf = x.astype(jnp.float32)
    y = xf * lax.rsqrt(jnp.mean(xf * xf, axis=-1, keepdims=True) + RMS_EPS)
    return (y * g.astype(jnp.float32)).astype(x.dtype)


def head_rms_norm(t, g):
    tf = t.astype(jnp.float32)
    y = tf * lax.rsqrt(jnp.mean(tf * tf, axis=-1, keepdims=True) + RMS_EPS)
    return (y * g.astype(jnp.float32)[:, None, :]).astype(t.dtype)


def swiglu(x, w_gate, w_up, w_down):
    return (jax.nn.silu(x @ w_gate) * (x @ w_up)) @ w_down


def ssm_discretize(a_re, a_im, log_dt, b_re, b_im):
    f32 = jnp.float32
    a_re = a_re.astype(f32)
    a_im = a_im.astype(f32)
    dt = jnp.exp(log_dt.astype(f32))[:, None]
    mag = jnp.exp(a_re * dt)
    ab_re = mag * jnp.cos(a_im * dt)
    ab_im = mag * jnp.sin(a_im * dt)
    inv = 1.0 / (a_re * a_re + a_im * a_im)
    f_re = ((ab_re - 1.0) * a_re + ab_im * a_im) * inv
    f_im = (ab_im * a_re - (ab_re - 1.0) * a_im) * inv
    b_re = b_re.astype(f32)
    b_im = b_im.astype(f32)
    bb_re = f_re[..., None] * b_re - f_im[..., None] * b_im
    bb_im = f_re[..., None] * b_im + f_im[..., None] * b_re
    return ab_re, ab_im, bb_re, bb_im


def complex_affine_combine(e1, e2):
    a1r, a1i, b1r, b1i = e1
    a2r, a2i, b2r, b2i = e2
    return (a2r * a1r - a2i * a1i,
            a2r * a1i + a2i * a1r,
            a2r * b1r - a2i * b1i + b2r,
            a2r * b1i + a2i * b1r + b2i)


def s5_branch(u, h0_re, h0_im, a_re, a_im, log_dt, b_re, b_im, c_re, c_im, d_skip, w_glu, b_glu):
    f32 = jnp.float32
    nb, seq_len, _ = u.shape
    uf = u.astype(f32)
    ab_re, ab_im, bb_re, bb_im = ssm_discretize(a_re, a_im, log_dt, b_re, b_im)
    ug = uf.reshape(nb, seq_len, SSM_GROUPS, SSM_CH_PER_GROUP)
    x_re = jnp.einsum("blgc,gpc->blgp", ug, bb_re)
    x_im = jnp.einsum("blgc,gpc->blgp", ug, bb_im)
    h0_re = h0_re.astype(f32)
    h0_im = h0_im.astype(f32)
    x_re = x_re.at[:, 0].add(ab_re * h0_re - ab_im * h0_im)
    x_im = x_im.at[:, 0].add(ab_re * h0_im + ab_im * h0_re)
    a_re_t = jnp.broadcast_to(ab_re, x_re.shape)
    a_im_t = jnp.broadcast_to(ab_im, x_im.shape)
    _, _, s_re, s_im = lax.associative_scan(
        complex_affine_combine, (a_re_t, a_im_t, x_re, x_im), axis=1)
    y = (jnp.einsum("blgp,gcp->blgc", s_re, c_re.astype(f32))
         - jnp.einsum("blgp,gcp->blgc", s_im, c_im.astype(f32)))
    y = y.reshape(nb, seq_len, SSM_WIDTH) + d_skip.astype(f32) * uf
    y = jax.nn.gelu(y)
    y = y * jax.nn.sigmoid(y @ w_glu.astype(f32) + b_glu.astype(f32))
    return y.astype(u.dtype), s_re[:, -1], s_im[:, -1]


def dilated_group_attention(q, k_ext, v_ext, pos0, window, dilation):
    nb, lq, nh, hd = q.shape
    qb = Q_BLOCK if lq % Q_BLOCK == 0 else lq
    n_blk = lq // qb
    n_keys = window // dilation + 1
    rel = jnp.arange(qb)[:, None] + window - dilation * jnp.arange(n_keys)[None, :]
    scale = HEAD_DIM ** -0.5
    qf = q.astype(jnp.float32)

    def one_block(blk):
        start = blk * qb
        qs = lax.dynamic_slice_in_dim(qf, start, qb, axis=1)
        ks = lax.dynamic_slice_in_dim(k_ext, start, qb + window, axis=1)
        vs = lax.dynamic_slice_in_dim(v_ext, start, qb + window, axis=1)
        kg = ks[:, rel].astype(jnp.float32)
        vg = vs[:, rel].astype(jnp.float32)
        s = jnp.einsum("bqhd,bqjhd->bqhj", qs, kg) * scale
        valid = (pos0 - window + start + rel) >= 0
        s = jnp.where(valid[None, :, None, :], s, -jnp.inf)
        m = jnp.max(s, axis=-1)
        p = jnp.exp(s - m[..., None])
        den = jnp.sum(p, axis=-1)
        o = jnp.einsum("bqhj,bqjhd->bqhd", p, vg) / den[..., None]
        return o, m, den

    o, m, den = lax.map(one_block, jnp.arange(n_blk))
    o = jnp.moveaxis(o, 0, 1).reshape(nb, lq, nh, hd)
    m = jnp.moveaxis(m, 0, 1).reshape(nb, lq, nh)
    den = jnp.moveaxis(den, 0, 1).reshape(nb, lq, nh)
    return o, m, den


def dilated_attention_branch(q, k, v, kv_prev, pos0):
    nb, lq = q.shape[0], q.shape[1]
    outs, maxes, dens, new_bufs = [], [], [], []
    for g, (window, dilation) in enumerate(ATTN_GROUPS):
        prev = kv_prev[g].astype(k.dtype)
        n_prev = prev.shape[1]
        pad = jnp.zeros((nb, window - n_prev, HEADS_PER_GROUP, HEAD_DIM), k.dtype)
        kg = k[:, :, g]
        vg = v[:, :, g]
        k_ext = jnp.concatenate([pad, prev[:, :, 0], kg], axis=1)
        v_ext = jnp.concatenate([pad, prev[:, :, 1], vg], axis=1)
        o, m, den = dilated_group_attention(q[:, :, g], k_ext, v_ext, pos0, window, dilation)
        outs.append(o)
        maxes.append(m)
        dens.append(den)
        kv_all = jnp.concatenate([prev, jnp.stack([kg, vg], axis=2)], axis=1)
        keep = min(window, pos0 + lq)
        new_bufs.append(kv_all[:, kv_all.shape[1] - keep:])
    m_all = jnp.stack(maxes)
    den_all = jnp.stack(dens)
    o_all = jnp.stack(outs)
    m_top = jnp.max(m_all, axis=0)
    wts = den_all * jnp.exp(m_all - m_top[None])
    out = jnp.sum(wts[..., None] * o_all, axis=0) / jnp.sum(wts, axis=0)[..., None]
    return out.reshape(nb, lq, ATTN_OUT_WIDTH).astype(q.dtype), tuple(new_bufs)


def trunk_layer(x, pos0, h0_re, h0_im, kv_prev, w):
    nb, seq_len, _ = x.shape
    x = x + 0.5 * swiglu(rms_norm(x, w["g_ffn1"]), w["w1_gate"], w["w1_up"], w["w1_down"])
    h = rms_norm(x, w["g_mix"])
    proj = h @ w["w_in"]
    heads = (nb, seq_len, N_ATTN_GROUPS, HEADS_PER_GROUP, HEAD_DIM)
    u = proj[..., :COL_Q]
    q = head_rms_norm(proj[..., COL_Q:COL_K].reshape(heads), w["g_q"])
    k = head_rms_norm(proj[..., COL_K:COL_V].reshape(heads), w["g_k"])
    v = proj[..., COL_V:COL_GATE_SSM].reshape(heads)
    gate_ssm = jax.nn.sigmoid(proj[..., COL_GATE_SSM:COL_GATE_ATTN])
    gate_attn = jax.nn.sigmoid(proj[..., COL_GATE_ATTN:])
    y_ssm, hT_re, hT_im = s5_branch(u, h0_re, h0_im, w["ssm_a_re"], w["ssm_a_im"], w["ssm_log_dt"],
                                    w["ssm_b_re"], w["ssm_b_im"], w["ssm_c_re"], w["ssm_c_im"],
                                    w["ssm_d"], w["w_glu"], w["b_glu"])
    y_attn, kv_new = dilated_attention_branch(q, k, v, kv_prev, pos0)
    mixed = gate_ssm * (y_ssm @ w["w_ssm_proj"]) + gate_attn * (y_attn @ w["w_attn_proj"])
    x = x + mixed @ w["w_o"]
    x = x + 0.5 * swiglu(rms_norm(x, w["g_ffn2"]), w["w2_gate"], w["w2_up"], w["w2_down"])
    return x, hT_re, hT_im, kv_new


def setup_inputs(seed: int = 0) -> dict:
    key = jax.random.key(seed)
    keys = iter(jax.random.split(key, 40))
    f32 = jnp.float32

    def nrm(shape, scale=1.0):
        return jax.random.normal(next(keys), shape, f32) * scale

    def gain(shape):
        return 1.0 + nrm(shape, 0.02)

    L = DEPTH
    inp = {}
    inp["x_prompt"] = nrm((BATCH, SEQ, D_MODEL))
    inp["x_sample"] = nrm((DEC_BATCH, DEC_SEQ, D_MODEL))
    for window, _ in ATTN_GROUPS:
        inp["cache_kv_w%d" % window] = nrm(
            (L, DEC_BATCH, min(window, PAST_LEN), 2, HEADS_PER_GROUP, HEAD_DIM))
    inp["state_ssm_re"] = nrm((L, DEC_BATCH, SSM_GROUPS, SSM_STATE), 0.1)
    inp["state_ssm_im"] = nrm((L, DEC_BATCH, SSM_GROUPS, SSM_STATE), 0.1)
    inp["g_ffn1"] = gain((L, D_MODEL))
    inp["w1_gate"] = nrm((L, D_MODEL, D_FF), D_MODEL ** -0.5)
    inp["w1_up"] = nrm((L, D_MODEL, D_FF), D_MODEL ** -0.5)
    inp["w1_down"] = nrm((L, D_FF, D_MODEL), D_FF ** -0.5)
    inp["g_mix"] = gain((L, D_MODEL))
    inp["w_in"] = nrm((L, D_MODEL, IN_WIDTH), D_MODEL ** -0.5)
    inp["g_q"] = gain((L, N_ATTN_GROUPS, HEAD_DIM))
    inp["g_k"] = gain((L, N_ATTN_GROUPS, HEAD_DIM))
    inp["ssm_a_re"] = -0.5 + nrm((L, SSM_GROUPS, SSM_STATE), 0.01)
    inp["ssm_a_im"] = (jnp.pi * jnp.arange(SSM_STATE, dtype=f32)
                       + nrm((L, SSM_GROUPS, SSM_STATE), 0.01))
    inp["ssm_log_dt"] = jax.random.uniform(next(keys), (L, SSM_GROUPS), f32,
                                           math.log(DT_MIN), math.log(DT_MAX))
    inp["ssm_b_re"] = nrm((L, SSM_GROUPS, SSM_STATE, SSM_CH_PER_GROUP), (2 * SSM_CH_PER_GROUP) ** -0.5)
    inp["ssm_b_im"] = nrm((L, SSM_GROUPS, SSM_STATE, SSM_CH_PER_GROUP), (2 * SSM_CH_PER_GROUP) ** -0.5)
    inp["ssm_c_re"] = nrm((L, SSM_GROUPS, SSM_CH_PER_GROUP, SSM_STATE), SSM_STATE ** -0.5)
    inp["ssm_c_im"] = nrm((L, SSM_GROUPS, SSM_CH_PER_GROUP, SSM_STATE), SSM_STATE ** -0.5)
    inp["ssm_d"] = nrm((L, SSM_WIDTH))
    inp["w_glu"] = nrm((L, SSM_WIDTH, SSM_WIDTH), SSM_WIDTH ** -0.5)
    inp["b_glu"] = nrm((L, SSM_WIDTH), 0.02)
    inp["w_ssm_proj"] = nrm((L, SSM_WIDTH, D_MODEL), SSM_WIDTH ** -0.5)
    inp["w_attn_proj"] = nrm((L, ATTN_OUT_WIDTH, D_MODEL), ATTN_OUT_WIDTH ** -0.5)
    inp["w_o"] = nrm((L, D_MODEL, D_MODEL), D_MODEL ** -0.5)
    inp["g_ffn2"] = gain((L, D_MODEL))
    inp["w2_gate"] = nrm((L, D_MODEL, D_FF), D_MODEL ** -0.5)
    inp["w2_up"] = nrm((L, D_MODEL, D_FF), D_MODEL ** -0.5)
    inp["w2_down"] = nrm((L, D_FF, D_MODEL), D_FF ** -0.5)
    return inp


def reference(x_prompt, x_sample, cache_kv_w128, cache_kv_w512, cache_kv_w2048,
              state_ssm_re, state_ssm_im,
              g_ffn1, w1_gate, w1_up, w1_down, g_mix, w_in, g_q, g_k,
              ssm_a_re, ssm_a_im, ssm_log_dt, ssm_b_re, ssm_b_im, ssm_c_re, ssm_c_im,
              ssm_d, w_glu, b_glu, w_ssm_proj, w_attn_proj, w_o,
              g_ffn2, w2_gate, w2_up, w2_down):
    caches = (cache_kv_w128, cache_kv_w512, cache_kv_w2048)
    nb_p = x_prompt.shape[0]
    y_p = x_prompt
    y_s = x_sample
    p_kv = ([], [], [])
    s_kv = ([], [], [])
    p_re, p_im, s_re, s_im = [], [], [], []
    for layer in range(DEPTH):
        w = {
            "g_ffn1": g_ffn1[layer], "w1_gate": w1_gate[layer], "w1_up": w1_up[layer],
            "w1_down": w1_down[layer], "g_mix": g_mix[layer], "w_in": w_in[layer],
            "g_q": g_q[layer], "g_k": g_k[layer],
            "ssm_a_re": ssm_a_re[layer], "ssm_a_im": ssm_a_im[layer],
            "ssm_log_dt": ssm_log_dt[layer], "ssm_b_re": ssm_b_re[layer],
            "ssm_b_im": ssm_b_im[layer], "ssm_c_re": ssm_c_re[layer], "ssm_c_im": ssm_c_im[layer],
            "ssm_d": ssm_d[layer], "w_glu": w_glu[layer], "b_glu": b_glu[layer],
            "w_ssm_proj": w_ssm_proj[layer], "w_attn_proj": w_attn_proj[layer], "w_o": w_o[layer],
            "g_ffn2": g_ffn2[layer], "w2_gate": w2_gate[layer], "w2_up": w2_up[layer],
            "w2_down": w2_down[layer],
        }
        empty_kv = tuple(jnp.zeros((nb_p, 0, 2, HEADS_PER_GROUP, HEAD_DIM), x_prompt.dtype)
                         for _ in ATTN_GROUPS)
        h0 = jnp.zeros((nb_p, SSM_GROUPS, SSM_STATE), jnp.float32)
        y_p, hp_re, hp_im, kvp = trunk_layer(y_p, 0, h0, h0, empty_kv, w)
        y_s, hs_re, hs_im, kvs = trunk_layer(y_s, PAST_LEN, state_ssm_re[layer], state_ssm_im[layer],
                                             tuple(c[layer] for c in caches), w)
        for g in range(N_ATTN_GROUPS):
            p_kv[g].append(kvp[g])
            s_kv[g].append(kvs[g])
        p_re.append(hp_re)
        p_im.append(hp_im)
        s_re.append(hs_re)
        s_im.append(hs_im)
    sdt = state_ssm_re.dtype
    return (y_p, y_s,
            jnp.stack(p_kv[0]), jnp.stack(p_kv[1]), jnp.stack(p_kv[2]),
            jnp.stack(p_re).astype(sdt), jnp.stack(p_im).astype(sdt),
            jnp.stack(s_kv[0]), jnp.stack(s_kv[1]), jnp.stack(s_kv[2]),
            jnp.stack(s_re).astype(sdt), jnp.stack(s_im).astype(sdt))
```

```python
import numpy as np
from contextlib import ExitStack
import concourse.bass as bass
import concourse.mybir as mybir
from concourse.bass_utils import run_bass_kernel_spmd

F32 = mybir.dt.float32
BF16 = mybir.dt.bfloat16
ALU = mybir.AluOpType
AF = mybir.ActivationFunctionType
AX = mybir.AxisListType

NCORES = 8
D = 1024
DFF = 2816
NF = DFF // 128
SEQ = 8192
NS = 4
ST = 512
QTR = 2048
EPS = 1e-6


class Prog:
    ENG = ["tensor", "vector", "scalar", "gpsimd", "sync"]

    def __init__(self, nc, stack):
        self.nc = nc
        self.stack = stack
        self.q = {e: [] for e in self.ENG}
        self.cnt = {e: 0 for e in self.ENG}
        self.esem = {e: stack.enter_context(nc.semaphore("es_" + e)) for e in self.ENG if e != "sync"}
        self.lastw = {}
        self.readers = {}
        self.waited = {e: {} for e in self.ENG}
        self.dpool = {}
        self.dpi = {}
        for e, n in (("sync", 12), ("gpsimd", 6), ("scalar", 4)):
            self.dpool[e] = [[stack.enter_context(nc.semaphore("ds_%s%d" % (e, i))), 0] for i in range(n)]
            self.dpi[e] = 0
        self.nops = 0

    def _need(self, eng, tk):
        sem, val = tk
        if val <= 0:
            return
        w = self.waited[eng]
        if w.get(id(sem), 0) >= val:
            return
        w[id(sem)] = val
        self.q[eng].append(lambda e, sem=sem, val=val: e.wait_ge(sem, val))

    def _deps(self, eng, reads, writes):
        for k in reads:
            t = self.lastw.get(k)
            if t is not None:
                self._need(eng, t)
        for k in writes:
            t = self.lastw.get(k)
            if t is not None:
                self._need(eng, t)
            for t in self.readers.get(k, ()):
                self._need(eng, t)

    def _record(self, tk, reads, writes):
        for k in reads:
            self.readers.setdefault(k, []).append(tk)
        for k in writes:
            self.lastw[k] = tk
            self.readers[k] = []

    def op(self, eng, fn, reads=(), writes=()):
        self._deps(eng, reads, writes)
        self.cnt[eng] += 1
        v = self.cnt[eng]
        sem = self.esem[eng]
        self.q[eng].append(lambda e, fn=fn, sem=sem: fn(e).then_inc(sem, 1))
        tk = (sem, v)
        if eng == "tensor":
            self.waited[eng][id(sem)] = v
        self._record(tk, reads, writes)
        self.nops += 1
        return tk

    def dma(self, eng, out, in_, reads=(), writes=(), **kw):
        pool = self.dpool[eng]
        i = self.dpi[eng]
        self.dpi[eng] = (i + 1) % len(pool)
        sem, cur = pool[i]
        self._deps(eng, reads, writes)
        self._need(eng, (sem, cur))
        pool[i][1] = cur + 16
        self.q[eng].append(lambda e, out=out, in_=in_, sem=sem, kw=kw: e.dma_start(out=out, in_=in_, **kw).then_inc(sem, 16))
        tk = (sem, cur + 16)
        self._record(tk, reads, writes)
        self.nops += 1
        return tk

    def barrier(self):
        for e in self.ENG:
            for pe in self.dpool:
                for sem, cur in self.dpool[pe]:
                    self._need(e, (sem, cur))
            for ce in self.esem:
                if ce != e:
                    self._need(e, (self.esem[ce], self.cnt[ce]))

    def flush(self):
        nc = self.nc
        q = self.q
        self.q = {e: [] for e in self.ENG}
        with nc.Block() as block:
            @block.tensor
            def _(e):
                for f in q["tensor"]:
                    f(e)

            @block.vector
            def _(e):
                for f in q["vector"]:
                    f(e)

            @block.scalar
            def _(e):
                for f in q["scalar"]:
                    f(e)

            @block.gpsimd
            def _(e):
                for f in q["gpsimd"]:
                    f(e)

            @block.sync
            def _(e):
                for f in q["sync"]:
                    f(e)

    def finish(self):
        self.barrier()
        self.flush()


def tiles_of(total):
    return [(t0, min(ST, total - t0)) for t0 in range(0, total, ST)]


class K:
    def __init__(self, debug=()):
        self.debug = set(debug)
        self.nc = bass.Bass("TRN2", target_bir_lowering=False)
        self.stack = ExitStack()
        self.P = Prog(self.nc, self.stack)
        self.ins = {}
        self.outs = {}
        self._uid = 0

    def din(self, name, shape, dt=F32):
        t = self.nc.dram_tensor(name, list(shape), dt, kind="ExternalInput").ap()
        self.ins[name] = t
        return t

    def dout(self, name, shape, dt=F32):
        t = self.nc.dram_tensor(name, list(shape), dt, kind="ExternalOutput").ap()
        self.outs[name] = t
        return t

    def dscr(self, name, shape, dt=F32):
        kind = "ExternalOutput" if name in self.debug else "Internal"
        t = self.nc.dram_tensor(name, list(shape), dt, kind=kind).ap()
        if name in self.debug:
            self.outs[name] = t
        return t

    def sb(self, name, shape, dt=F32):
        return self.stack.enter_context(self.nc.sbuf_tensor(name, list(shape), dt))

    def ps(self, name, shape, dt=F32):
        return self.stack.enter_context(self.nc.psum_tensor(name, list(shape), dt))

    def sbp(self, name, shape, dt=F32):
        self._uid += 1
        return self.ph.enter_context(self.nc.sbuf_tensor("%s_%d" % (name, self._uid), list(shape), dt))

    def build(self):
        nc, P = self.nc, self.P
        NT = SEQ + NS
        self.NT = NT
        xp = self.din("xp", [SEQ, D])
        xs = self.din("xs", [NS, D])
        w = {}
        for nm, shp in (("g_ffn1", [D]), ("w1_gate", [D, DFF]), ("w1_up", [D, DFF]), ("w1_down", [DFF, D]),
                        ("g_ffn2", [D]), ("w2_gate", [D, DFF]), ("w2_up", [D, DFF]), ("w2_down", [DFF, D]),
                        ("g_mix", [D]), ("w_in", [D, 4864]), ("g_q", [1, 192]), ("g_k", [1, 192]),
                        ("ssm_a_re", [16, 128]), ("ssm_a_im", [16, 128]), ("ssm_ldt", [16, 128]),
                        ("Bm_re", [16, 128, 128]), ("Bm_im", [16, 128, 128]), ("Cm_re", [16, 128, 128]), ("Cm_im", [16, 128, 128]),
                        ("ssm_d", [512]), ("w_glu", [512, 512]), ("b_glu", [512]),
                        ("mask_cur", [128, 256]), ("mask_prev", [128, 256]), ("mask_halo", [128, 256]), ("tvals", [128, ST]),
                        ("w_ssm_proj", [512, D]), ("w_attn_proj", [256, D]), ("w_o", [D, D])):
            w[nm] = self.din(nm, shp)
        self.w = w
        ident = self.din("ident", [128, 128])
        w["ident"] = ident
        self.cache = [self.din("c%d" % W, [NS, W, 512]) for W in (128, 512, 2048)]
        yp = self.dout("yp", [QTR, D])
        ys = self.dout("ys", [NS, D])
        self.kvp = [self.dout("kvp%d" % W, [W, 512]) for W in (128, 512, 2048)]
        self.kvs = [self.dout("kvs%d" % W, [NS, W, 512]) for W in (128, 512, 2048)]
        self.st_in = [self.din("st_re", [NS, 2048]), self.din("st_im", [NS, 2048])]
        self.st_out_p = [self.dout("stp_re", [16, 128]), self.dout("stp_im", [16, 128])]
        self.st_out_s = [self.dout("sts_re", [NS, 2048]), self.dout("sts_im", [NS, 2048])]
        self.yssT_s = self.dscr("yssT_s", [4, 128, NT], BF16)
        self.yatT_s = self.dscr("yatT_s", [2, 128, NT], BF16)
        x1T = self.dscr("x1T", [8, 128, NT])
        self.x1T = x1T
        self.hT_s = self.dscr("hT_s", [8, 128, NT], BF16)
        self.uT_s = self.dscr("uT_s", [4, 128, NT], BF16)
        self.kv_s = [self.dscr("kv_s%d" % g, [NT, 512]) for g in range(3)]
        self.q_s = [self.dscr("q_s%d" % g, [NT, 256]) for g in range(3)]

        self.identF = self.sb("identF", [128, 128], F32)
        P.dma("sync", self.identF[:], ident[:, :], writes=["identF"])
        self.onesB = self.sb("onesB", [128, 128], BF16)
        P.op("vector", lambda e: e.memset(self.onesB[:], 1.0), writes=["onesB"])
        self.epsC = self.sb("epsC", [128, 1], F32)
        P.op("vector", lambda e: e.memset(self.epsC[:], EPS), writes=["epsC"])
        self.bank = [self.ps("bank%d" % i, [128, 512], F32) for i in range(8)]
        self.srcs = [("p", t0, n) for (t0, n) in tiles_of(SEQ)] + [("s", 0, NS)]
        self.own0 = SEQ - QTR
        self.own_tis = [ti for ti, (kind, t0, n) in enumerate(self.srcs) if kind == "s" or t0 >= self.own0]

        for g, W in enumerate((128, 512, 2048)):
            for b in range(NS):
                P.dma("sync", self.kvs[g][b, 0:W - 1, :], self.cache[g][b, 1:W, :], writes=[("kvs", g, b)])

        with ExitStack() as ph:
            self.ph = ph
            self.alloc_ffn()
            self.ffn_phase(1, w["g_ffn1"], w["w1_gate"], w["w1_up"], w["w1_down"], self.srcs,
                           src_tok=(xp, xs), src_T=None, dst_T=x1T, dst_tok=None)
            P.barrier()
            P.flush()
        with ExitStack() as ph:
            self.ph = ph
            self.phase_b1()
            P.barrier()
            P.flush()
        with ExitStack() as ph:
            self.ph = ph
            self.phase_b2()
            P.barrier()
            P.flush()
        with ExitStack() as ph:
            self.ph = ph
            self.phase_b3()
            P.barrier()
            P.flush()
        with ExitStack() as ph:
            self.ph = ph
            self.phase_b4()
            P.barrier()
            P.flush()
        with ExitStack() as ph:
            self.ph = ph
            self.alloc_ffn()
            self.ffn_phase(2, w["g_ffn2"], w["w2_gate"], w["w2_up"], w["w2_down"], self.srcs,
                           src_tok=None, src_T=x1T, dst_T=None, dst_tok=(yp, ys), only=self.own_tis)
            P.finish()
        return nc

    def alloc_ffn(self):
        self.wg = self.sbp("wg", [128, 8, DFF], BF16)
        self.wu = self.sbp("wu", [128, 8, DFF], BF16)
        self.wd = self.sbp("wd", [128, NF, D], BF16)
        self.gcol = self.sbp("gcol", [128, 8], F32)
        self.xin = [self.sbp("xin%d" % i, [128, D], F32) for i in range(2)]
        self.xT = self.sbp("xT", [128, 8, ST], F32)
        self.sq = self.sbp("sq", [128, 8, ST], BF16)
        self.hT = self.sbp("hT", [128, 8, ST], BF16)
        self.rstd = self.sbp("rstd", [128, ST], F32)
        self.aT = self.sbp("aT", [128, NF, ST], BF16)
        self.sil = [self.sbp("sil%d" % i, [128, ST], F32) for i in range(2)]

    def rmsnorm(self, n, nb=6):
        P = self.P
        xT, sq, hT, rstd, bank = self.xT, self.sq, self.hT, self.rstd, self.bank
        for k in range(8):
            P.op("scalar", lambda e, k=k: e.activation(sq[:, k, :n], xT[:, k, :n], AF.Square),
                 reads=[("xT", k)], writes=[("sq", k)])

        def nrm(e):
            last = None
            for k in range(8):
                last = e.matmul(bank[nb][:, :n], self.onesB[:], sq[:, k, :n], start=(k == 0), stop=(k == 7))
            return last
        P.op("tensor", nrm, reads=[("sq", k) for k in range(8)] + ["onesB"], writes=[("bank", nb)])
        P.op("scalar", lambda e: e.activation(rstd[:, :n], bank[nb][:, :n], AF.Sqrt, bias=self.epsC[:], scale=1.0 / D),
             reads=[("bank", nb), "epsC"], writes=["rstd"])
        P.op("vector", lambda e: e.reciprocal(rstd[:, :n], rstd[:, :n]), reads=["rstd"], writes=["rstd"])
        for k in range(8):
            P.op("vector", lambda e, k=k: e.scalar_tensor_tensor(
                hT[:, k, :n], xT[:, k, :n], self.gcol[:, k:k + 1], rstd[:, :n], ALU.mult, ALU.mult),
                reads=[("xT", k), "gcol", "rstd"], writes=[("hT", k)])

    def phase_b1(self):
        P, nc, w = self.P, self.nc, self.w
        NW = 2816
        self.winA = self.sbp("winA", [128, 8, NW], BF16)
        for k in range(8):
            P.dma("gpsimd", self.winA[:, k, :], w["w_in"][k * 128:(k + 1) * 128, 0:NW], writes=[("winA", k)], max_dma_last_dim=4096)
        self.gcol = self.sbp("gcol", [128, 8], F32)
        P.dma("sync", self.gcol[:], w["g_mix"].rearrange("(k p) -> p k", p=128), writes=["gcol"], allow_slow_non_contiguous=True)
        self.gqb = self.sbp("gqb", [128, 192], F32)
        self.gkb = self.sbp("gkb", [128, 192], F32)
        P.dma("sync", self.gqb[:], w["g_q"].partition_broadcast(128), writes=["gqb"])
        P.dma("sync", self.gkb[:], w["g_k"].partition_broadcast(128), writes=["gkb"])
        P.op("vector", lambda e: e.tensor_scalar(self.gqb[:], self.gqb[:], 0.125, None, ALU.mult), reads=["gqb"], writes=["gqb"])
        self.xT = self.sbp("xT", [128, 8, ST], F32)
        self.sq = self.sbp("sq", [128, 8, ST], BF16)
        self.hT = self.sbp("hT", [128, 8, ST], BF16)
        self.rstd = self.sbp("rstd", [128, ST], F32)
        self.uTt = self.sbp("uTt", [128, 4, ST], BF16)
        self.sqq = [self.sbp("sqq%d" % i, [128, 512], F32) for i in range(3)]
        self.ssum = [self.sbp("ssum%d" % i, [128, 8], F32) for i in range(3)]
        self.kvt = [self.sbp("kvt%d" % i, [128, 512], F32) for i in range(3)]
        self.qt = [self.sbp("qt%d" % i, [128, 256], F32) for i in range(3)]
        self.unit = 0
        for ti, (kind, t0, n) in enumerate(self.srcs):
            self._b1_tile(ti, kind, t0, n)
        allkv = lambda g: [("kv_s", g, ti) for ti in range(len(self.srcs))]
        for g, W in enumerate((128, 512, 2048)):
            P.dma("sync", self.kvp[g][:, :], self.kv_s[g][SEQ - W:SEQ, :], reads=allkv(g), writes=[("kvp", g)])
            for b in range(NS):
                P.dma("sync", self.kvs[g][b, W - 1:W, :], self.kv_s[g][SEQ + b:SEQ + b + 1, :], reads=allkv(g), writes=[("kvs", g, b)])

    def _b1_tile(self, ti, kind, t0, n):
        P = self.P
        xT, hT, bank, winA = self.xT, self.hT, self.bank, self.winA
        gt0 = t0 if kind == "p" else SEQ
        nsub = (n + 127) // 128
        P.dma("sync", xT[:, :, :n], self.x1T[:, :, gt0:gt0 + n].rearrange("k p n -> p k n"),
              reads=[("x1T", ti)], writes=[("xT", k) for k in range(8)])
        self.rmsnorm(n, nb=0)
        hk = [("hT", k) for k in range(8)]
        wk = [("winA", k) for k in range(8)]
        need_h = ti in self.own_tis
        need_kv = kind == "s" or t0 >= self.own0 - QTR
        if need_h:
            P.dma("sync", self.hT_s[:, :, gt0:gt0 + n].rearrange("k p n -> p k n"), hT[:, :, :n], reads=hk, writes=[("hT_s", ti)])
        for m in range(4):
            bu = bank[m % 2]

            def mmu(e, m=m, bu=bu):
                last = None
                for k in range(8):
                    last = e.matmul(bu[:, :n], winA[:, k, m * 128:(m + 1) * 128], hT[:, k, :n], start=(k == 0), stop=(k == 7))
                return last
            P.op("tensor", mmu, reads=hk + wk, writes=[("bank", m % 2)])
            P.op("scalar", lambda e, m=m, bu=bu: e.copy(self.uTt[:, m, :n], bu[:, :n]), reads=[("bank", m % 2)], writes=[("uTt", m)])
        P.dma("sync", self.uT_s[:, :, gt0:gt0 + n].rearrange("k p n -> p k n"), self.uTt[:, :, :n],
              reads=[("uTt", m) for m in range(4)], writes=[("uT_s", ti)])
        if not need_kv:
            return
        for s in range(nsub):
            r = min(128, n - s * 128)
            for g in range(3):
                self._b1_qkv(ti, gt0 + s * 128, s, r, g)

    def _b1_qkv(self, ti, row0, s, r, g):
        P = self.P
        hT, bank, winA = self.hT, self.bank, self.winA
        u = self.unit
        self.unit += 1
        bA = bank[2 + (u % 3)]
        bB = bank[5 + (u % 3)]
        kA, kB = ("bank", 2 + u % 3), ("bank", 5 + u % 3)
        sqq, ssum, kvt, qt = self.sqq[u % 3], self.ssum[u % 3], self.kvt[u % 3], self.qt[u % 3]
        ks = lambda nm: (nm, u % 3)
        hk = [("hT", k) for k in range(8)]
        wk = [("winA", k) for k in range(8)]

        def mmkv(e):
            last = None
            for part, c0 in ((0, 1280 + 256 * g), (1, 2048 + 256 * g)):
                for k in range(8):
                    last = e.matmul(bA[:r, part * 256:(part + 1) * 256], hT[:, k, s * 128:s * 128 + r], winA[:, k, c0:c0 + 256],
                                    start=(k == 0), stop=(k == 7))
            return last

        def mmq(e):
            last = None
            c0 = 512 + 256 * g
            for k in range(8):
                last = e.matmul(bB[:r, 0:256], hT[:, k, s * 128:s * 128 + r], winA[:, k, c0:c0 + 256], start=(k == 0), stop=(k == 7))
            return last
        P.op("tensor", mmkv, reads=hk + wk, writes=[kA])
        P.op("tensor", mmq, reads=hk + wk, writes=[kB])
        P.op("scalar", lambda e: e.activation(sqq[:r, 0:256], bB[:r, 0:256], AF.Square), reads=[kB], writes=[ks("sqq")])
        P.op("scalar", lambda e: e.activation(sqq[:r, 256:512], bA[:r, 0:256], AF.Square), reads=[kA], writes=[ks("sqq")])
        P.op("vector", lambda e: e.tensor_reduce(ssum[:r, :], sqq[:r, :].rearrange("p (h d) -> p h d", d=64), AX.X, ALU.add),
             reads=[ks("sqq")], writes=[ks("ssum")])
        P.op("scalar", lambda e: e.activation(ssum[:r, :], ssum[:r, :], AF.Sqrt, bias=self.epsC[:r, :], scale=1.0 / 64),
             reads=[ks("ssum"), "epsC"], writes=[ks("ssum")])
        P.op("vector", lambda e: e.reciprocal(ssum[:r, :], ssum[:r, :]), reads=[ks("ssum")], writes=[ks("ssum")])
        v3 = lambda ap: ap.rearrange("p (h d) -> p h d", d=64)
        P.op("vector", lambda e: e.tensor_tensor(v3(kvt[:r, 0:256]), v3(bA[:r, 0:256]),
                                                 ssum[:r, 4:8].unsqueeze(2).broadcast_to([r, 4, 64]), ALU.mult),
             reads=[kA, ks("ssum")], writes=[ks("kvt")])
        P.op("vector", lambda e: e.tensor_tensor(v3(kvt[:r, 0:256]), v3(kvt[:r, 0:256]),
                                                 self.gkb[:r, g * 64:(g + 1) * 64].unsqueeze(1).broadcast_to([r, 4, 64]), ALU.mult),
             reads=[ks("kvt"), "gkb"], writes=[ks("kvt")])
        P.op("scalar", lambda e: e.copy(kvt[:r, 256:512], bA[:r, 256:512]), reads=[kA], writes=[ks("kvt")])
        P.op("vector", lambda e: e.tensor_tensor(v3(qt[:r, :]), v3(bB[:r, 0:256]),
                                                 ssum[:r, 0:4].unsqueeze(2).broadcast_to([r, 4, 64]), ALU.mult),
             reads=[kB, ks("ssum")], writes=[ks("qt")])
        P.op("vector", lambda e: e.tensor_tensor(v3(qt[:r, :]), v3(qt[:r, :]),
                                                 self.gqb[:r, g * 64:(g + 1) * 64].unsqueeze(1).broadcast_to([r, 4, 64]), ALU.mult),
             reads=[ks("qt"), "gqb"], writes=[ks("qt")])
        P.dma("sync", self.kv_s[g][row0:row0 + r, :], kvt[:r, :], reads=[ks("kvt")], writes=[("kv_s", g, ti)])
        P.dma("sync", self.q_s[g][row0:row0 + r, :], qt[:r, :], reads=[ks("qt")], writes=[("q_s", g, ti)])

    def phase_b2(self):
        P, nc, w = self.P, self.nc, self.w
        TS = 16
        T = {}

        def tl(nm, shape=(128, TS), dt=F32):
            T[nm] = self.sbp(nm, list(shape), dt)
            return T[nm]
        for nm in ("a_re", "a_im", "ldt"):
            tl(nm)
            P.dma("sync", T[nm][:], w["ssm_" + nm].rearrange("s q -> q s"), writes=[nm], allow_slow_non_contiguous=True)
        V = lambda fn, r, wr: P.op("vector", fn, reads=r, writes=wr)
        A = lambda fn, r, wr: P.op("scalar", fn, reads=r, writes=wr)
        tt = lambda o, a, b, op: V(lambda e: e.tensor_tensor(T[o][:], T[a][:], T[b][:], op), [a, b], [o])
        for nm in ("dt", "ar", "ai", "mag", "rs", "rc", "sn", "cs", "abr", "abi", "nabi", "sq1", "sq2", "inv", "em1", "t1", "t2", "f_re", "f_im"):
            tl(nm)
        A(lambda e: e.activation(T["dt"][:], T["ldt"][:], AF.Exp), ["ldt"], ["dt"])
        tt("ar", "a_re", "dt", ALU.mult)
        tt("ai", "a_im", "dt", ALU.mult)
        A(lambda e: e.activation(T["mag"][:], T["ar"][:], AF.Exp), ["ar"], ["mag"])
        PI = float(np.pi)
        tl("rtmp")
        T["rint"] = self.sbp("rint", [128, TS], mybir.dt.int32)
        tl("rmask")

        def reduce_generic(dst_ap, src_ap, off, tmp_ap, int_ap, mask_ap, kd):
            V(lambda e: e.tensor_scalar(dst_ap, src_ap, off, None, ALU.add), [kd, "ai", "mm0"], [kd])
            V(lambda e: e.tensor_scalar(tmp_ap, dst_ap, 1.0 / (2 * PI), None, ALU.mult), [kd], ["mm1"])
            V(lambda e: e.tensor_copy(int_ap, tmp_ap), ["mm1"], ["g_int"])
            V(lambda e: e.tensor_copy(tmp_ap, int_ap), ["g_int"], ["mm1"])
            V(lambda e: e.scalar_tensor_tensor(dst_ap, tmp_ap, -2 * PI, dst_ap, ALU.mult, ALU.add), ["mm1", kd], [kd])
            V(lambda e: e.tensor_scalar(mask_ap, dst_ap, PI, None, ALU.is_gt), [kd], ["mm2"])
            V(lambda e: e.scalar_tensor_tensor(dst_ap, mask_ap, -2 * PI, dst_ap, ALU.mult, ALU.add), ["mm2", kd], [kd])
            V(lambda e: e.tensor_scalar(mask_ap, dst_ap, -PI, None, ALU.is_lt), [kd], ["mm2"])
            V(lambda e: e.scalar_tensor_tensor(dst_ap, mask_ap, 2 * PI, dst_ap, ALU.mult, ALU.add), ["mm2", kd], [kd])
            V(lambda e: e.tensor_scalar(dst_ap, dst_ap, PI, -PI, ALU.min, ALU.max), [kd], [kd])

        def reduce_angle(dst, off):
            reduce_generic(T[dst][:], T["ai"][:], off, T["rtmp"][:], T["rint"][:], T["rmask"][:], dst)
        reduce_angle("rs", 0.0)
        reduce_angle("rc", 0.5 * PI)
        A(lambda e: e.activation(T["sn"][:], T["rs"][:], AF.Sin), ["rs"], ["sn"])
        A(lambda e: e.activation(T["cs"][:], T["rc"][:], AF.Sin), ["rc"], ["cs"])
        tt("abr", "mag", "cs", ALU.mult)
        tt("abi", "mag", "sn", ALU.mult)
        tt("sq1", "a_re", "a_re", ALU.mult)
        tt("sq2", "a_im", "a_im", ALU.mult)
        tt("inv", "sq1", "sq2", ALU.add)
        V(lambda e: e.reciprocal(T["inv"][:], T["inv"][:]), ["inv"], ["inv"])
        V(lambda e: e.tensor_scalar(T["em1"][:], T["abr"][:], -1.0, None, ALU.add), ["abr"], ["em1"])
        tt("t1", "em1", "a_re", ALU.mult)
        tt("t2", "abi", "a_im", ALU.mult)
        tt("t1", "t1", "t2", ALU.add)
        tt("f_re", "t1", "inv", ALU.mult)
        tt("t1", "abi", "a_re", ALU.mult)
        tt("t2", "em1", "a_im", ALU.mult)
        tt("t1", "t1", "t2", ALU.subtract)
        tt("f_im", "t1", "inv", ALU.mult)
        tv = tl("tvals", (128, ST))
        P.dma("sync", tv[:], w["tvals"][:, :], writes=["tvals"])
        E = [tl("E_re", (128, TS, ST)), tl("E_im", (128, TS, ST))]
        FE = [tl("FE_re", (128, TS, ST)), tl("FE_im", (128, TS, ST))]
        mm_ = [tl("mm%d" % i, (128, ST)) for i in range(4)]
        ang, g_tmp, g_msk = mm_[0], mm_[1], mm_[2]
        g_int = self.sbp("g_int", [128, ST], mybir.dt.int32)
        for s_ in range(TS):
            V(lambda e, s_=s_: e.tensor_scalar(ang[:], tv[:], T["ai"][:, s_:s_ + 1], None, ALU.mult), ["tvals", "ai"], ["mm0"])
            for j, off in ((1, 0.0), (0, 0.5 * PI)):
                reduce_generic(E[j][:, s_, :], ang[:], off, g_tmp[:], g_int[:], g_msk[:], ("E", j, s_))
                A(lambda e, j=j, s_=s_: e.activation(E[j][:, s_, :], E[j][:, s_, :], AF.Sin), [("E", j, s_)], [("E", j, s_)])
            fr, fi = T["f_re"][:, s_:s_ + 1], T["f_im"][:, s_:s_ + 1]
            V(lambda e, s_=s_, fi=fi: e.tensor_scalar(g_tmp[:], E[1][:, s_, :], fi, None, ALU.mult), [("E", 1, s_), "f_im"], ["mm1"])
            V(lambda e, s_=s_, fr=fr: e.scalar_tensor_tensor(FE[0][:, s_, :], E[0][:, s_, :], fr, g_tmp[:], ALU.mult, ALU.add), [("E", 0, s_), "f_re", "mm1"], ["FE"])
            V(lambda e, s_=s_, fr=fr: e.tensor_scalar(g_tmp[:], E[1][:, s_, :], fr, None, ALU.mult), [("E", 1, s_), "f_re"], ["mm1"])
            V(lambda e, s_=s_, fi=fi: e.scalar_tensor_tensor(FE[1][:, s_, :], E[0][:, s_, :], fi, g_tmp[:], ALU.mult, ALU.subtract), [("E", 0, s_), "f_im", "mm1"], ["FE"])
        zs = [tl("zs0", (128, ST)), tl("zs1", (128, ST))]
        NL = 1
        PR, PIm, NPI = tl("PR", (128, NL, TS)), tl("PIm", (128, NL, TS)), tl("NPI", (128, NL, TS))
        V(lambda e: e.tensor_copy(PR[:, 0, :], T["abr"][:]), ["abr"], ["PR"])
        V(lambda e: e.tensor_copy(PIm[:, 0, :], T["abi"][:]), ["abi"], ["PIm"])
        for k in range(1, NL):
            V(lambda e, k=k: e.tensor_tensor(T["t1"][:], PR[:, k - 1, :], PR[:, k - 1, :], ALU.mult), ["PR"], ["t1"])
            V(lambda e, k=k: e.tensor_tensor(T["t2"][:], PIm[:, k - 1, :], PIm[:, k - 1, :], ALU.mult), ["PIm"], ["t2"])
            V(lambda e, k=k: e.tensor_tensor(PIm[:, k, :], PR[:, k - 1, :], PIm[:, k - 1, :], ALU.mult), ["PR", "PIm"], ["PIm"])
            V(lambda e, k=k: e.tensor_scalar(PIm[:, k, :], PIm[:, k, :], 2.0, None, ALU.mult), ["PIm"], ["PIm"])
            V(lambda e, k=k: e.tensor_tensor(PR[:, k, :], T["t1"][:], T["t2"][:], ALU.subtract), ["t1", "t2"], ["PR"])
        V(lambda e: e.tensor_scalar(NPI[:], PIm[:], -1.0, None, ALU.mult), ["PIm"], ["NPI"])
        Bm = [tl("Bm_re", (128, TS, 128), BF16), tl("Bm_im", (128, TS, 128), BF16)]
        Cm = [tl("Cm_re", (128, TS, 128), BF16), tl("Cm_im", (128, TS, 128), BF16)]
        for t_, nm in ((Bm[0], "Bm_re"), (Bm[1], "Bm_im"), (Cm[0], "Cm_re"), (Cm[1], "Cm_im")):
            P.dma("gpsimd", t_[:], w[nm].rearrange("s r c -> r s c"), writes=[nm])
        wglu = tl("wglu", (128, 4, 512), BF16)
        P.dma("gpsimd", wglu[:], w["w_glu"].rearrange("(k p) f -> p k f", p=128), writes=["wglu"])
        dcol, bcol = tl("dcol", (128, 4)), tl("bcol", (128, 4))
        P.dma("sync", dcol[:], w["ssm_d"].rearrange("(c p) -> p c", p=128), writes=["dcol"], allow_slow_non_contiguous=True)
        P.dma("sync", bcol[:], w["b_glu"].rearrange("(c p) -> p c", p=128), writes=["bcol"], allow_slow_non_contiguous=True)
        car = [tl("car_re"), tl("car_im")]
        V(lambda e: e.memset(car[0][:], 0.0), [], ["car"])
        V(lambda e: e.memset(car[1][:], 0.0), [], ["car"])
        h0s = [tl("h0s_re", (128, NS, TS)), tl("h0s_im", (128, NS, TS))]
        for j in range(2):
            for b in range(NS):
                P.dma("sync", h0s[j][:, b, :], self.st_in[j][b].rearrange("(s q) -> q s", q=128), writes=["h0s"], allow_slow_non_contiguous=True)
        sts = [tl("sts_re", (128, NS, TS)), tl("sts_im", (128, NS, TS))]
        uT = tl("uT", (128, 4, ST), BF16)
        pp = [[tl("pp%d%d" % (i, j), (128, ST)) for j in range(2)] for i in range(2)]
        tmp = [mm_[2], mm_[3]]
        sbr, sbi = tl("sbr", (128, ST), BF16), tl("sbi", (128, ST), BF16)
        yraw, x2, tg = tl("yraw", (128, ST)), tl("x2", (128, ST)), tl("tg", (128, ST))
        ygf, ygb, yss = tl("ygf", (128, 4, ST)), tl("ygb", (128, 4, ST), BF16), tl("yss", (128, 4, ST), BF16)
        bank = self.bank
        self._b2 = dict(E=E, FE=FE, zs=zs, mm=mm_, T=T, PR=PR, PIm=PIm, NPI=NPI, Bm=Bm, Cm=Cm, wglu=wglu, dcol=dcol, bcol=bcol, car=car, h0s=h0s, sts=sts,
                        uT=uT, pp=pp, tmp=tmp, sbr=sbr, sbi=sbi, yraw=yraw, x2=x2, tg=tg, ygf=ygf, ygb=ygb, yss=yss)
        for ti, (kind, t0, n) in enumerate(self.srcs):
            self._b2_tile(ti, kind, t0, n)
        P.dma("sync", self.st_out_p[0].rearrange("s q -> q s"), car[0][:], reads=["car"], writes=["stp0"], allow_slow_non_contiguous=True)
        P.dma("sync", self.st_out_p[1].rearrange("s q -> q s"), car[1][:], reads=["car"], writes=["stp1"], allow_slow_non_contiguous=True)
        for j in range(2):
            for b in range(NS):
                P.dma("sync", self.st_out_s[j][b].rearrange("(s q) -> q s", q=128), sts[j][:, b, :], reads=["sts"], writes=[("stso", j, b)], allow_slow_non_contiguous=True)

    def _b2_tile(self, ti, kind, t0, n):
        P = self.P
        B = self._b2
        T, PR, PIm, NPI, Bm, Cm, car = B["T"], B["PR"], B["PIm"], B["NPI"], B["Bm"], B["Cm"], B["car"]
        uT, pp, tmp, sbr, sbi = B["uT"], B["pp"], B["tmp"], B["sbr"], B["sbi"]
        bank = self.bank
        gt0 = t0 if kind == "p" else SEQ
        V = lambda fn, r, wr: P.op("vector", fn, reads=r, writes=wr)
        is_own = kind == "s" or t0 >= self.own0
        P.dma("sync", uT[:, :, :n], self.uT_s[:, :, gt0:gt0 + n].rearrange("k p n -> p k n"), reads=[("uT_s", ti)], writes=["uT"])
        for c in range(4):
            for s4 in range(4):
                s = 4 * c + s4
                zb = (bank[0], bank[1]) if s % 2 == 0 else (bank[4], bank[5])
                zk = (("bank", 0), ("bank", 1)) if s % 2 == 0 else (("bank", 4), ("bank", 5))
                for j in range(2):
                    P.op("tensor", lambda e, j=j, s=s, c=c, zb=zb: e.matmul(zb[j][:, :n], Bm[j][:, s, :], uT[:, c, :n], start=True, stop=True),
                         reads=["uT", "Bm_re", "Bm_im"], writes=[zk[j]])
                fre, fim = T["f_re"][:, s:s + 1], T["f_im"][:, s:s + 1]
                cur = pp[0]
                if kind == "p":
                    E, FE, zs, mm = B["E"], B["FE"], B["zs"], B["mm"]
                    G = lambda fn, r, wr: P.op("gpsimd", fn, reads=r, writes=wr)
                    V(lambda e, s=s, zb=zb: e.tensor_tensor(mm[0][:, :n], FE[0][:, s, :n], zb[0][:, :n], ALU.mult), ["FE", zk[0]], ["mm0"])
                    V(lambda e, s=s, zb=zb: e.tensor_tensor(mm[1][:, :n], FE[1][:, s, :n], zb[1][:, :n], ALU.mult), ["FE", zk[1]], ["mm1"])
                    G(lambda e: e.tensor_tensor(mm[0][:, :n], mm[0][:, :n], mm[1][:, :n], ALU.subtract), ["mm0", "mm1"], ["mm0"])
                    V(lambda e, s=s, zb=zb: e.tensor_tensor(mm[2][:, :n], FE[0][:, s, :n], zb[1][:, :n], ALU.mult), ["FE", zk[1]], ["mm2"])
                    V(lambda e, s=s, zb=zb: e.tensor_tensor(mm[3][:, :n], FE[1][:, s, :n], zb[0][:, :n], ALU.mult), ["FE", zk[0]], ["mm3"])
                    G(lambda e: e.tensor_tensor(mm[2][:, :n], mm[2][:, :n], mm[3][:, :n], ALU.add), ["mm2", "mm3"], ["mm2"])
                    rho = T["mag"][:, s:s + 1].broadcast_to([128, n])
                    V(lambda e, s=s, rho=rho: e.tensor_tensor_scan(pp[1][0][:, :n], rho, mm[0][:, :n], car[0][:, s:s + 1], ALU.mult, ALU.add),
                      ["mm0", "car", "mag"], ["pp10"])
                    V(lambda e, s=s, rho=rho: e.tensor_tensor_scan(pp[1][1][:, :n], rho, mm[2][:, :n], car[1][:, s:s + 1], ALU.mult, ALU.add),
                      ["mm2", "car", "mag"], ["pp11"])
                    wr_, wi_ = pp[1][0], pp[1][1]
                    if not is_own:
                        cl = slice(n - 1, n)
                        V(lambda e, s=s: e.tensor_tensor(mm[0][:, 0:1], wi_[:, cl], E[1][:, s, cl], ALU.mult), [("E", 1, s), "pp11"], ["mm0"])
                        V(lambda e, s=s: e.scalar_tensor_tensor(car[0][:, s:s + 1], wr_[:, cl], E[0][:, s, cl], mm[0][:, 0:1], ALU.mult, ALU.subtract),
                          [("E", 0, s), "pp10", "mm0"], ["car"])
                        V(lambda e, s=s: e.tensor_tensor(mm[2][:, 0:1], wr_[:, cl], E[1][:, s, cl], ALU.mult), [("E", 1, s), "pp10"], ["mm2"])
                        V(lambda e, s=s: e.scalar_tensor_tensor(car[1][:, s:s + 1], wi_[:, cl], E[0][:, s, cl], mm[2][:, 0:1], ALU.mult, ALU.add),
                          [("E", 0, s), "pp11", "mm2"], ["car"])
                        continue
                    G(lambda e, s=s: e.tensor_tensor(mm[0][:, :n], E[0][:, s, :n], wr_[:, :n], ALU.mult), [("E", 0, s), "pp10"], ["mm0"])
                    G(lambda e, s=s: e.tensor_tensor(mm[1][:, :n], E[1][:, s, :n], wi_[:, :n], ALU.mult), [("E", 1, s), "pp11"], ["mm1"])
                    G(lambda e: e.tensor_tensor(cur[0][:, :n], mm[0][:, :n], mm[1][:, :n], ALU.subtract), ["mm0", "mm1"], ["pp00"])
                    V(lambda e, s=s: e.tensor_tensor(mm[2][:, :n], E[0][:, s, :n], wi_[:, :n], ALU.mult), [("E", 0, s), "pp11"], ["mm2"])
                    V(lambda e, s=s: e.tensor_tensor(mm[3][:, :n], E[1][:, s, :n], wr_[:, :n], ALU.mult), [("E", 1, s), "pp10"], ["mm3"])
                    V(lambda e: e.tensor_tensor(cur[1][:, :n], mm[2][:, :n], mm[3][:, :n], ALU.add), ["mm2", "mm3"], ["pp01"])
                    ci = 0
                else:
                    V(lambda e, zb=zb, fim=fim: e.tensor_scalar(tmp[0][:, :n], zb[1][:, :n], fim, None, ALU.mult), [zk[1], "f_im"], ["mm2"])
                    V(lambda e, zb=zb, fre=fre, cur=cur: e.scalar_tensor_tensor(cur[0][:, :n], zb[0][:, :n], fre, tmp[0][:, :n], ALU.mult, ALU.subtract),
                      [zk[0], "f_re", "mm2"], ["pp00"])
                    V(lambda e, zb=zb, fim=fim: e.tensor_scalar(tmp[1][:, :n], zb[0][:, :n], fim, None, ALU.mult), [zk[0], "f_im"], ["mm3"])
                    V(lambda e, zb=zb, fre=fre, cur=cur: e.scalar_tensor_tensor(cur[1][:, :n], zb[1][:, :n], fre, tmp[1][:, :n], ALU.mult, ALU.add),
                      [zk[1], "f_re", "mm3"], ["pp01"])
                    a0, b0, nb0 = PR[:, 0, s:s + 1], PIm[:, 0, s:s + 1], NPI[:, 0, s:s + 1]
                    hr, hi = B["h0s"][0][:, :, s], B["h0s"][1][:, :, s]
                    w0 = n
                    V(lambda e, cur=cur, hr=hr, a0=a0: e.scalar_tensor_tensor(cur[0][:, :w0], hr, a0, cur[0][:, :w0], ALU.mult, ALU.add), ["pp00", "h0s", "PR"], ["pp00"])
                    V(lambda e, cur=cur, hi=hi, nb0=nb0: e.scalar_tensor_tensor(cur[0][:, :w0], hi, nb0, cur[0][:, :w0], ALU.mult, ALU.add), ["pp00", "h0s", "NPI"], ["pp00"])
                    V(lambda e, cur=cur, hi=hi, a0=a0: e.scalar_tensor_tensor(cur[1][:, :w0], hi, a0, cur[1][:, :w0], ALU.mult, ALU.add), ["pp01", "h0s", "PR"], ["pp01"])
                    V(lambda e, cur=cur, hr=hr, b0=b0: e.scalar_tensor_tensor(cur[1][:, :w0], hr, b0, cur[1][:, :w0], ALU.mult, ALU.add), ["pp01", "h0s", "PIm"], ["pp01"])
                    ci = 0
                X = pp[ci]
                kx = ["pp%d0" % ci, "pp%d1" % ci]
                if kind == "p":
                    P.op("scalar", lambda e, X=X, s=s: e.copy(car[0][:, s:s + 1], X[0][:, n - 1:n]), reads=[kx[0]], writes=["car"])
                    P.op("scalar", lambda e, X=X, s=s: e.copy(car[1][:, s:s + 1], X[1][:, n - 1:n]), reads=[kx[1]], writes=["car"])
                else:
                    P.op("scalar", lambda e, X=X, s=s: e.copy(B["sts"][0][:, :, s], X[0][:, :n]), reads=[kx[0]], writes=["sts"])
                    P.op("scalar", lambda e, X=X, s=s: e.copy(B["sts"][1][:, :, s], X[1][:, :n]), reads=[kx[1]], writes=["sts"])
                P.op("scalar", lambda e, X=X: e.copy(sbr[:, :n], X[0][:, :n]), reads=[kx[0]], writes=["sbr"])
                P.op("scalar", lambda e, X=X: e.mul(sbi[:, :n], X[1][:, :n], -1.0), reads=[kx[1]], writes=["sbi"])

                def mmy(e, s=s, s4=s4):
                    e.matmul(bank[7][:, :n], Cm[0][:, s, :], sbr[:, :n], start=(s4 == 0), stop=False)
                    return e.matmul(bank[7][:, :n], Cm[1][:, s, :], sbi[:, :n], start=False, stop=(s4 == 3))
                P.op("tensor", mmy, reads=["sbr", "sbi", "Cm_re", "Cm_im"], writes=[("bank", 7)])
            if not is_own:
                continue
            yraw, x2, tg, ygf, ygb = B["yraw"], B["x2"], B["tg"], B["ygf"], B["ygb"]
            V(lambda e, c=c: e.scalar_tensor_tensor(yraw[:, :n], uT[:, c, :n], B["dcol"][:, c:c + 1], bank[7][:, :n], ALU.mult, ALU.add),
              ["uT", "dcol", ("bank", 7)], ["yraw"])
            P.op("scalar", lambda e: e.activation(x2[:, :n], yraw[:, :n], AF.Square), reads=["yraw"], writes=["x2"])
            V(lambda e: e.tensor_scalar(x2[:, :n], x2[:, :n], 0.044715, 1.0, ALU.mult, ALU.add), ["x2"], ["x2"])
            V(lambda e: e.tensor_tensor(tg[:, :n], x2[:, :n], yraw[:, :n], ALU.mult), ["x2", "yraw"], ["tg"])
            P.op("scalar", lambda e: e.activation(tg[:, :n], tg[:, :n], AF.Sigmoid, scale=1.5957691216057308), reads=["tg"], writes=["tg"])
            V(lambda e, c=c: e.tensor_tensor(ygf[:, c, :n], yraw[:, :n], tg[:, :n], ALU.mult), ["tg", "yraw"], [("ygf", c)])
            P.op("gpsimd", lambda e, c=c: e.tensor_copy(ygb[:, c, :n], ygf[:, c, :n]), reads=[("ygf", c)], writes=[("ygb", c)])
        if not is_own:
            return
        wglu, yss = B["wglu"], B["yss"]
        for m in range(4):
            bg = bank[2 + m % 2]

            def mmg(e, m=m, bg=bg):
                last = None
                for k in range(4):
                    last = e.matmul(bg[:, :n], wglu[:, k, m * 128:(m + 1) * 128], ygb[:, k, :n], start=(k == 0), stop=(k == 3))
                return last
            P.op("tensor", mmg, reads=[("ygb", k) for k in range(4)] + ["wglu"], writes=[("bank", 2 + m % 2)])
            P.op("scalar", lambda e, m=m, bg=bg: e.activation(tg[:, :n], bg[:, :n], AF.Sigmoid, bias=B["bcol"][:, m:m + 1]),
                 reads=[("bank", 2 + m % 2), "bcol"], writes=["tg"])
            V(lambda e, m=m: e.tensor_tensor(yss[:, m, :n], ygf[:, m, :n], tg[:, :n], ALU.mult), ["tg", ("ygf", m)], [("yss", m)])
        P.dma("sync", self.yssT_s[:, :, gt0:gt0 + n].rearrange("k p n -> p k n"), yss[:, :, :n],
              reads=[("yss", m) for m in range(4)], writes=[("yssT_s", ti)])

    def phase_b3(self):
        P, w = self.P, self.w
        tl = self.sbp
        A = {}
        A["mask"] = [tl("mask_cur", [128, 256], BF16), tl("mask_prev", [128, 256], BF16), tl("mask_halo", [128, 256], BF16)]
        P.dma("gpsimd", A["mask"][0][:], w["mask_cur"][:, :], writes=["mask"])
        P.dma("gpsimd", A["mask"][1][:], w["mask_prev"][:, :], writes=["mask"])
        P.dma("gpsimd", A["mask"][2][:], w["mask_halo"][:, :], writes=["mask"])
        A["identB"] = tl("identB", [128, 128], BF16)
        P.dma("gpsimd", A["identB"][:], w["ident"][:, :], writes=["identB"])
        A["kv"] = [tl("kvA%d" % i, [128, 512], F32) for i in range(2)]
        A["qf"] = [tl("qf%d" % i, [128, 256], F32) for i in range(2)]
        A["kT"] = [tl("kT%d" % i, [128, 2, 128], BF16) for i in range(3)]
        A["Vz"] = [tl("Vz%d" % i, [128, 4, 128], BF16) for i in range(3)]
        A["qTz"] = [[tl("qTz%d%d" % (i, hh), [128, 2, 128], BF16) for hh in range(2)] for i in range(2)]
        A["onesZ"] = [tl("onesZ%d" % hh, [128, 128], BF16) for hh in range(2)]
        for i in range(3):
            P.op("vector", lambda e, i=i: e.memset(A["Vz"][i][:], 0.0), writes=[("Vb", i)])
        for i in range(2):
            for hh in range(2):
                P.op("vector", lambda e, i=i, hh=hh: e.memset(A["qTz"][i][hh][:], 0.0), writes=[("qT", i)])
        for hh in range(2):
            P.op("vector", lambda e, hh=hh: e.memset(A["onesZ"][hh][:], 0.0), writes=["onesZ"])
            P.op("vector", lambda e, hh=hh: e.memset(A["onesZ"][hh][:, 64 * hh:64 * hh + 64], 1.0), reads=["onesZ"], writes=["onesZ"])
        A["Pe"] = [tl("Pe%d" % i, [128, 512], BF16) for i in range(2)]
        A["Pm"] = [tl("Pm%d" % i, [128, 512], BF16) for i in range(2)]
        BLK = 2048
        A["accN"] = tl("accN", [128, 2, BLK], F32)
        A["accD"] = tl("accD", [128, 2, BLK], F32)
        A["yat"] = tl("yat", [128, 2, BLK], BF16)
        self._b3 = A
        self.pi = 0
        self.ui = 0
        V = lambda fn, r, wr: P.op("vector", fn, reads=r, writes=wr)
        groups = ((128, 1), (512, 4), (2048, 16))
        nblk = SEQ // BLK
        for bb in range(nblk - 1, nblk):
            V(lambda e: e.memset(A["accN"][:], 0.0), [], ["accN"])
            V(lambda e: e.memset(A["accD"][:], 0.0), [], ["accD"])
            for g, (W, dl) in enumerate(groups):
                span = 128 * dl
                for r in range(dl):
                    prev = None
                    for bk in range(BLK // span):
                        base = bb * BLK + bk * span
                        if prev is None and base >= span:
                            prev = self._b3_prep(g, base - span, r, dl, 128)
                        cur = self._b3_prep(g, base, r, dl, 128)
                        qT = self._b3_q(g, base, r, dl, 128)
                        pm = 2 if (base - span) < self.own0 else 1
                        ksets = [(cur, 128, 0)] + ([(prev, 128, pm)] if prev is not None else [])
                        c0 = bk * span + r
                        views = lambda acc, pr, c0=c0, dl=dl, span=span: acc[:, pr, c0:c0 + 127 * dl + 1:dl] if dl > 1 else acc[:, pr, c0:c0 + 128]
                        self._b3_unit(qT, 128, ksets, views)
                        prev = cur
            self._b3_norm(bb * BLK, BLK, A["accN"], A["accD"], A["yat"])
        V(lambda e: e.memset(A["accN"][:], 0.0), [], ["accN"])
        V(lambda e: e.memset(A["accD"][:], 0.0), [], ["accD"])
        for b in range(NS):
            for g, (W, dl) in enumerate(groups):
                cset = self._b3_prep(g, None, 0, dl, 128, cache=(g, b))
                sset = self._b3_prep(g, SEQ + b, 0, 1, 1)
                qT = self._b3_q(g, SEQ + b, 0, 1, 1)
                views = lambda acc, pr, b=b: acc[:, pr, b:b + 1]
                self._b3_unit(qT, 1, [(cset, 128, None), (sset, 1, None)], views)
        self._b3_norm(SEQ, NS, A["accN"], A["accD"], A["yat"])

    def _b3_prep(self, g, base, r, dl, nk, cache=None):
        P, A, bank = self.P, self._b3, self.bank
        i = self.pi % 3
        j = self.pi % 2
        self.pi += 1
        kv, kT, Vz = A["kv"][j], A["kT"][i], A["Vz"][i]
        if cache is not None:
            cg, b = cache
            W = (128, 512, 2048)[cg]
            src = self.cache[cg][b, :, :].rearrange("(m d) f -> d m f", d=dl)[0]
            rd = []
        elif dl > 1:
            src = self.kv_s[g][base:base + 128 * dl, :].rearrange("(m d) f -> d m f", d=dl)[r]
            rd = [("kv_s", g, ti) for ti in range(len(self.srcs))]
        else:
            src = self.kv_s[g][base:base + nk, :]
            rd = [("kv_s", g, ti) for ti in range(len(self.srcs))]
        P.dma("sync", kv[:nk, :], src, reads=rd, writes=[("kvA", j)])
        tb = bank[6 + j]
        for pr in range(2):
            P.op("tensor", lambda e, pr=pr: e.transpose(tb[:, pr * 128:pr * 128 + nk], kv[:nk, pr * 128:(pr + 1) * 128], self.identF[:nk, :nk]),
                 reads=[("kvA", j), "identF"], writes=[("bank", 6 + j)])
        P.op("scalar", lambda e: e.copy(kT[:, :, :nk], tb[:, 0:256].rearrange("p (a n) -> p a n", a=2)[:, :, :nk]),
             reads=[("bank", 6 + j)], writes=[("kT", i)])
        v4 = kv[:nk, 256:512].rearrange("p (h d) -> p h d", d=64)
        P.op("gpsimd", lambda e: e.tensor_copy(Vz[:nk, 0::2, 0:64], v4[:, 0::2, :]), reads=[("kvA", j)], writes=[("Vb", i)])
        P.op("gpsimd", lambda e: e.tensor_copy(Vz[:nk, 1::2, 64:128], v4[:, 1::2, :]), reads=[("kvA", j)], writes=[("Vb", i)])
        return i

    def _b3_q(self, g, base, r, dl, nq):
        P, A, bank = self.P, self._b3, self.bank
        j = self.ui % 2
        qf, qTz = A["qf"][j], A["qTz"][j]
        if dl > 1:
            src = self.q_s[g][base:base + 128 * dl, :].rearrange("(m d) f -> d m f", d=dl)[r]
        else:
            src = self.q_s[g][base:base + nq, :]
        P.dma("sync", qf[:nq, :], src, reads=[("q_s", g, ti) for ti in range(len(self.srcs))], writes=[("qf", j)])
        tb = bank[6 + j]
        for pr in range(2):
            P.op("tensor", lambda e, pr=pr: e.transpose(tb[:, 256 + pr * 128:256 + pr * 128 + nq], qf[:nq, pr * 128:(pr + 1) * 128], self.identF[:nq, :nq]),
                 reads=[("qf", j), "identF"], writes=[("bank", 6 + j)])
        P.op("vector", lambda e: e.tensor_copy(qTz[0][0:64, :, :nq], tb[0:64, 256:512].rearrange("p (a n) -> p a n", a=2)[:, :, :nq]),
             reads=[("bank", 6 + j)], writes=[("qT", j)])
        P.op("vector", lambda e: e.tensor_copy(qTz[1][64:128, :, :nq], tb[64:128, 256:512].rearrange("p (a n) -> p a n", a=2)[:, :, :nq]),
             reads=[("bank", 6 + j)], writes=[("qT", j)])
        return j

    def _b3_unit(self, qi, nq, ksets, views):
        P, A, bank = self.P, self._b3, self.bank
        qTz = A["qTz"][qi]
        V = lambda fn, r, wr: P.op("vector", fn, reads=r, writes=wr)
        for pr in range(2):
            u = self.ui
            self.ui += 1
            j = u % 2
            bS, bN, bD = bank[0 + j], bank[2 + j], bank[4 + j]
            kS, kN, kD = ("bank", j), ("bank", 2 + j), ("bank", 4 + j)
            Pe, Pm = A["Pe"][j], A["Pm"][j]

            def mms(e, pr=pr):
                last = None
                for si, (ki, nk, mk) in enumerate(ksets):
                    kT = A["kT"][ki]
                    for hh in range(2):
                        slot = si * 2 + hh
                        last = e.matmul(bS[:nk, slot * nq:(slot + 1) * nq], kT[:, pr, :nk],
                                        qTz[hh][:, pr, :nq], start=True, stop=True)
                return last
            P.op("tensor", mms, reads=[("kT", ki) for ki, _, _ in ksets] + [("qT", qi)], writes=[kS])
            for si, (ki, nk, mk) in enumerate(ksets):
                lo, hi = si * 2 * nq, (si + 1) * 2 * nq
                P.op("scalar", lambda e, nk=nk, lo=lo, hi=hi: e.activation(Pe[:nk, lo:hi], bS[:nk, lo:hi], AF.Exp),
                     reads=[kS], writes=[("Pe", j, si)])
                if mk is not None:
                    P.op("gpsimd", lambda e, nk=nk, lo=lo, hi=hi, mk=mk: e.tensor_tensor(Pm[:nk, lo:hi], Pe[:nk, lo:hi], A["mask"][mk][:nk, :], ALU.mult),
                         reads=[("Pe", j, si), "mask"], writes=[("Pm", j, si)])
                else:
                    P.op("gpsimd", lambda e, nk=nk, lo=lo, hi=hi: e.tensor_copy(Pm[:nk, lo:hi], Pe[:nk, lo:hi]),
                         reads=[("Pe", j, si)], writes=[("Pm", j, si)])

            def mmav(e, pr=pr):
                last = None
                ns = len(ksets)
                tot = 2 * ns
                for lhs_of, bO in ((lambda ki, nk, hh: A["Vz"][ki][:nk, 2 * pr + hh, :], bN), (lambda ki, nk, hh: A["onesZ"][hh][:nk, :], bD)):
                    c = 0
                    for hh in range(2):
                        for si, (ki, nk, mk) in enumerate(ksets):
                            slot = si * 2 + hh
                            last = e.matmul(bO[:, :nq], lhs_of(ki, nk, hh), Pm[:nk, slot * nq:(slot + 1) * nq],
                                            start=(c == 0), stop=(c == tot - 1))
                            c += 1
                return last
            P.op("tensor", mmav, reads=[("Vb", ki) for ki, _, _ in ksets] + [("Pm", j, si) for si in range(len(ksets))] + ["onesZ"],
                 writes=[kN, kD])
            vn, vd = views(A["accN"], pr), views(A["accD"], pr)
            V(lambda e, vn=vn: e.tensor_tensor(vn, vn, bN[:, :nq], ALU.add), [kN, "accN"], ["accN"])
            V(lambda e, vd=vd: e.tensor_tensor(vd, vd, bD[:, :nq], ALU.add), [kD, "accD"], ["accD"])

    def _b3_norm(self, col0, n, accN, accD, yat):
        P = self.P
        V = lambda fn, r, wr: P.op("vector", fn, reads=r, writes=wr)
        V(lambda e: e.reciprocal(accD[:, :, :n], accD[:, :, :n]), ["accD"], ["accD"])
        V(lambda e: e.tensor_tensor(yat[:, :, :n], accN[:, :, :n], accD[:, :, :n], ALU.mult), ["accN", "accD"], ["yat"])
        P.dma("sync", self.yatT_s[:, :, col0:col0 + n].rearrange("k p n -> p k n"), yat[:, :, :n], reads=["yat"], writes=[("yatT_s", col0)])

    def phase_b4(self):
        P, w = self.P, self.w
        tl = self.sbp
        wing = tl("wing", [128, 8, 2048], BF16)
        for k in range(8):
            P.dma("gpsimd", wing[:, k, :], w["w_in"][k * 128:(k + 1) * 128, 2816:4864], writes=[("wing", k)], max_dma_last_dim=4096)
        wsp, wap, wo = tl("wsp", [128, 4, D], BF16), tl("wap", [128, 2, D], BF16), tl("wo", [128, 8, D], BF16)
        P.dma("gpsimd", wsp[:], w["w_ssm_proj"].rearrange("(k p) f -> p k f", p=128), writes=["wsp"])
        P.dma("gpsimd", wap[:], w["w_attn_proj"].rearrange("(k p) f -> p k f", p=128), writes=["wap"])
        P.dma("gpsimd", wo[:], w["w_o"].rearrange("(k p) f -> p k f", p=128), writes=["wo"])
        B = dict(wing=wing, wsp=wsp, wap=wap, wo=wo,
                 hT=tl("hT4", [128, 8, ST], BF16), yss=tl("yss4", [128, 4, ST], BF16), yat=tl("yat4", [128, 2, ST], BF16),
                 xT=tl("xT4", [128, 8, ST], F32), mixed=tl("mixed", [128, 8, ST], BF16),
                 sg=[tl("sg%d" % i, [128, ST], F32) for i in range(2)], tm=[tl("tm%d" % i, [128, ST], F32) for i in range(2)])
        self._b4 = B
        for ti, (kind, t0, n) in enumerate(self.srcs):
            if ti in self.own_tis:
                self._b4_tile(ti, kind, t0, n)

    def _b4_tile(self, ti, kind, t0, n):
        P, B, bank = self.P, self._b4, self.bank
        gt0 = t0 if kind == "p" else SEQ
        hT, yss, yat, xT, mixed, sg, tm = B["hT"], B["yss"], B["yat"], B["xT"], B["mixed"], B["sg"], B["tm"]
        wing, wsp, wap, wo = B["wing"], B["wsp"], B["wap"], B["wo"]
        V = lambda fn, r, wr: P.op("vector", fn, reads=r, writes=wr)
        fm = lambda t: t[:, :, gt0:gt0 + n].rearrange("k p n -> p k n")
        P.dma("sync", hT[:, :, :n], fm(self.hT_s), writes=["hT4"])
        P.dma("sync", yss[:, :, :n], fm(self.yssT_s), writes=["yss4"])
        P.dma("sync", yat[:, :, :n], fm(self.yatT_s), writes=["yat4"])
        P.dma("sync", xT[:, :, :n], fm(self.x1T), reads=[("x1T", ti)], writes=["xT4"])
        wk = [("wing", k) for k in range(8)]
        for m in range(8):
            for br, (src, nk_, wp, key, coff) in enumerate(((yss, 4, wsp, "yss4", 0), (yat, 2, wap, "yat4", 1024))):
                bP, bG = bank[2 * br], bank[2 * br + 1]
                kP, kG = ("bank", 2 * br), ("bank", 2 * br + 1)

                def mmp(e, m=m, src=src, nk_=nk_, wp=wp, bP=bP):
                    last = None
                    for k in range(nk_):
                        last = e.matmul(bP[:, :n], wp[:, k, m * 128:(m + 1) * 128], src[:, k, :n], start=(k == 0), stop=(k == nk_ - 1))
                    return last

                def mmg(e, m=m, coff=coff, bG=bG):
                    last = None
                    for k in range(8):
                        last = e.matmul(bG[:, :n], wing[:, k, coff + m * 128:coff + (m + 1) * 128], hT[:, k, :n], start=(k == 0), stop=(k == 7))
                    return last
                P.op("tensor", mmp, reads=[key, "wsp", "wap"], writes=[kP])
                P.op("tensor", mmg, reads=["hT4"] + wk, writes=[kG])
                P.op("scalar", lambda e, br=br, bG=bG: e.activation(sg[br][:, :n], bG[:, :n], AF.Sigmoid), reads=[kG], writes=[("sg", br)])
                V(lambda e, br=br, bP=bP: e.tensor_tensor(tm[br][:, :n], sg[br][:, :n], bP[:, :n], ALU.mult), [("sg", br), kP], [("tm", br)])
            P.op("gpsimd", lambda e, m=m: e.tensor_tensor(mixed[:, m, :n], tm[0][:, :n], tm[1][:, :n], ALU.add),
                 reads=[("tm", 0), ("tm", 1)], writes=[("mixed", m)])
        for m in range(8):
            bo = bank[4 + m % 2]

            def mmo(e, m=m, bo=bo):
                last = None
                for k in range(8):
                    last = e.matmul(bo[:, :n], wo[:, k, m * 128:(m + 1) * 128], mixed[:, k, :n], start=(k == 0), stop=(k == 7))
                return last
            P.op("tensor", mmo, reads=[("mixed", k) for k in range(8)] + ["wo"], writes=[("bank", 4 + m % 2)])
            V(lambda e, m=m, bo=bo: e.tensor_tensor(xT[:, m, :n], xT[:, m, :n], bo[:, :n], ALU.add), [("bank", 4 + m % 2), "xT4"], ["xT4"])
        P.dma("sync", fm(self.x1T), xT[:, :, :n], reads=["xT4"], writes=[("x1T", ti)])

    def load_ffn_weights(self, tag, g, wgate, wup, wdown):
        P = self.P
        P.dma("sync", self.gcol[:], g.rearrange("(k p) -> p k", p=128), writes=["gcol"], allow_slow_non_contiguous=True)
        for k in range(8):
            P.dma("gpsimd", self.wg[:, k, :], wgate[k * 128:(k + 1) * 128, :], writes=[("wg", k)], max_dma_last_dim=4096)
            P.dma("gpsimd", self.wu[:, k, :], wup[k * 128:(k + 1) * 128, :], writes=[("wu", k)], max_dma_last_dim=4096)
        for f in range(NF):
            P.dma("gpsimd", self.wd[:, f, :], wdown[f * 128:(f + 1) * 128, :], writes=[("wd", f)], max_dma_last_dim=4096)

    def ffn_phase(self, tag, g, wgate, wup, wdown, srcs, src_tok, src_T, dst_T, dst_tok, only=None):
        P, nc = self.P, self.nc
        self.load_ffn_weights(tag, g, wgate, wup, wdown)
        xT, sq, hT, rstd, aT = self.xT, self.sq, self.hT, self.rstd, self.aT
        bank = self.bank
        for ti, (kind, t0, n) in enumerate(srcs):
            if only is not None and ti not in only:
                continue
            self._ffn_tile(ti, kind, t0, n, src_tok, src_T, dst_T, dst_tok)

    def _ffn_tile(self, ti, kind, t0, n, src_tok, src_T, dst_T, dst_tok):
        P, nc = self.P, self.nc
        xT, sq, hT, rstd, aT = self.xT, self.sq, self.hT, self.rstd, self.aT
        bank = self.bank
        if True:
            gt0 = t0 if kind == "p" else SEQ
            nsub = (n + 127) // 128
            if src_tok is not None:
                src = src_tok[0] if kind == "p" else src_tok[1]
                for s in range(nsub):
                    r = min(128, n - s * 128)
                    xin = self.xin[s % 2]
                    P.dma("sync", xin[:r, :], src[t0 + s * 128: t0 + s * 128 + r, :], writes=[("xin", s % 2)])
                    for k in range(8):
                        b = bank[k]
                        P.op("tensor", lambda e, b=b, xin=xin, k=k, s=s, r=r: e.transpose(
                            b[:, s * 128: s * 128 + r], xin[:r, k * 128:(k + 1) * 128], self.identF[:r, :r]),
                            reads=[("xin", s % 2), "identF"], writes=[("bank", k)])
                for k in range(8):
                    P.op("scalar" if k % 2 else "vector",
                         (lambda e, k=k: e.copy(xT[:, k, :n], bank[k][:, :n])) if k % 2 else
                         (lambda e, k=k: e.tensor_copy(xT[:, k, :n], bank[k][:, :n])),
                         reads=[("bank", k)], writes=[("xT", k)])
            else:
                P.dma("sync", xT[:, :, :n], src_T[:, :, gt0:gt0 + n].rearrange("k p n -> p k n"),
                      writes=[("xT", k) for k in range(8)])
            for k in range(8):
                P.op("scalar", lambda e, k=k: e.activation(sq[:, k, :n], xT[:, k, :n], AF.Square),
                     reads=[("xT", k)], writes=[("sq", k)])

            def nrm(e):
                last = None
                for k in range(8):
                    last = e.matmul(bank[6][:, :n], self.onesB[:], sq[:, k, :n], start=(k == 0), stop=(k == 7))
                return last
            P.op("tensor", nrm, reads=[("sq", k) for k in range(8)] + ["onesB"], writes=[("bank", 6)])
            P.op("scalar", lambda e: e.activation(rstd[:, :n], bank[6][:, :n], AF.Sqrt, bias=self.epsC[:], scale=1.0 / D),
                 reads=[("bank", 6), "epsC"], writes=["rstd"])
            P.op("vector", lambda e: e.reciprocal(rstd[:, :n], rstd[:, :n]), reads=["rstd"], writes=["rstd"])
            for k in range(8):
                P.op("vector", lambda e, k=k: e.scalar_tensor_tensor(
                    hT[:, k, :n], xT[:, k, :n], self.gcol[:, k:k + 1], rstd[:, :n], ALU.mult, ALU.mult),
                    reads=[("xT", k), "gcol", "rstd"], writes=[("hT", k)])
            for f in range(NF):
                bg = bank[f % 2]
                bu = bank[2 + f % 2]
                sil = self.sil[f % 2]

                def mmg(e, f=f, bg=bg):
                    last = None
                    for k in range(8):
                        last = e.matmul(bg[:, :n], self.wg[:, k, f * 128:(f + 1) * 128], hT[:, k, :n], start=(k == 0), stop=(k == 7))
                    return last

                def mmu(e, f=f, bu=bu):
                    last = None
                    for k in range(8):
                        last = e.matmul(bu[:, :n], self.wu[:, k, f * 128:(f + 1) * 128], hT[:, k, :n], start=(k == 0), stop=(k == 7))
                    return last
                hk = [("hT", k) for k in range(8)]
                P.op("tensor", mmg, reads=hk + [("wg", k) for k in range(8)], writes=[("bank", f % 2)])
                P.op("tensor", mmu, reads=hk + [("wu", k) for k in range(8)], writes=[("bank", 2 + f % 2)])
                P.op("scalar", lambda e, bg=bg, sil=sil: e.activation(sil[:, :n], bg[:, :n], AF.Silu),
                     reads=[("bank", f % 2)], writes=[("sil", f % 2)])
                P.op("vector", lambda e, f=f, bu=bu, sil=sil: e.tensor_tensor(aT[:, f, :n], sil[:, :n], bu[:, :n], ALU.mult),
                     reads=[("sil", f % 2), ("bank", 2 + f % 2)], writes=[("aT", f)])
            for m in range(8):
                bd = bank[4 + m % 2]

                def mmd(e, m=m, bd=bd):
                    last = None
                    for f in range(NF):
                        last = e.matmul(bd[:, :n], self.wd[:, f, m * 128:(m + 1) * 128], aT[:, f, :n], start=(f == 0), stop=(f == NF - 1))
                    return last
                P.op("tensor", mmd, reads=[("aT", f) for f in range(NF)] + [("wd", f) for f in range(NF)], writes=[("bank", 4 + m % 2)])
                P.op("vector", lambda e, m=m, bd=bd: e.scalar_tensor_tensor(
                    xT[:, m, :n], bd[:, :n], 0.5, xT[:, m, :n], ALU.mult, ALU.add),
                    reads=[("bank", 4 + m % 2), ("xT", m)], writes=[("xT", m)])
            if dst_T is not None:
                P.dma("sync", dst_T[:, :, gt0:gt0 + n].rearrange("k p n -> p k n"), xT[:, :, :n],
                      reads=[("xT", k) for k in range(8)], writes=[("x1T", ti)])
            if dst_tok is not None:
                dst = dst_tok[0] if kind == "p" else dst_tok[1]
                for s in range(nsub):
                    r = min(128, n - s * 128)
                    xo = self.xin[s % 2]
                    for k in range(8):
                        bk = bank[6 + (k // 4)]
                        P.op("tensor", lambda e, bk=bk, k=k, s=s, r=r: e.transpose(
                            bk[:r, (k % 4) * 128:(k % 4 + 1) * 128], xT[:, k, s * 128: s * 128 + r], self.identF[:, :]),
                            reads=[("xT", k), "identF"], writes=[("bank", 6 + k // 4)])
                    P.op("vector", lambda e, xo=xo, r=r: e.tensor_copy(xo[:r, 0:512], bank[6][:r, :]),
                         reads=[("bank", 6)], writes=[("xin", s % 2)])
                    P.op("scalar", lambda e, xo=xo, r=r: e.copy(xo[:r, 512:1024], bank[7][:r, :]),
                         reads=[("bank", 7)], writes=[("xin", s % 2)])
                    o0 = (t0 - self.own0) if kind == "p" else t0
                    P.dma("sync", dst[o0 + s * 128: o0 + s * 128 + r, :], xo[:r, :], reads=[("xin", s % 2)],
                          writes=[("yout", kind, t0, s)])


_CACHE = {}
WINS = (128, 512, 2048)


def make_in_maps(inp, seq=None):
    seq = SEQ if seq is None else seq
    ident = np.eye(128, dtype=np.float32)
    caches = [inp["cache_kv_w%d" % W][0].reshape(32, W, 512) for W in WINS]
    tvals = np.ascontiguousarray(np.tile(np.arange(1, ST + 1, dtype=np.float32)[None, :], (128, 1)))
    kk, qq = np.meshgrid(np.arange(128), np.arange(128), indexing="ij")
    mask_cur = np.ascontiguousarray(np.tile((kk <= qq).astype(np.float32), (1, 2)))
    mask_prev = np.ascontiguousarray(np.tile((kk >= qq).astype(np.float32), (1, 2)))
    ssm_mats = {nm: np.zeros((16, 128, 128), np.float32) for nm in ("Bm_re", "Bm_im", "Cm_re", "Cm_im")}
    for s_ in range(16):
        for gi in range(2):
            g = 2 * s_ + gi
            r0 = 32 * (s_ % 4) + 16 * gi
            for nm, src in (("Bm_re", "ssm_b_re"), ("Bm_im", "ssm_b_im")):
                ssm_mats[nm][s_, r0:r0 + 16, 64 * gi:64 * gi + 64] = inp[src][0][g].T
            for nm, src in (("Cm_re", "ssm_c_re"), ("Cm_im", "ssm_c_im")):
                ssm_mats[nm][s_, 64 * gi:64 * gi + 64, r0:r0 + 16] = inp[src][0][g].T
    in_maps = []
    for c in range(NCORES):
        m = {}
        b_, q_ = c // 4, c % 4
        xw = np.zeros((seq, D), np.float32)
        nreal = QTR * (q_ + 1)
        xw[seq - nreal:] = inp["x_prompt"][b_][:nreal]
        m["xp"] = xw
        m["mask_halo"] = mask_prev if q_ > 0 else np.zeros_like(mask_prev)
        m["xs"] = np.ascontiguousarray(inp["x_sample"][c * NS:(c + 1) * NS, 0, :])
        for nm in ("g_ffn1", "w1_gate", "w1_up", "w1_down", "g_ffn2", "w2_gate", "w2_up", "w2_down", "g_mix", "w_in"):
            m[nm] = np.ascontiguousarray(inp[nm][0])
        m["g_q"] = np.ascontiguousarray(inp["g_q"][0].reshape(1, 192))
        m["g_k"] = np.ascontiguousarray(inp["g_k"][0].reshape(1, 192))
        m["ident"] = ident
        m["ssm_a_re"] = np.ascontiguousarray(inp["ssm_a_re"][0].reshape(16, 128))
        m["ssm_a_im"] = np.ascontiguousarray(inp["ssm_a_im"][0].reshape(16, 128))
        m["ssm_ldt"] = np.ascontiguousarray(np.repeat(inp["ssm_log_dt"][0].reshape(16, 2), 64, axis=1))
        for nm in ("Bm_re", "Bm_im", "Cm_re", "Cm_im"):
            m[nm] = ssm_mats[nm]
        m["ssm_d"] = np.ascontiguousarray(inp["ssm_d"][0])
        m["w_glu"] = np.ascontiguousarray(inp["w_glu"][0])
        m["b_glu"] = np.ascontiguousarray(inp["b_glu"][0])
        m["tvals"] = tvals
        m["mask_cur"] = mask_cur
        m["mask_prev"] = mask_prev
        for nm in ("w_ssm_proj", "w_attn_proj", "w_o"):
            m[nm] = np.ascontiguousarray(inp[nm][0])
        m["st_re"] = np.ascontiguousarray(inp["state_ssm_re"][0][c * NS:(c + 1) * NS].reshape(NS, 2048))
        m["st_im"] = np.ascontiguousarray(inp["state_ssm_im"][0][c * NS:(c + 1) * NS].reshape(NS, 2048))
        for g, W in enumerate(WINS):
            m["c%d" % W] = np.ascontiguousarray(caches[g][c * NS:(c + 1) * NS])
        in_maps.append(m)
    return in_maps


def kernel(**inputs):
    inp = {k: np.asarray(v) for k, v in inputs.items()}
    kb = _CACHE.get("k")
    if kb is None:
        kb = K()
        kb.build()
        _CACHE["k"] = kb
    in_maps = make_in_maps(inp)
    res = run_bass_kernel_spmd(kb.nc, in_maps, core_ids=list(range(NCORES))).results
    y_p = np.stack([np.concatenate([res[4 * b + q]["yp"] for q in range(4)]) for b in range(2)])
    y_s = np.concatenate([res[c]["ys"] for c in range(NCORES)])[:, None, :]
    outs = [y_p, y_s]
    for W in WINS:
        outs.append(np.stack([res[4 * b + 3]["kvp%d" % W] for b in range(2)]).reshape(1, 2, W, 2, 4, 64))
    outs.append(np.stack([res[4 * b + 3]["stp_re"] for b in range(2)]).reshape(1, 2, 32, 64))
    outs.append(np.stack([res[4 * b + 3]["stp_im"] for b in range(2)]).reshape(1, 2, 32, 64))
    for W in WINS:
        outs.append(np.concatenate([res[c]["kvs%d" % W] for c in range(NCORES)]).reshape(1, 32, W, 2, 4, 64))
    outs.append(np.concatenate([res[c]["sts_re"] for c in range(NCORES)]).reshape(1, 32, 32, 64))
    outs.append(np.concatenate([res[c]["sts_im"] for c in range(NCORES)]).reshape(1, 32, 32, 64))
    return tuple(outs)
```

```python
import numpy as np
from contextlib import ExitStack
import concourse.bass as bass
import concourse.mybir as mybir
from concourse.bass_utils import run_bass_kernel_spmd

F32 = mybir.dt.float32
BF16 = mybir.dt.bfloat16
ALU = mybir.AluOpType
AF = mybir.ActivationFunctionType
AX = mybir.AxisListType

NCORES = 8
D = 1024
DFF = 2816
NF = DFF // 128
SEQ = 8192
NS = 4
ST = 512
QTR = 2048
EPS = 1e-6


class Prog:
    ENG = ["tensor", "vector", "scalar", "gpsimd", "sync"]

    def __init__(self, nc, stack):
        self.nc = nc
        self.stack = stack
        self.q = {e: [] for e in self.ENG}
        self.cnt = {e: 0 for e in self.ENG}
        self.esem = {e: stack.enter_context(nc.semaphore("es_" + e)) for e in self.ENG if e != "sync"}
        self.lastw = {}
        self.readers = {}
        self.waited = {e: {} for e in self.ENG}
        self.dpool = {}
        self.dpi = {}
        for e, n in (("sync", 12), ("gpsimd", 6), ("scalar", 4)):
            self.dpool[e] = [[stack.enter_context(nc.semaphore("ds_%s%d" % (e, i))), 0] for i in range(n)]
            self.dpi[e] = 0
        self.nops = 0

    def _need(self, eng, tk):
        sem, val = tk
        if val <= 0:
            return
        w = self.waited[eng]
        if w.get(id(sem), 0) >= val:
            return
        w[id(sem)] = val
        self.q[eng].append(lambda e, sem=sem, val=val: e.wait_ge(sem, val))

    def _deps(self, eng, reads, writes):
        for k in reads:
            t = self.lastw.get(k)
            if t is not None:
                self._need(eng, t)
        for k in writes:
            t = self.lastw.get(k)
            if t is not None:
                self._need(eng, t)
            for t in self.readers.get(k, ()):
                self._need(eng, t)

    def _record(self, tk, reads, writes):
        for k in reads:
            self.readers.setdefault(k, []).append(tk)
        for k in writes:
            self.lastw[k] = tk
            self.readers[k] = []

    def op(self, eng, fn, reads=(), writes=()):
        self._deps(eng, reads, writes)
        self.cnt[eng] += 1
        v = self.cnt[eng]
        sem = self.esem[eng]
        self.q[eng].append(lambda e, fn=fn, sem=sem: fn(e).then_inc(sem, 1))
        tk = (sem, v)
        if eng == "tensor":
            self.waited[eng][id(sem)] = v
        self._record(tk, reads, writes)
        self.nops += 1
        return tk

    def dma(self, eng, out, in_, reads=(), writes=(), **kw):
        pool = self.dpool[eng]
        i = self.dpi[eng]
        self.dpi[eng] = (i + 1) % len(pool)
        sem, cur = pool[i]
        self._deps(eng, reads, writes)
        self._need(eng, (sem, cur))
        pool[i][1] = cur + 16
        self.q[eng].append(lambda e, out=out, in_=in_, sem=sem, kw=kw: e.dma_start(out=out, in_=in_, **kw).then_inc(sem, 16))
        tk = (sem, cur + 16)
        self._record(tk, reads, writes)
        self.nops += 1
        return tk

    def barrier(self):
        for e in self.ENG:
            for pe in self.dpool:
                for sem, cur in self.dpool[pe]:
                    self._need(e, (sem, cur))
            for ce in self.esem:
                if ce != e:
                    self._need(e, (self.esem[ce], self.cnt[ce]))

    def flush(self):
        nc = self.nc
        q = self.q
        self.q = {e: [] for e in self.ENG}
        with nc.Block() as block:
            @block.tensor
            def _(e):
                for f in q["tensor"]:
                    f(e)

            @block.vector
            def _(e):
                for f in q["vector"]:
                    f(e)

            @block.scalar
            def _(e):
                for f in q["scalar"]:
                    f(e)

            @block.gpsimd
            def _(e):
                for f in q["gpsimd"]:
                    f(e)

            @block.sync
            def _(e):
                for f in q["sync"]:
                    f(e)

    def finish(self):
        self.barrier()
        self.flush()


def tiles_of(total):
    return [(t0, min(ST, total - t0)) for t0 in range(0, total, ST)]


class K:
    def __init__(self, debug=()):
        self.debug = set(debug)
        self.nc = bass.Bass("TRN2", target_bir_lowering=False)
        self.stack = ExitStack()
        self.P = Prog(self.nc, self.stack)
        self.ins = {}
        self.outs = {}
        self._uid = 0

    def din(self, name, shape, dt=F32):
        t = self.nc.dram_tensor(name, list(shape), dt, kind="ExternalInput").ap()
        self.ins[name] = t
        return t

    def dout(self, name, shape, dt=F32):
        t = self.nc.dram_tensor(name, list(shape), dt, kind="ExternalOutput").ap()
        self.outs[name] = t
        return t

    def dscr(self, name, shape, dt=F32):
        kind = "ExternalOutput" if name in self.debug else "Internal"
        t = self.nc.dram_tensor(name, list(shape), dt, kind=kind).ap()
        if name in self.debug:
            self.outs[name] = t
        return t

    def sb(self, name, shape, dt=F32):
        return self.stack.enter_context(self.nc.sbuf_tensor(name, list(shape), dt))

    def ps(self, name, shape, dt=F32):
        return self.stack.enter_context(self.nc.psum_tensor(name, list(shape), dt))

    def sbp(self, name, shape, dt=F32):
        self._uid += 1
        return self.ph.enter_context(self.nc.sbuf_tensor("%s_%d" % (name, self._uid), list(shape), dt))

    def build(self):
        nc, P = self.nc, self.P
        NT = SEQ + NS
        self.NT = NT
        xTin = self.din("xTin", [8, 128, SEQ + NS])

        w = {}
        for nm, shp in (("g_ffn1", [D]), ("w1_gate", [D, DFF]), ("w1_up", [D, DFF]), ("w1_down", [DFF, D]),
                        ("g_ffn2", [D]), ("w2_gate", [D, DFF]), ("w2_up", [D, DFF]), ("w2_down", [DFF, D]),
                        ("g_mix", [D]), ("w_in", [D, 4864]), ("g_q", [1, 192]), ("g_k", [1, 192]),
                        ("ssm_a_re", [16, 128]), ("ssm_a_im", [16, 128]), ("ssm_ldt", [16, 128]),
                        ("Bm_re", [16, 128, 128]), ("Bm_im", [16, 128, 128]), ("Cm_re", [16, 128, 128]), ("Cm_im", [16, 128, 128]),
                        ("ssm_d", [512]), ("w_glu", [512, 512]), ("b_glu", [512]),
                        ("mask_cur", [128, 256]), ("mask_prev", [128, 256]), ("mask_halo", [128, 256]), ("tvals", [128, ST]),
                        ("w_ssm_proj", [512, D]), ("w_attn_proj", [256, D]), ("w_o", [D, D])):
            w[nm] = self.din(nm, shp)
        self.w = w
        ident = self.din("ident", [128, 128])
        w["ident"] = ident
        self.cache = [self.din("c%d" % W, [NS, W, 512]) for W in (128, 512, 2048)]
        yTout = self.dout("yTout", [8, 128, QTR + NS])

        self.kvp = [self.dout("kvp%d" % W, [W, 512]) for W in (128, 512, 2048)]
        self.kvs = [self.dout("kvs%d" % W, [NS, W, 512]) for W in (128, 512, 2048)]
        self.st_in = [self.din("st_re", [NS, 2048]), self.din("st_im", [NS, 2048])]
        self.st_out_p = [self.dout("stp_re", [16, 128]), self.dout("stp_im", [16, 128])]
        self.st_out_s = [self.dout("sts_re", [NS, 2048]), self.dout("sts_im", [NS, 2048])]
        self.yssT_s = self.dscr("yssT_s", [4, 128, NT], BF16)
        self.yatT_s = self.dscr("yatT_s", [2, 128, NT], BF16)
        x1T = self.dscr("x1T", [8, 128, NT])
        self.x1T = x1T
        self.hT_s = self.dscr("hT_s", [8, 128, NT], BF16)
        self.uT_s = self.dscr("uT_s", [4, 128, NT], BF16)
        self.kv_s = [self.dscr("kv_s%d" % g, [NT, 512]) for g in range(3)]
        self.q_s = [self.dscr("q_s%d" % g, [NT, 256]) for g in range(3)]

        self.identF = self.sb("identF", [128, 128], F32)
        P.dma("sync", self.identF[:], ident[:, :], writes=["identF"])
        self.onesB = self.sb("onesB", [128, 128], BF16)
        P.op("vector", lambda e: e.memset(self.onesB[:], 1.0), writes=["onesB"])
        self.epsC = self.sb("epsC", [128, 1], F32)
        P.op("vector", lambda e: e.memset(self.epsC[:], EPS), writes=["epsC"])
        self.bank = [self.ps("bank%d" % i, [128, 512], F32) for i in range(8)]
        self.srcs = [("p", t0, n) for (t0, n) in tiles_of(SEQ)] + [("s", 0, NS)]
        self.own0 = SEQ - QTR
        self.own_tis = [ti for ti, (kind, t0, n) in enumerate(self.srcs) if kind == "s" or t0 >= self.own0]

        for g, W in enumerate((128, 512, 2048)):
            for b in range(NS):
                P.dma("sync", self.kvs[g][b, 0:W - 1, :], self.cache[g][b, 1:W, :], writes=[("kvs", g, b)])

        with ExitStack() as ph:
            self.ph = ph
            self.alloc_ffn()
            self.ffn_phase(1, w["g_ffn1"], w["w1_gate"], w["w1_up"], w["w1_down"], self.srcs,
                           src_tok=None, src_T=xTin, dst_T=x1T, dst_tok=None)
            P.barrier()
            P.flush()
        with ExitStack() as ph:
            self.ph = ph
            self.phase_b1()
            P.barrier()
            P.flush()
        with ExitStack() as ph:
            self.ph = ph
            self.phase_b2()
            P.barrier()
            P.flush()
        with ExitStack() as ph:
            self.ph = ph
            self.phase_b3()
            P.barrier()
            P.flush()
        with ExitStack() as ph:
            self.ph = ph
            self.phase_b4()
            P.barrier()
            P.flush()
        with ExitStack() as ph:
            self.ph = ph
            self.alloc_ffn()
            self.ffn_phase(2, w["g_ffn2"], w["w2_gate"], w["w2_up"], w["w2_down"], self.srcs,
                           src_tok=None, src_T=x1T, dst_T=yTout, dst_tok=None, only=self.own_tis, dst_own=True)
            P.finish()
        return nc

    def alloc_ffn(self):
        self.wg = self.sbp("wg", [128, 8, DFF], BF16)
        self.wu = self.sbp("wu", [128, 8, DFF], BF16)
        self.wd = self.sbp("wd", [128, NF, D], BF16)
        self.gcol = self.sbp("gcol", [128, 8], F32)
        self.xin = [self.sbp("xin%d" % i, [128, D], F32) for i in range(2)]
        self.xT = self.sbp("xT", [128, 8, ST], F32)
        self.sq = self.sbp("sq", [128, 8, ST], BF16)
        self.hT = self.sbp("hT", [128, 8, ST], BF16)
        self.rstd = self.sbp("rstd", [128, ST], F32)
        self.aT = self.sbp("aT", [128, NF, ST], BF16)
        self.sil = [self.sbp("sil%d" % i, [128, ST], F32) for i in range(2)]

    def rmsnorm(self, n, nb=6):
        P = self.P
        xT, sq, hT, rstd, bank = self.xT, self.sq, self.hT, self.rstd, self.bank
        for k in range(8):
            P.op("scalar", lambda e, k=k: e.activation(sq[:, k, :n], xT[:, k, :n], AF.Square),
                 reads=[("xT", k)], writes=[("sq", k)])

        def nrm(e):
            last = None
            for k in range(8):
                last = e.matmul(bank[nb][:, :n], self.onesB[:], sq[:, k, :n], start=(k == 0), stop=(k == 7))
            return last
        P.op("tensor", nrm, reads=[("sq", k) for k in range(8)] + ["onesB"], writes=[("bank", nb)])
        P.op("scalar", lambda e: e.activation(rstd[:, :n], bank[nb][:, :n], AF.Sqrt, bias=self.epsC[:], scale=1.0 / D),
             reads=[("bank", nb), "epsC"], writes=["rstd"])
        P.op("vector", lambda e: e.reciprocal(rstd[:, :n], rstd[:, :n]), reads=["rstd"], writes=["rstd"])
        for k in range(8):
            P.op("vector", lambda e, k=k: e.scalar_tensor_tensor(
                hT[:, k, :n], xT[:, k, :n], self.gcol[:, k:k + 1], rstd[:, :n], ALU.mult, ALU.mult),
                reads=[("xT", k), "gcol", "rstd"], writes=[("hT", k)])

    def phase_b1(self):
        P, nc, w = self.P, self.nc, self.w
        NW = 2816
        self.winA = self.sbp("winA", [128, 8, NW], BF16)
        for k in range(8):
            P.dma("gpsimd", self.winA[:, k, :], w["w_in"][k * 128:(k + 1) * 128, 0:NW], writes=[("winA", k)], max_dma_last_dim=4096)
        self.gcol = self.sbp("gcol", [128, 8], F32)
        P.dma("sync", self.gcol[:], w["g_mix"].rearrange("(k p) -> p k", p=128), writes=["gcol"], allow_slow_non_contiguous=True)
        self.gqb = self.sbp("gqb", [128, 192], F32)
        self.gkb = self.sbp("gkb", [128, 192], F32)
        P.dma("sync", self.gqb[:], w["g_q"].partition_broadcast(128), writes=["gqb"])
        P.dma("sync", self.gkb[:], w["g_k"].partition_broadcast(128), writes=["gkb"])
        P.op("vector", lambda e: e.tensor_scalar(self.gqb[:], self.gqb[:], 0.125, None, ALU.mult), reads=["gqb"], writes=["gqb"])
        self.xT = self.sbp("xT", [128, 8, ST], F32)
        self.sq = self.sbp("sq", [128, 8, ST], BF16)
        self.hT = self.sbp("hT", [128, 8, ST], BF16)
        self.rstd = self.sbp("rstd", [128, ST], F32)
        self.uTt = self.sbp("uTt", [128, 4, ST], BF16)
        self.sqq = [self.sbp("sqq%d" % i, [128, 512], F32) for i in range(3)]
        self.ssum = [self.sbp("ssum%d" % i, [128, 8], F32) for i in range(3)]
        self.kvt = [self.sbp("kvt%d" % i, [128, 512], F32) for i in range(3)]
        self.qt = [self.sbp("qt%d" % i, [128, 256], F32) for i in range(3)]
        self.unit = 0
        for ti, (kind, t0, n) in enumerate(self.srcs):
            self._b1_tile(ti, kind, t0, n)
        allkv = lambda g: [("kv_s", g, ti) for ti in range(len(self.srcs))]
        for g, W in enumerate((128, 512, 2048)):
            P.dma("sync", self.kvp[g][:, :], self.kv_s[g][SEQ - W:SEQ, :], reads=allkv(g), writes=[("kvp", g)])
            for b in range(NS):
                P.dma("sync", self.kvs[g][b, W - 1:W, :], self.kv_s[g][SEQ + b:SEQ + b + 1, :], reads=allkv(g), writes=[("kvs", g, b)])

    def _b1_tile(self, ti, kind, t0, n):
        P = self.P
        xT, hT, bank, winA = self.xT, self.hT, self.bank, self.winA
        gt0 = t0 if kind == "p" else SEQ
        nsub = (n + 127) // 128
        P.dma("sync", xT[:, :, :n], self.x1T[:, :, gt0:gt0 + n].rearrange("k p n -> p k n"),
              reads=[("x1T", ti)], writes=[("xT", k) for k in range(8)])
        self.rmsnorm(n, nb=0)
        hk = [("hT", k) for k in range(8)]
        wk = [("winA", k) for k in range(8)]
        need_h = ti in self.own_tis
        need_kv = kind == "s" or t0 >= self.own0 - QTR
        if need_h:
            P.dma("sync", self.hT_s[:, :, gt0:gt0 + n].rearrange("k p n -> p k n"), hT[:, :, :n], reads=hk, writes=[("hT_s", ti)])
        for m in range(4):
            bu = bank[m % 2]

            def mmu(e, m=m, bu=bu):
                last = None
                for k in range(8):
                    last = e.matmul(bu[:, :n], winA[:, k, m * 128:(m + 1) * 128], hT[:, k, :n], start=(k == 0), stop=(k == 7))
                return last
            P.op("tensor", mmu, reads=hk + wk, writes=[("bank", m % 2)])
            P.op("scalar", lambda e, m=m, bu=bu: e.copy(self.uTt[:, m, :n], bu[:, :n]), reads=[("bank", m % 2)], writes=[("uTt", m)])
        P.dma("sync", self.uT_s[:, :, gt0:gt0 + n].rearrange("k p n -> p k n"), self.uTt[:, :, :n],
              reads=[("uTt", m) for m in range(4)], writes=[("uT_s", ti)])
        if not need_kv:
            return
        for s in range(nsub):
            r = min(128, n - s * 128)
            for g in range(3):
                self._b1_qkv(ti, gt0 + s * 128, s, r, g)

    def _b1_qkv(self, ti, row0, s, r, g):
        P = self.P
        hT, bank, winA = self.hT, self.bank, self.winA
        u = self.unit
        self.unit += 1
        bA = bank[2 + (u % 3)]
        bB = bank[5 + (u % 3)]
        kA, kB = ("bank", 2 + u % 3), ("bank", 5 + u % 3)
        sqq, ssum, kvt, qt = self.sqq[u % 3], self.ssum[u % 3], self.kvt[u % 3], self.qt[u % 3]
        ks = lambda nm: (nm, u % 3)
        hk = [("hT", k) for k in range(8)]
        wk = [("winA", k) for k in range(8)]

        def mmkv(e):
            last = None
            for part, c0 in ((0, 1280 + 256 * g), (1, 2048 + 256 * g)):
                for k in range(8):
                    last = e.matmul(bA[:r, part * 256:(part + 1) * 256], hT[:, k, s * 128:s * 128 + r], winA[:, k, c0:c0 + 256],
                                    start=(k == 0), stop=(k == 7))
            return last

        def mmq(e):
            last = None
            c0 = 512 + 256 * g
            for k in range(8):
                last = e.matmul(bB[:r, 0:256], hT[:, k, s * 128:s * 128 + r], winA[:, k, c0:c0 + 256], start=(k == 0), stop=(k == 7))
            return last
        P.op("tensor", mmkv, reads=hk + wk, writes=[kA])
        P.op("tensor", mmq, reads=hk + wk, writes=[kB])
        P.op("scalar", lambda e: e.activation(sqq[:r, 0:256], bB[:r, 0:256], AF.Square), reads=[kB], writes=[ks("sqq")])
        P.op("scalar", lambda e: e.activation(sqq[:r, 256:512], bA[:r, 0:256], AF.Square), reads=[kA], writes=[ks("sqq")])
        P.op("vector", lambda e: e.tensor_reduce(ssum[:r, :], sqq[:r, :].rearrange("p (h d) -> p h d", d=64), AX.X, ALU.add),
             reads=[ks("sqq")], writes=[ks("ssum")])
        P.op("scalar", lambda e: e.activation(ssum[:r, :], ssum[:r, :], AF.Sqrt, bias=self.epsC[:r, :], scale=1.0 / 64),
             reads=[ks("ssum"), "epsC"], writes=[ks("ssum")])
        P.op("vector", lambda e: e.reciprocal(ssum[:r, :], ssum[:r, :]), reads=[ks("ssum")], writes=[ks("ssum")])
        v3 = lambda ap: ap.rearrange("p (h d) -> p h d", d=64)
        P.op("vector", lambda e: e.tensor_tensor(v3(kvt[:r, 0:256]), v3(bA[:r, 0:256]),
                                                 ssum[:r, 4:8].unsqueeze(2).broadcast_to([r, 4, 64]), ALU.mult),
             reads=[kA, ks("ssum")], writes=[ks("kvt")])
        P.op("vector", lambda e: e.tensor_tensor(v3(kvt[:r, 0:256]), v3(kvt[:r, 0:256]),
                                                 self.gkb[:r, g * 64:(g + 1) * 64].unsqueeze(1).broadcast_to([r, 4, 64]), ALU.mult),
             reads=[ks("kvt"), "gkb"], writes=[ks("kvt")])
        P.op("scalar", lambda e: e.copy(kvt[:r, 256:512], bA[:r, 256:512]), reads=[kA], writes=[ks("kvt")])
        P.op("vector", lambda e: e.tensor_tensor(v3(qt[:r, :]), v3(bB[:r, 0:256]),
                                                 ssum[:r, 0:4].unsqueeze(2).broadcast_to([r, 4, 64]), ALU.mult),
             reads=[kB, ks("ssum")], writes=[ks("qt")])
        P.op("vector", lambda e: e.tensor_tensor(v3(qt[:r, :]), v3(qt[:r, :]),
                                                 self.gqb[:r, g * 64:(g + 1) * 64].unsqueeze(1).broadcast_to([r, 4, 64]), ALU.mult),
             reads=[ks("qt"), "gqb"], writes=[ks("qt")])
        P.dma("sync", self.kv_s[g][row0:row0 + r, :], kvt[:r, :], reads=[ks("kvt")], writes=[("kv_s", g, ti)])
        P.dma("sync", self.q_s[g][row0:row0 + r, :], qt[:r, :], reads=[ks("qt")], writes=[("q_s", g, ti)])

    def phase_b2(self):
        P, nc, w = self.P, self.nc, self.w
        TS = 16
        T = {}

        def tl(nm, shape=(128, TS), dt=F32):
            T[nm] = self.sbp(nm, list(shape), dt)
            return T[nm]
        for nm in ("a_re", "a_im", "ldt"):
            tl(nm)
            P.dma("sync", T[nm][:], w["ssm_" + nm].rearrange("s q -> q s"), writes=[nm], allow_slow_non_contiguous=True)
        V = lambda fn, r, wr: P.op("vector", fn, reads=r, writes=wr)
        A = lambda fn, r, wr: P.op("scalar", fn, reads=r, writes=wr)
        tt = lambda o, a, b, op: V(lambda e: e.tensor_tensor(T[o][:], T[a][:], T[b][:], op), [a, b], [o])
        for nm in ("dt", "ar", "ai", "mag", "rs", "rc", "sn", "cs", "abr", "abi", "nabi", "sq1", "sq2", "inv", "em1", "t1", "t2", "f_re", "f_im"):
            tl(nm)
        A(lambda e: e.activation(T["dt"][:], T["ldt"][:], AF.Exp), ["ldt"], ["dt"])
        tt("ar", "a_re", "dt", ALU.mult)
        tt("ai", "a_im", "dt", ALU.mult)
        A(lambda e: e.activation(T["mag"][:], T["ar"][:], AF.Exp), ["ar"], ["mag"])
        PI = float(np.pi)
        tl("rtmp")
        T["rint"] = self.sbp("rint", [128, TS], mybir.dt.int32)
        tl("rmask")

        def reduce_generic(dst_ap, src_ap, off, tmp_ap, int_ap, mask_ap, kd):
            V(lambda e: e.tensor_scalar(dst_ap, src_ap, off, None, ALU.add), [kd, "ai", "mm0"], [kd])
            V(lambda e: e.tensor_scalar(tmp_ap, dst_ap, 1.0 / (2 * PI), None, ALU.mult), [kd], ["mm1"])
            V(lambda e: e.tensor_copy(int_ap, tmp_ap), ["mm1"], ["g_int"])
            V(lambda e: e.tensor_copy(tmp_ap, int_ap), ["g_int"], ["mm1"])
            V(lambda e: e.scalar_tensor_tensor(dst_ap, tmp_ap, -2 * PI, dst_ap, ALU.mult, ALU.add), ["mm1", kd], [kd])
            V(lambda e: e.tensor_scalar(mask_ap, dst_ap, PI, None, ALU.is_gt), [kd], ["mm2"])
            V(lambda e: e.scalar_tensor_tensor(dst_ap, mask_ap, -2 * PI, dst_ap, ALU.mult, ALU.add), ["mm2", kd], [kd])
            V(lambda e: e.tensor_scalar(mask_ap, dst_ap, -PI, None, ALU.is_lt), [kd], ["mm2"])
            V(lambda e: e.scalar_tensor_tensor(dst_ap, mask_ap, 2 * PI, dst_ap, ALU.mult, ALU.add), ["mm2", kd], [kd])
            V(lambda e: e.tensor_scalar(dst_ap, dst_ap, PI, -PI, ALU.min, ALU.max), [kd], [kd])

        def reduce_angle(dst, off):
            reduce_generic(T[dst][:], T["ai"][:], off, T["rtmp"][:], T["rint"][:], T["rmask"][:], dst)
        reduce_angle("rs", 0.0)
        reduce_angle("rc", 0.5 * PI)
        A(lambda e: e.activation(T["sn"][:], T["rs"][:], AF.Sin), ["rs"], ["sn"])
        A(lambda e: e.activation(T["cs"][:], T["rc"][:], AF.Sin), ["rc"], ["cs"])
        tt("abr", "mag", "cs", ALU.mult)
        tt("abi", "mag", "sn", ALU.mult)
        tt("sq1", "a_re", "a_re", ALU.mult)
        tt("sq2", "a_im", "a_im", ALU.mult)
        tt("inv", "sq1", "sq2", ALU.add)
        V(lambda e: e.reciprocal(T["inv"][:], T["inv"][:]), ["inv"], ["inv"])
        V(lambda e: e.tensor_scalar(T["em1"][:], T["abr"][:], -1.0, None, ALU.add), ["abr"], ["em1"])
        tt("t1", "em1", "a_re", ALU.mult)
        tt("t2", "abi", "a_im", ALU.mult)
        tt("t1", "t1", "t2", ALU.add)
        tt("f_re", "t1", "inv", ALU.mult)
        tt("t1", "abi", "a_re", ALU.mult)
        tt("t2", "em1", "a_im", ALU.mult)
        tt("t1", "t1", "t2", ALU.subtract)
        tt("f_im", "t1", "inv", ALU.mult)
        tv = tl("tvals", (128, ST))
        P.dma("sync", tv[:], w["tvals"][:, :], writes=["tvals"])
        E = [tl("E_re", (128, TS, ST)), tl("E_im", (128, TS, ST))]
        FE = [tl("FE_re", (128, TS, ST)), tl("FE_im", (128, TS, ST))]
        mm_ = [tl("mm%d" % i, (128, ST)) for i in range(4)]
        ang, g_tmp, g_msk = mm_[0], mm_[1], mm_[2]
        g_int = self.sbp("g_int", [128, ST], mybir.dt.int32)
        for s_ in range(TS):
            V(lambda e, s_=s_: e.tensor_scalar(ang[:], tv[:], T["ai"][:, s_:s_ + 1], None, ALU.mult), ["tvals", "ai"], ["mm0"])
            for j, off in ((1, 0.0), (0, 0.5 * PI)):
                reduce_generic(E[j][:, s_, :], ang[:], off, g_tmp[:], g_int[:], g_msk[:], ("E", j, s_))
                A(lambda e, j=j, s_=s_: e.activation(E[j][:, s_, :], E[j][:, s_, :], AF.Sin), [("E", j, s_)], [("E", j, s_)])
            fr, fi = T["f_re"][:, s_:s_ + 1], T["f_im"][:, s_:s_ + 1]
            V(lambda e, s_=s_, fi=fi: e.tensor_scalar(g_tmp[:], E[1][:, s_, :], fi, None, ALU.mult), [("E", 1, s_), "f_im"], ["mm1"])
            V(lambda e, s_=s_, fr=fr: e.scalar_tensor_tensor(FE[0][:, s_, :], E[0][:, s_, :], fr, g_tmp[:], ALU.mult, ALU.add), [("E", 0, s_), "f_re", "mm1"], ["FE"])
            V(lambda e, s_=s_, fr=fr: e.tensor_scalar(g_tmp[:], E[1][:, s_, :], fr, None, ALU.mult), [("E", 1, s_), "f_re"], ["mm1"])
            V(lambda e, s_=s_, fi=fi: e.scalar_tensor_tensor(FE[1][:, s_, :], E[0][:, s_, :], fi, g_tmp[:], ALU.mult, ALU.subtract), [("E", 0, s_), "f_im", "mm1"], ["FE"])
        zs = [tl("zs0", (128, ST)), tl("zs1", (128, ST))]
        NL = 1
        PR, PIm, NPI = tl("PR", (128, NL, TS)), tl("PIm", (128, NL, TS)), tl("NPI", (128, NL, TS))
        V(lambda e: e.tensor_copy(PR[:, 0, :], T["abr"][:]), ["abr"], ["PR"])
        V(lambda e: e.tensor_copy(PIm[:, 0, :], T["abi"][:]), ["abi"], ["PIm"])
        for k in range(1, NL):
            V(lambda e, k=k: e.tensor_tensor(T["t1"][:], PR[:, k - 1, :], PR[:, k - 1, :], ALU.mult), ["PR"], ["t1"])
            V(lambda e, k=k: e.tensor_tensor(T["t2"][:], PIm[:, k - 1, :], PIm[:, k - 1, :], ALU.mult), ["PIm"], ["t2"])
            V(lambda e, k=k: e.tensor_tensor(PIm[:, k, :], PR[:, k - 1, :], PIm[:, k - 1, :], ALU.mult), ["PR", "PIm"], ["PIm"])
            V(lambda e, k=k: e.tensor_scalar(PIm[:, k, :], PIm[:, k, :], 2.0, None, ALU.mult), ["PIm"], ["PIm"])
            V(lambda e, k=k: e.tensor_tensor(PR[:, k, :], T["t1"][:], T["t2"][:], ALU.subtract), ["t1", "t2"], ["PR"])
        V(lambda e: e.tensor_scalar(NPI[:], PIm[:], -1.0, None, ALU.mult), ["PIm"], ["NPI"])
        Bm = [tl("Bm_re", (128, TS, 128), BF16), tl("Bm_im", (128, TS, 128), BF16)]
        Cm = [tl("Cm_re", (128, TS, 128), BF16), tl("Cm_im", (128, TS, 128), BF16)]
        for t_, nm in ((Bm[0], "Bm_re"), (Bm[1], "Bm_im"), (Cm[0], "Cm_re"), (Cm[1], "Cm_im")):
            P.dma("gpsimd", t_[:], w[nm].rearrange("s r c -> r s c"), writes=[nm])
        wglu = tl("wglu", (128, 4, 512), BF16)
        P.dma("gpsimd", wglu[:], w["w_glu"].rearrange("(k p) f -> p k f", p=128), writes=["wglu"])
        dcol, bcol = tl("dcol", (128, 4)), tl("bcol", (128, 4))
        P.dma("sync", dcol[:], w["ssm_d"].rearrange("(c p) -> p c", p=128), writes=["dcol"], allow_slow_non_contiguous=True)
        P.dma("sync", bcol[:], w["b_glu"].rearrange("(c p) -> p c", p=128), writes=["bcol"], allow_slow_non_contiguous=True)
        car = [tl("car_re"), tl("car_im")]
        V(lambda e: e.memset(car[0][:], 0.0), [], ["car"])
        V(lambda e: e.memset(car[1][:], 0.0), [], ["car"])
        h0s = [tl("h0s_re", (128, NS, TS)), tl("h0s_im", (128, NS, TS))]
        for j in range(2):
            for b in range(NS):
                P.dma("sync", h0s[j][:, b, :], self.st_in[j][b].rearrange("(s q) -> q s", q=128), writes=["h0s"], allow_slow_non_contiguous=True)
        sts = [tl("sts_re", (128, NS, TS)), tl("sts_im", (128, NS, TS))]
        uT = tl("uT", (128, 4, ST), BF16)
        pp = [[tl("pp%d%d" % (i, j), (128, ST)) for j in range(2)] for i in range(2)]
        tmp = [mm_[2], mm_[3]]
        sbr, sbi = tl("sbr", (128, ST), BF16), tl("sbi", (128, ST), BF16)
        yraw, x2, tg = tl("yraw", (128, ST)), tl("x2", (128, ST)), tl("tg", (128, ST))
        ygf, ygb, yss = tl("ygf", (128, 4, ST)), tl("ygb", (128, 4, ST), BF16), tl("yss", (128, 4, ST), BF16)
        bank = self.bank
        self._b2 = dict(E=E, FE=FE, zs=zs, mm=mm_, T=T, PR=PR, PIm=PIm, NPI=NPI, Bm=Bm, Cm=Cm, wglu=wglu, dcol=dcol, bcol=bcol, car=car, h0s=h0s, sts=sts,
                        uT=uT, pp=pp, tmp=tmp, sbr=sbr, sbi=sbi, yraw=yraw, x2=x2, tg=tg, ygf=ygf, ygb=ygb, yss=yss)
        for ti, (kind, t0, n) in enumerate(self.srcs):
            self._b2_tile(ti, kind, t0, n)
        P.dma("sync", self.st_out_p[0].rearrange("s q -> q s"), car[0][:], reads=["car"], writes=["stp0"], allow_slow_non_contiguous=True)
        P.dma("sync", self.st_out_p[1].rearrange("s q -> q s"), car[1][:], reads=["car"], writes=["stp1"], allow_slow_non_contiguous=True)
        for j in range(2):
            for b in range(NS):
                P.dma("sync", self.st_out_s[j][b].rearrange("(s q) -> q s", q=128), sts[j][:, b, :], reads=["sts"], writes=[("stso", j, b)], allow_slow_non_contiguous=True)

    def _b2_tile(self, ti, kind, t0, n):
        P = self.P
        B = self._b2
        T, PR, PIm, NPI, Bm, Cm, car = B["T"], B["PR"], B["PIm"], B["NPI"], B["Bm"], B["Cm"], B["car"]
        uT, pp, tmp, sbr, sbi = B["uT"], B["pp"], B["tmp"], B["sbr"], B["sbi"]
        bank = self.bank
        gt0 = t0 if kind == "p" else SEQ
        V = lambda fn, r, wr: P.op("vector", fn, reads=r, writes=wr)
        is_own = kind == "s" or t0 >= self.own0
        P.dma("sync", uT[:, :, :n], self.uT_s[:, :, gt0:gt0 + n].rearrange("k p n -> p k n"), reads=[("uT_s", ti)], writes=["uT"])
        for c in range(4):
            for s4 in range(4):
                s = 4 * c + s4
                zb = (bank[0], bank[1]) if s % 2 == 0 else (bank[4], bank[5])
                zk = (("bank", 0), ("bank", 1)) if s % 2 == 0 else (("bank", 4), ("bank", 5))
                for j in range(2):
                    P.op("tensor", lambda e, j=j, s=s, c=c, zb=zb: e.matmul(zb[j][:, :n], Bm[j][:, s, :], uT[:, c, :n], start=True, stop=True),
                         reads=["uT", "Bm_re", "Bm_im"], writes=[zk[j]])
                fre, fim = T["f_re"][:, s:s + 1], T["f_im"][:, s:s + 1]
                cur = pp[0]
                if kind == "p":
                    E, FE, zs, mm = B["E"], B["FE"], B["zs"], B["mm"]
                    G = lambda fn, r, wr: P.op("gpsimd", fn, reads=r, writes=wr)
                    V(lambda e, s=s, zb=zb: e.tensor_tensor(mm[0][:, :n], FE[0][:, s, :n], zb[0][:, :n], ALU.mult), ["FE", zk[0]], ["mm0"])
                    V(lambda e, s=s, zb=zb: e.tensor_tensor(mm[1][:, :n], FE[1][:, s, :n], zb[1][:, :n], ALU.mult), ["FE", zk[1]], ["mm1"])
                    G(lambda e: e.tensor_tensor(mm[0][:, :n], mm[0][:, :n], mm[1][:, :n], ALU.subtract), ["mm0", "mm1"], ["mm0"])
                    V(lambda e, s=s, zb=zb: e.tensor_tensor(mm[2][:, :n], FE[0][:, s, :n], zb[1][:, :n], ALU.mult), ["FE", zk[1]], ["mm2"])
                    V(lambda e, s=s, zb=zb: e.tensor_tensor(mm[3][:, :n], FE[1][:, s, :n], zb[0][:, :n], ALU.mult), ["FE", zk[0]], ["mm3"])
                    G(lambda e: e.tensor_tensor(mm[2][:, :n], mm[2][:, :n], mm[3][:, :n], ALU.add), ["mm2", "mm3"], ["mm2"])
                    rho = T["mag"][:, s:s + 1].broadcast_to([128, n])
                    V(lambda e, s=s, rho=rho: e.tensor_tensor_scan(pp[1][0][:, :n], rho, mm[0][:, :n], car[0][:, s:s + 1], ALU.mult, ALU.add),
                      ["mm0", "car", "mag"], ["pp10"])
                    V(lambda e, s=s, rho=rho: e.tensor_tensor_scan(pp[1][1][:, :n], rho, mm[2][:, :n], car[1][:, s:s + 1], ALU.mult, ALU.add),
                      ["mm2", "car", "mag"], ["pp11"])
                    wr_, wi_ = pp[1][0], pp[1][1]
                    if not is_own:
                        cl = slice(n - 1, n)
                        V(lambda e, s=s: e.tensor_tensor(mm[0][:, 0:1], wi_[:, cl], E[1][:, s, cl], ALU.mult), [("E", 1, s), "pp11"], ["mm0"])
                        V(lambda e, s=s: e.scalar_tensor_tensor(car[0][:, s:s + 1], wr_[:, cl], E[0][:, s, cl], mm[0][:, 0:1], ALU.mult, ALU.subtract),
                          [("E", 0, s), "pp10", "mm0"], ["car"])
                        V(lambda e, s=s: e.tensor_tensor(mm[2][:, 0:1], wr_[:, cl], E[1][:, s, cl], ALU.mult), [("E", 1, s), "pp10"], ["mm2"])
                        V(lambda e, s=s: e.scalar_tensor_tensor(car[1][:, s:s + 1], wi_[:, cl], E[0][:, s, cl], mm[2][:, 0:1], ALU.mult, ALU.add),
                          [("E", 0, s), "pp11", "mm2"], ["car"])
                        continue
                    G(lambda e, s=s: e.tensor_tensor(mm[0][:, :n], E[0][:, s, :n], wr_[:, :n], ALU.mult), [("E", 0, s), "pp10"], ["mm0"])
                    G(lambda e, s=s: e.tensor_tensor(mm[1][:, :n], E[1][:, s, :n], wi_[:, :n], ALU.mult), [("E", 1, s), "pp11"], ["mm1"])
                    G(lambda e: e.tensor_tensor(cur[0][:, :n], mm[0][:, :n], mm[1][:, :n], ALU.subtract), ["mm0", "mm1"], ["pp00"])
                    V(lambda e, s=s: e.tensor_tensor(mm[2][:, :n], E[0][:, s, :n], wi_[:, :n], ALU.mult), [("E", 0, s), "pp11"], ["mm2"])
                    V(lambda e, s=s: e.tensor_tensor(mm[3][:, :n], E[1][:, s, :n], wr_[:, :n], ALU.mult), [("E", 1, s), "pp10"], ["mm3"])
                    V(lambda e: e.tensor_tensor(cur[1][:, :n], mm[2][:, :n], mm[3][:, :n], ALU.add), ["mm2", "mm3"], ["pp01"])
                    ci = 0
                else:
                    V(lambda e, zb=zb, fim=fim: e.tensor_scalar(tmp[0][:, :n], zb[1][:, :n], fim, None, ALU.mult), [zk[1], "f_im"], ["mm2"])
                    V(lambda e, zb=zb, fre=fre, cur=cur: e.scalar_tensor_tensor(cur[0][:, :n], zb[0][:, :n], fre, tmp[0][:, :n], ALU.mult, ALU.subtract),
                      [zk[0], "f_re", "mm2"], ["pp00"])
                    V(lambda e, zb=zb, fim=fim: e.tensor_scalar(tmp[1][:, :n], zb[0][:, :n], fim, None, ALU.mult), [zk[0], "f_im"], ["mm3"])
                    V(lambda e, zb=zb, fre=fre, cur=cur: e.scalar_tensor_tensor(cur[1][:, :n], zb[1][:, :n], fre, tmp[1][:, :n], ALU.mult, ALU.add),
                      [zk[1], "f_re", "mm3"], ["pp01"])
                    a0, b0, nb0 = PR[:, 0, s:s + 1], PIm[:, 0, s:s + 1], NPI[:, 0, s:s + 1]
                    hr, hi = B["h0s"][0][:, :, s], B["h0s"][1][:, :, s]
                    w0 = n
                    V(lambda e, cur=cur, hr=hr, a0=a0: e.scalar_tensor_tensor(cur[0][:, :w0], hr, a0, cur[0][:, :w0], ALU.mult, ALU.add), ["pp00", "h0s", "PR"], ["pp00"])
                    V(lambda e, cur=cur, hi=hi, nb0=nb0: e.scalar_tensor_tensor(cur[0][:, :w0], hi, nb0, cur[0][:, :w0], ALU.mult, ALU.add), ["pp00", "h0s", "NPI"], ["pp00"])
                    V(lambda e, cur=cur, hi=hi, a0=a0: e.scalar_tensor_tensor(cur[1][:, :w0], hi, a0, cur[1][:, :w0], ALU.mult, ALU.add), ["pp01", "h0s", "PR"], ["pp01"])
                    V(lambda e, cur=cur, hr=hr, b0=b0: e.scalar_tensor_tensor(cur[1][:, :w0], hr, b0, cur[1][:, :w0], ALU.mult, ALU.add), ["pp01", "h0s", "PIm"], ["pp01"])
                    ci = 0
                X = pp[ci]
                kx = ["pp%d0" % ci, "pp%d1" % ci]
                if kind == "p":
                    P.op("scalar", lambda e, X=X, s=s: e.copy(car[0][:, s:s + 1], X[0][:, n - 1:n]), reads=[kx[0]], writes=["car"])
                    P.op("scalar", lambda e, X=X, s=s: e.copy(car[1][:, s:s + 1], X[1][:, n - 1:n]), reads=[kx[1]], writes=["car"])
                else:
                    P.op("scalar", lambda e, X=X, s=s: e.copy(B["sts"][0][:, :, s], X[0][:, :n]), reads=[kx[0]], writes=["sts"])
                    P.op("scalar", lambda e, X=X, s=s: e.copy(B["sts"][1][:, :, s], X[1][:, :n]), reads=[kx[1]], writes=["sts"])
                P.op("scalar", lambda e, X=X: e.copy(sbr[:, :n], X[0][:, :n]), reads=[kx[0]], writes=["sbr"])
                P.op("scalar", lambda e, X=X: e.mul(sbi[:, :n], X[1][:, :n], -1.0), reads=[kx[1]], writes=["sbi"])

                def mmy(e, s=s, s4=s4):
                    e.matmul(bank[7][:, :n], Cm[0][:, s, :], sbr[:, :n], start=(s4 == 0), stop=False)
                    return e.matmul(bank[7][:, :n], Cm[1][:, s, :], sbi[:, :n], start=False, stop=(s4 == 3))
                P.op("tensor", mmy, reads=["sbr", "sbi", "Cm_re", "Cm_im"], writes=[("bank", 7)])
            if not is_own:
                continue
            yraw, x2, tg, ygf, ygb = B["yraw"], B["x2"], B["tg"], B["ygf"], B["ygb"]
            V(lambda e, c=c: e.scalar_tensor_tensor(yraw[:, :n], uT[:, c, :n], B["dcol"][:, c:c + 1], bank[7][:, :n], ALU.mult, ALU.add),
              ["uT", "dcol", ("bank", 7)], ["yraw"])
            P.op("scalar", lambda e: e.activation(x2[:, :n], yraw[:, :n], AF.Square), reads=["yraw"], writes=["x2"])
            V(lambda e: e.tensor_scalar(x2[:, :n], x2[:, :n], 0.044715, 1.0, ALU.mult, ALU.add), ["x2"], ["x2"])
            V(lambda e: e.tensor_tensor(tg[:, :n], x2[:, :n], yraw[:, :n], ALU.mult), ["x2", "yraw"], ["tg"])
            P.op("scalar", lambda e: e.activation(tg[:, :n], tg[:, :n], AF.Sigmoid, scale=1.5957691216057308), reads=["tg"], writes=["tg"])
            V(lambda e, c=c: e.tensor_tensor(ygf[:, c, :n], yraw[:, :n], tg[:, :n], ALU.mult), ["tg", "yraw"], [("ygf", c)])
            P.op("gpsimd", lambda e, c=c: e.tensor_copy(ygb[:, c, :n], ygf[:, c, :n]), reads=[("ygf", c)], writes=[("ygb", c)])
        if not is_own:
            return
        wglu, yss = B["wglu"], B["yss"]
        for m in range(4):
            bg = bank[2 + m % 2]

            def mmg(e, m=m, bg=bg):
                last = None
                for k in range(4):
                    last = e.matmul(bg[:, :n], wglu[:, k, m * 128:(m + 1) * 128], ygb[:, k, :n], start=(k == 0), stop=(k == 3))
                return last
            P.op("tensor", mmg, reads=[("ygb", k) for k in range(4)] + ["wglu"], writes=[("bank", 2 + m % 2)])
            P.op("scalar", lambda e, m=m, bg=bg: e.activation(tg[:, :n], bg[:, :n], AF.Sigmoid, bias=B["bcol"][:, m:m + 1]),
                 reads=[("bank", 2 + m % 2), "bcol"], writes=["tg"])
            V(lambda e, m=m: e.tensor_tensor(yss[:, m, :n], ygf[:, m, :n], tg[:, :n], ALU.mult), ["tg", ("ygf", m)], [("yss", m)])
        P.dma("sync", self.yssT_s[:, :, gt0:gt0 + n].rearrange("k p n -> p k n"), yss[:, :, :n],
              reads=[("yss", m) for m in range(4)], writes=[("yssT_s", ti)])

    def phase_b3(self):
        P, w = self.P, self.w
        tl = self.sbp
        A = {}
        A["mask"] = [tl("mask_cur", [128, 256], BF16), tl("mask_prev", [128, 256], BF16), tl("mask_halo", [128, 256], BF16)]
        P.dma("gpsimd", A["mask"][0][:], w["mask_cur"][:, :], writes=["mask"])
        P.dma("gpsimd", A["mask"][1][:], w["mask_prev"][:, :], writes=["mask"])
        P.dma("gpsimd", A["mask"][2][:], w["mask_halo"][:, :], writes=["mask"])
        A["identB"] = tl("identB", [128, 128], BF16)
        P.dma("gpsimd", A["identB"][:], w["ident"][:, :], writes=["identB"])
        A["kv"] = [tl("kvA%d" % i, [128, 512], F32) for i in range(2)]
        A["qf"] = [tl("qf%d" % i, [128, 256], F32) for i in range(2)]
        A["kT"] = [tl("kT%d" % i, [128, 2, 128], BF16) for i in range(3)]
        A["Vz"] = [tl("Vz%d" % i, [128, 4, 128], BF16) for i in range(3)]
        A["qTz"] = [[tl("qTz%d%d" % (i, hh), [128, 2, 128], BF16) for hh in range(2)] for i in range(2)]
        A["onesZ"] = [tl("onesZ%d" % hh, [128, 128], BF16) for hh in range(2)]
        for i in range(3):
            P.op("vector", lambda e, i=i: e.memset(A["Vz"][i][:], 0.0), writes=[("Vb", i)])
        for i in range(2):
            for hh in range(2):
                P.op("vector", lambda e, i=i, hh=hh: e.memset(A["qTz"][i][hh][:], 0.0), writes=[("qT", i)])
        for hh in range(2):
            P.op("vector", lambda e, hh=hh: e.memset(A["onesZ"][hh][:], 0.0), writes=["onesZ"])
            P.op("vector", lambda e, hh=hh: e.memset(A["onesZ"][hh][:, 64 * hh:64 * hh + 64], 1.0), reads=["onesZ"], writes=["onesZ"])
        A["Pe"] = [tl("Pe%d" % i, [128, 512], BF16) for i in range(2)]
        A["Pm"] = [tl("Pm%d" % i, [128, 512], BF16) for i in range(2)]
        BLK = 2048
        A["accN"] = tl("accN", [128, 2, BLK], F32)
        A["accD"] = tl("accD", [128, 2, BLK], F32)
        A["yat"] = tl("yat", [128, 2, BLK], BF16)
        self._b3 = A
        self.pi = 0
        self.ui = 0
        V = lambda fn, r, wr: P.op("vector", fn, reads=r, writes=wr)
        groups = ((128, 1), (512, 4), (2048, 16))
        nblk = SEQ // BLK
        for bb in range(nblk - 1, nblk):
            V(lambda e: e.memset(A["accN"][:], 0.0), [], ["accN"])
            V(lambda e: e.memset(A["accD"][:], 0.0), [], ["accD"])
            for g, (W, dl) in enumerate(groups):
                span = 128 * dl
                for r in range(dl):
                    prev = None
                    for bk in range(BLK // span):
                        base = bb * BLK + bk * span
                        if prev is None and base >= span:
                            prev = self._b3_prep(g, base - span, r, dl, 128)
                        cur = self._b3_prep(g, base, r, dl, 128)
                        qT = self._b3_q(g, base, r, dl, 128)
                        pm = 2 if (base - span) < self.own0 else 1
                        ksets = [(cur, 128, 0)] + ([(prev, 128, pm)] if prev is not None else [])
                        c0 = bk * span + r
                        views = lambda acc, pr, c0=c0, dl=dl, span=span: acc[:, pr, c0:c0 + 127 * dl + 1:dl] if dl > 1 else acc[:, pr, c0:c0 + 128]
                        self._b3_unit(qT, 128, ksets, views)
                        prev = cur
            self._b3_norm(bb * BLK, BLK, A["accN"], A["accD"], A["yat"])
        V(lambda e: e.memset(A["accN"][:], 0.0), [], ["accN"])
        V(lambda e: e.memset(A["accD"][:], 0.0), [], ["accD"])
        for b in range(NS):
            for g, (W, dl) in enumerate(groups):
                cset = self._b3_prep(g, None, 0, dl, 128, cache=(g, b))
                sset = self._b3_prep(g, SEQ + b, 0, 1, 1)
                qT = self._b3_q(g, SEQ + b, 0, 1, 1)
                views = lambda acc, pr, b=b: acc[:, pr, b:b + 1]
                self._b3_unit(qT, 1, [(cset, 128, None), (sset, 1, None)], views)
        self._b3_norm(SEQ, NS, A["accN"], A["accD"], A["yat"])

    def _b3_prep(self, g, base, r, dl, nk, cache=None):
        P, A, bank = self.P, self._b3, self.bank
        i = self.pi % 3
        j = self.pi % 2
        self.pi += 1
        kv, kT, Vz = A["kv"][j], A["kT"][i], A["Vz"][i]
        if cache is not None:
            cg, b = cache
            W = (128, 512, 2048)[cg]
            src = self.cache[cg][b, :, :].rearrange("(m d) f -> d m f", d=dl)[0]
            rd = []
        elif dl > 1:
            src = self.kv_s[g][base:base + 128 * dl, :].rearrange("(m d) f -> d m f", d=dl)[r]
            rd = [("kv_s", g, ti) for ti in range(len(self.srcs))]
        else:
            src = self.kv_s[g][base:base + nk, :]
            rd = [("kv_s", g, ti) for ti in range(len(self.srcs))]
        P.dma("sync", kv[:nk, :], src, reads=rd, writes=[("kvA", j)])
        tb = bank[6 + j]
        for pr in range(2):
            P.op("tensor", lambda e, pr=pr: e.transpose(tb[:, pr * 128:pr * 128 + nk], kv[:nk, pr * 128:(pr + 1) * 128], self.identF[:nk, :nk]),
                 reads=[("kvA", j), "identF"], writes=[("bank", 6 + j)])
        P.op("scalar", lambda e: e.copy(kT[:, :, :nk], tb[:, 0:256].rearrange("p (a n) -> p a n", a=2)[:, :, :nk]),
             reads=[("bank", 6 + j)], writes=[("kT", i)])
        v4 = kv[:nk, 256:512].rearrange("p (h d) -> p h d", d=64)
        P.op("gpsimd", lambda e: e.tensor_copy(Vz[:nk, 0::2, 0:64], v4[:, 0::2, :]), reads=[("kvA", j)], writes=[("Vb", i)])
        P.op("gpsimd", lambda e: e.tensor_copy(Vz[:nk, 1::2, 64:128], v4[:, 1::2, :]), reads=[("kvA", j)], writes=[("Vb", i)])
        return i

    def _b3_q(self, g, base, r, dl, nq):
        P, A, bank = self.P, self._b3, self.bank
        j = self.ui % 2
        qf, qTz = A["qf"][j], A["qTz"][j]
        if dl > 1:
            src = self.q_s[g][base:base + 128 * dl, :].rearrange("(m d) f -> d m f", d=dl)[r]
        else:
            src = self.q_s[g][base:base + nq, :]
        P.dma("sync", qf[:nq, :], src, reads=[("q_s", g, ti) for ti in range(len(self.srcs))], writes=[("qf", j)])
        tb = bank[6 + j]
        for pr in range(2):
            P.op("tensor", lambda e, pr=pr: e.transpose(tb[:, 256 + pr * 128:256 + pr * 128 + nq], qf[:nq, pr * 128:(pr + 1) * 128], self.identF[:nq, :nq]),
                 reads=[("qf", j), "identF"], writes=[("bank", 6 + j)])
        P.op("vector", lambda e: e.tensor_copy(qTz[0][0:64, :, :nq], tb[0:64, 256:512].rearrange("p (a n) -> p a n", a=2)[:, :, :nq]),
             reads=[("bank", 6 + j)], writes=[("qT", j)])
        P.op("vector", lambda e: e.tensor_copy(qTz[1][64:128, :, :nq], tb[64:128, 256:512].rearrange("p (a n) -> p a n", a=2)[:, :, :nq]),
             reads=[("bank", 6 + j)], writes=[("qT", j)])
        return j

    def _b3_unit(self, qi, nq, ksets, views):
        P, A, bank = self.P, self._b3, self.bank
        qTz = A["qTz"][qi]
        V = lambda fn, r, wr: P.op("vector", fn, reads=r, writes=wr)
        for pr in range(2):
            u = self.ui
            self.ui += 1
            j = u % 2
            bS, bN, bD = bank[0 + j], bank[2 + j], bank[4 + j]
            kS, kN, kD = ("bank", j), ("bank", 2 + j), ("bank", 4 + j)
            Pe, Pm = A["Pe"][j], A["Pm"][j]

            def mms(e, pr=pr):
                last = None
                for si, (ki, nk, mk) in enumerate(ksets):
                    kT = A["kT"][ki]
                    for hh in range(2):
                        slot = si * 2 + hh
                        last = e.matmul(bS[:nk, slot * nq:(slot + 1) * nq], kT[:, pr, :nk],
                                        qTz[hh][:, pr, :nq], start=True, stop=True)
                return last
            P.op("tensor", mms, reads=[("kT", ki) for ki, _, _ in ksets] + [("qT", qi)], writes=[kS])
            for si, (ki, nk, mk) in enumerate(ksets):
                lo, hi = si * 2 * nq, (si + 1) * 2 * nq
                P.op("scalar", lambda e, nk=nk, lo=lo, hi=hi: e.activation(Pe[:nk, lo:hi], bS[:nk, lo:hi], AF.Exp),
                     reads=[kS], writes=[("Pe", j, si)])
                if mk is not None:
                    P.op("gpsimd", lambda e, nk=nk, lo=lo, hi=hi, mk=mk: e.tensor_tensor(Pm[:nk, lo:hi], Pe[:nk, lo:hi], A["mask"][mk][:nk, :], ALU.mult),
                         reads=[("Pe", j, si), "mask"], writes=[("Pm", j, si)])
                else:
                    P.op("gpsimd", lambda e, nk=nk, lo=lo, hi=hi: e.tensor_copy(Pm[:nk, lo:hi], Pe[:nk, lo:hi]),
                         reads=[("Pe", j, si)], writes=[("Pm", j, si)])

            def mmav(e, pr=pr):
                last = None
                ns = len(ksets)
                tot = 2 * ns
                for lhs_of, bO in ((lambda ki, nk, hh: A["Vz"][ki][:nk, 2 * pr + hh, :], bN), (lambda ki, nk, hh: A["onesZ"][hh][:nk, :], bD)):
                    c = 0
                    for hh in range(2):
                        for si, (ki, nk, mk) in enumerate(ksets):
                            slot = si * 2 + hh
                            last = e.matmul(bO[:, :nq], lhs_of(ki, nk, hh), Pm[:nk, slot * nq:(slot + 1) * nq],
                                            start=(c == 0), stop=(c == tot - 1))
                            c += 1
                return last
            P.op("tensor", mmav, reads=[("Vb", ki) for ki, _, _ in ksets] + [("Pm", j, si) for si in range(len(ksets))] + ["onesZ"],
                 writes=[kN, kD])
            vn, vd = views(A["accN"], pr), views(A["accD"], pr)
            V(lambda e, vn=vn: e.tensor_tensor(vn, vn, bN[:, :nq], ALU.add), [kN, "accN"], ["accN"])
            V(lambda e, vd=vd: e.tensor_tensor(vd, vd, bD[:, :nq], ALU.add), [kD, "accD"], ["accD"])

    def _b3_norm(self, col0, n, accN, accD, yat):
        P = self.P
        V = lambda fn, r, wr: P.op("vector", fn, reads=r, writes=wr)
        V(lambda e: e.reciprocal(accD[:, :, :n], accD[:, :, :n]), ["accD"], ["accD"])
        V(lambda e: e.tensor_tensor(yat[:, :, :n], accN[:, :, :n], accD[:, :, :n], ALU.mult), ["accN", "accD"], ["yat"])
        P.dma("sync", self.yatT_s[:, :, col0:col0 + n].rearrange("k p n -> p k n"), yat[:, :, :n], reads=["yat"], writes=[("yatT_s", col0)])

    def phase_b4(self):
        P, w = self.P, self.w
        tl = self.sbp
        wing = tl("wing", [128, 8, 2048], BF16)
        for k in range(8):
            P.dma("gpsimd", wing[:, k, :], w["w_in"][k * 128:(k + 1) * 128, 2816:4864], writes=[("wing", k)], max_dma_last_dim=4096)
        wsp, wap, wo = tl("wsp", [128, 4, D], BF16), tl("wap", [128, 2, D], BF16), tl("wo", [128, 8, D], BF16)
        P.dma("gpsimd", wsp[:], w["w_ssm_proj"].rearrange("(k p) f -> p k f", p=128), writes=["wsp"])
        P.dma("gpsimd", wap[:], w["w_attn_proj"].rearrange("(k p) f -> p k f", p=128), writes=["wap"])
        P.dma("gpsimd", wo[:], w["w_o"].rearrange("(k p) f -> p k f", p=128), writes=["wo"])
        B = dict(wing=wing, wsp=wsp, wap=wap, wo=wo,
                 hT=tl("hT4", [128, 8, ST], BF16), yss=tl("yss4", [128, 4, ST], BF16), yat=tl("yat4", [128, 2, ST], BF16),
                 xT=tl("xT4", [128, 8, ST], F32), mixed=tl("mixed", [128, 8, ST], BF16),
                 sg=[tl("sg%d" % i, [128, ST], F32) for i in range(2)], tm=[tl("tm%d" % i, [128, ST], F32) for i in range(2)])
        self._b4 = B
        for ti, (kind, t0, n) in enumerate(self.srcs):
            if ti in self.own_tis:
                self._b4_tile(ti, kind, t0, n)

    def _b4_tile(self, ti, kind, t0, n):
        P, B, bank = self.P, self._b4, self.bank
        gt0 = t0 if kind == "p" else SEQ
        hT, yss, yat, xT, mixed, sg, tm = B["hT"], B["yss"], B["yat"], B["xT"], B["mixed"], B["sg"], B["tm"]
        wing, wsp, wap, wo = B["wing"], B["wsp"], B["wap"], B["wo"]
        V = lambda fn, r, wr: P.op("vector", fn, reads=r, writes=wr)
        fm = lambda t: t[:, :, gt0:gt0 + n].rearrange("k p n -> p k n")
        P.dma("sync", hT[:, :, :n], fm(self.hT_s), writes=["hT4"])
        P.dma("sync", yss[:, :, :n], fm(self.yssT_s), writes=["yss4"])
        P.dma("sync", yat[:, :, :n], fm(self.yatT_s), writes=["yat4"])
        P.dma("sync", xT[:, :, :n], fm(self.x1T), reads=[("x1T", ti)], writes=["xT4"])
        wk = [("wing", k) for k in range(8)]
        for m in range(8):
            for br, (src, nk_, wp, key, coff) in enumerate(((yss, 4, wsp, "yss4", 0), (yat, 2, wap, "yat4", 1024))):
                bP, bG = bank[2 * br], bank[2 * br + 1]
                kP, kG = ("bank", 2 * br), ("bank", 2 * br + 1)

                def mmp(e, m=m, src=src, nk_=nk_, wp=wp, bP=bP):
                    last = None
                    for k in range(nk_):
                        last = e.matmul(bP[:, :n], wp[:, k, m * 128:(m + 1) * 128], src[:, k, :n], start=(k == 0), stop=(k == nk_ - 1))
                    return last

                def mmg(e, m=m, coff=coff, bG=bG):
                    last = None
                    for k in range(8):
                        last = e.matmul(bG[:, :n], wing[:, k, coff + m * 128:coff + (m + 1) * 128], hT[:, k, :n], start=(k == 0), stop=(k == 7))
                    return last
                P.op("tensor", mmp, reads=[key, "wsp", "wap"], writes=[kP])
                P.op("tensor", mmg, reads=["hT4"] + wk, writes=[kG])
                P.op("scalar", lambda e, br=br, bG=bG: e.activation(sg[br][:, :n], bG[:, :n], AF.Sigmoid), reads=[kG], writes=[("sg", br)])
                V(lambda e, br=br, bP=bP: e.tensor_tensor(tm[br][:, :n], sg[br][:, :n], bP[:, :n], ALU.mult), [("sg", br), kP], [("tm", br)])
            P.op("gpsimd", lambda e, m=m: e.tensor_tensor(mixed[:, m, :n], tm[0][:, :n], tm[1][:, :n], ALU.add),
                 reads=[("tm", 0), ("tm", 1)], writes=[("mixed", m)])
        for m in range(8):
            bo = bank[4 + m % 2]

            def mmo(e, m=m, bo=bo):
                last = None
                for k in range(8):
                    last = e.matmul(bo[:, :n], wo[:, k, m * 128:(m + 1) * 128], mixed[:, k, :n], start=(k == 0), stop=(k == 7))
                return last
            P.op("tensor", mmo, reads=[("mixed", k) for k in range(8)] + ["wo"], writes=[("bank", 4 + m % 2)])
            V(lambda e, m=m, bo=bo: e.tensor_tensor(xT[:, m, :n], xT[:, m, :n], bo[:, :n], ALU.add), [("bank", 4 + m % 2), "xT4"], ["xT4"])
        P.dma("sync", fm(self.x1T), xT[:, :, :n], reads=["xT4"], writes=[("x1T", ti)])

    def load_ffn_weights(self, tag, g, wgate, wup, wdown):
        P = self.P
        P.dma("sync", self.gcol[:], g.rearrange("(k p) -> p k", p=128), writes=["gcol"], allow_slow_non_contiguous=True)
        for k in range(8):
            P.dma("gpsimd", self.wg[:, k, :], wgate[k * 128:(k + 1) * 128, :], writes=[("wg", k)], max_dma_last_dim=4096)
            P.dma("gpsimd", self.wu[:, k, :], wup[k * 128:(k + 1) * 128, :], writes=[("wu", k)], max_dma_last_dim=4096)
        for f in range(NF):
            P.dma("gpsimd", self.wd[:, f, :], wdown[f * 128:(f + 1) * 128, :], writes=[("wd", f)], max_dma_last_dim=4096)

    def ffn_phase(self, tag, g, wgate, wup, wdown, srcs, src_tok, src_T, dst_T, dst_tok, only=None, dst_own=False):
        P, nc = self.P, self.nc
        self.load_ffn_weights(tag, g, wgate, wup, wdown)
        xT, sq, hT, rstd, aT = self.xT, self.sq, self.hT, self.rstd, self.aT
        bank = self.bank
        for ti, (kind, t0, n) in enumerate(srcs):
            if only is not None and ti not in only:
                continue
            self._ffn_tile(ti, kind, t0, n, src_tok, src_T, dst_T, dst_tok, dst_own)

    def _ffn_tile(self, ti, kind, t0, n, src_tok, src_T, dst_T, dst_tok, dst_own=False):
        P, nc = self.P, self.nc
        xT, sq, hT, rstd, aT = self.xT, self.sq, self.hT, self.rstd, self.aT
        bank = self.bank
        if True:
            gt0 = t0 if kind == "p" else SEQ
            nsub = (n + 127) // 128
            if src_tok is not None:
                src = src_tok[0] if kind == "p" else src_tok[1]
                for s in range(nsub):
                    r = min(128, n - s * 128)
                    xin = self.xin[s % 2]
                    P.dma("sync", xin[:r, :], src[t0 + s * 128: t0 + s * 128 + r, :], writes=[("xin", s % 2)])
                    for k in range(8):
                        b = bank[k]
                        P.op("tensor", lambda e, b=b, xin=xin, k=k, s=s, r=r: e.transpose(
                            b[:, s * 128: s * 128 + r], xin[:r, k * 128:(k + 1) * 128], self.identF[:r, :r]),
                            reads=[("xin", s % 2), "identF"], writes=[("bank", k)])
                for k in range(8):
                    P.op("scalar" if k % 2 else "vector",
                         (lambda e, k=k: e.copy(xT[:, k, :n], bank[k][:, :n])) if k % 2 else
                         (lambda e, k=k: e.tensor_copy(xT[:, k, :n], bank[k][:, :n])),
                         reads=[("bank", k)], writes=[("xT", k)])
            else:
                P.dma("sync", xT[:, :, :n], src_T[:, :, gt0:gt0 + n].rearrange("k p n -> p k n"),
                      writes=[("xT", k) for k in range(8)])
            for k in range(8):
                P.op("scalar", lambda e, k=k: e.activation(sq[:, k, :n], xT[:, k, :n], AF.Square),
                     reads=[("xT", k)], writes=[("sq", k)])

            def nrm(e):
                last = None
                for k in range(8):
                    last = e.matmul(bank[6][:, :n], self.onesB[:], sq[:, k, :n], start=(k == 0), stop=(k == 7))
                return last
            P.op("tensor", nrm, reads=[("sq", k) for k in range(8)] + ["onesB"], writes=[("bank", 6)])
            P.op("scalar", lambda e: e.activation(rstd[:, :n], bank[6][:, :n], AF.Sqrt, bias=self.epsC[:], scale=1.0 / D),
                 reads=[("bank", 6), "epsC"], writes=["rstd"])
            P.op("vector", lambda e: e.reciprocal(rstd[:, :n], rstd[:, :n]), reads=["rstd"], writes=["rstd"])
            for k in range(8):
                P.op("vector", lambda e, k=k: e.scalar_tensor_tensor(
                    hT[:, k, :n], xT[:, k, :n], self.gcol[:, k:k + 1], rstd[:, :n], ALU.mult, ALU.mult),
                    reads=[("xT", k), "gcol", "rstd"], writes=[("hT", k)])
            for f in range(NF):
                bg = bank[f % 2]
                bu = bank[2 + f % 2]
                sil = self.sil[f % 2]

                def mmg(e, f=f, bg=bg):
                    last = None
                    for k in range(8):
                        last = e.matmul(bg[:, :n], self.wg[:, k, f * 128:(f + 1) * 128], hT[:, k, :n], start=(k == 0), stop=(k == 7))
                    return last

                def mmu(e, f=f, bu=bu):
                    last = None
                    for k in range(8):
                        last = e.matmul(bu[:, :n], self.wu[:, k, f * 128:(f + 1) * 128], hT[:, k, :n], start=(k == 0), stop=(k == 7))
                    return last
                hk = [("hT", k) for k in range(8)]
                P.op("tensor", mmg, reads=hk + [("wg", k) for k in range(8)], writes=[("bank", f % 2)])
                P.op("tensor", mmu, reads=hk + [("wu", k) for k in range(8)], writes=[("bank", 2 + f % 2)])
                P.op("scalar", lambda e, bg=bg, sil=sil: e.activation(sil[:, :n], bg[:, :n], AF.Silu),
                     reads=[("bank", f % 2)], writes=[("sil", f % 2)])
                P.op("vector", lambda e, f=f, bu=bu, sil=sil: e.tensor_tensor(aT[:, f, :n], sil[:, :n], bu[:, :n], ALU.mult),
                     reads=[("sil", f % 2), ("bank", 2 + f % 2)], writes=[("aT", f)])
            for m in range(8):
                bd = bank[4 + m % 2]

                def mmd(e, m=m, bd=bd):
                    last = None
                    for f in range(NF):
                        last = e.matmul(bd[:, :n], self.wd[:, f, m * 128:(m + 1) * 128], aT[:, f, :n], start=(f == 0), stop=(f == NF - 1))
                    return last
                P.op("tensor", mmd, reads=[("aT", f) for f in range(NF)] + [("wd", f) for f in range(NF)], writes=[("bank", 4 + m % 2)])
                P.op("vector", lambda e, m=m, bd=bd: e.scalar_tensor_tensor(
                    xT[:, m, :n], bd[:, :n], 0.5, xT[:, m, :n], ALU.mult, ALU.add),
                    reads=[("bank", 4 + m % 2), ("xT", m)], writes=[("xT", m)])
            if dst_T is not None:
                dc = gt0 if not dst_own else ((t0 - self.own0) if kind == "p" else QTR)
                P.dma("sync", dst_T[:, :, dc:dc + n].rearrange("k p n -> p k n"), xT[:, :, :n],
                      reads=[("xT", k) for k in range(8)], writes=[("x1T", ti) if not dst_own else ("yTout", ti)])
            if dst_tok is not None:
                dst = dst_tok[0] if kind == "p" else dst_tok[1]
                for s in range(nsub):
                    r = min(128, n - s * 128)
                    xo = self.xin[s % 2]
                    for k in range(8):
                        bk = bank[6 + (k // 4)]
                        P.op("tensor", lambda e, bk=bk, k=k, s=s, r=r: e.transpose(
                            bk[:r, (k % 4) * 128:(k % 4 + 1) * 128], xT[:, k, s * 128: s * 128 + r], self.identF[:, :]),
                            reads=[("xT", k), "identF"], writes=[("bank", 6 + k // 4)])
                    P.op("vector", lambda e, xo=xo, r=r: e.tensor_copy(xo[:r, 0:512], bank[6][:r, :]),
                         reads=[("bank", 6)], writes=[("xin", s % 2)])
                    P.op("scalar", lambda e, xo=xo, r=r: e.copy(xo[:r, 512:1024], bank[7][:r, :]),
                         reads=[("bank", 7)], writes=[("xin", s % 2)])
                    o0 = (t0 - self.own0) if kind == "p" else t0
                    P.dma("sync", dst[o0 + s * 128: o0 + s * 128 + r, :], xo[:r, :], reads=[("xin", s % 2)],
                          writes=[("yout", kind, t0, s)])


_CACHE = {}
WINS = (128, 512, 2048)


def make_in_maps(inp, seq=None):
    seq = SEQ if seq is None else seq
    ident = np.eye(128, dtype=np.float32)
    caches = [inp["cache_kv_w%d" % W][0].reshape(32, W, 512) for W in WINS]
    tvals = np.ascontiguousarray(np.tile(np.arange(1, ST + 1, dtype=np.float32)[None, :], (128, 1)))
    kk, qq = np.meshgrid(np.arange(128), np.arange(128), indexing="ij")
    mask_cur = np.ascontiguousarray(np.tile((kk <= qq).astype(np.float32), (1, 2)))
    mask_prev = np.ascontiguousarray(np.tile((kk >= qq).astype(np.float32), (1, 2)))
    ssm_mats = {nm: np.zeros((16, 128, 128), np.float32) for nm in ("Bm_re", "Bm_im", "Cm_re", "Cm_im")}
    for s_ in range(16):
        for gi in range(2):
            g = 2 * s_ + gi
            r0 = 32 * (s_ % 4) + 16 * gi
            for nm, src in (("Bm_re", "ssm_b_re"), ("Bm_im", "ssm_b_im")):
                ssm_mats[nm][s_, r0:r0 + 16, 64 * gi:64 * gi + 64] = inp[src][0][g].T
            for nm, src in (("Cm_re", "ssm_c_re"), ("Cm_im", "ssm_c_im")):
                ssm_mats[nm][s_, 64 * gi:64 * gi + 64, r0:r0 + 16] = inp[src][0][g].T
    in_maps = []
    for c in range(NCORES):
        m = {}
        b_, q_ = c // 4, c % 4
        xw = np.zeros((seq, D), np.float32)
        nreal = QTR * (q_ + 1)
        xw[seq - nreal:] = inp["x_prompt"][b_][:nreal]
        xs_ = inp["x_sample"][c * NS:(c + 1) * NS, 0, :]
        m["xTin"] = np.ascontiguousarray(np.concatenate([xw, xs_], axis=0).T).reshape(8, 128, seq + NS)
        m["mask_halo"] = mask_prev if q_ > 0 else np.zeros_like(mask_prev)
        for nm in ("g_ffn1", "w1_gate", "w1_up", "w1_down", "g_ffn2", "w2_gate", "w2_up", "w2_down", "g_mix", "w_in"):
            m[nm] = np.ascontiguousarray(inp[nm][0])
        m["g_q"] = np.ascontiguousarray(inp["g_q"][0].reshape(1, 192))
        m["g_k"] = np.ascontiguousarray(inp["g_k"][0].reshape(1, 192))
        m["ident"] = ident
        m["ssm_a_re"] = np.ascontiguousarray(inp["ssm_a_re"][0].reshape(16, 128))
        m["ssm_a_im"] = np.ascontiguousarray(inp["ssm_a_im"][0].reshape(16, 128))
        m["ssm_ldt"] = np.ascontiguousarray(np.repeat(inp["ssm_log_dt"][0].reshape(16, 2), 64, axis=1))
        for nm in ("Bm_re", "Bm_im", "Cm_re", "Cm_im"):
            m[nm] = ssm_mats[nm]
        m["ssm_d"] = np.ascontiguousarray(inp["ssm_d"][0])
        m["w_glu"] = np.ascontiguousarray(inp["w_glu"][0])
        m["b_glu"] = np.ascontiguousarray(inp["b_glu"][0])
        m["tvals"] = tvals
        m["mask_cur"] = mask_cur
        m["mask_prev"] = mask_prev
        for nm in ("w_ssm_proj", "w_attn_proj", "w_o"):
            m[nm] = np.ascontiguousarray(inp[nm][0])
        m["st_re"] = np.ascontiguousarray(inp["state_ssm_re"][0][c * NS:(c + 1) * NS].reshape(NS, 2048))
        m["st_im"] = np.ascontiguousarray(inp["state_ssm_im"][0][c * NS:(c + 1) * NS].reshape(NS, 2048))
        for g, W in enumerate(WINS):
            m["c%d" % W] = np.ascontiguousarray(caches[g][c * NS:(c + 1) * NS])
        in_maps.append(m)
    return in_maps


def kernel(**inputs):
    inp = {k: np.asarray(v) for k, v in inputs.items()}
    kb = _CACHE.get("k")
    if kb is None:
        kb = K()
        kb.build()
        _CACHE["k"] = kb
    in_maps = make_in_maps(inp)
    res = run_bass_kernel_spmd(kb.nc, in_maps, core_ids=list(range(NCORES))).results
    yT = [res[c]["yTout"].reshape(D, QTR + NS) for c in range(NCORES)]
    y_p = np.stack([np.concatenate([np.ascontiguousarray(yT[4 * b + q][:, :QTR].T) for q in range(4)]) for b in range(2)])
    y_s = np.concatenate([np.ascontiguousarray(yT[c][:, QTR:].T) for c in range(NCORES)])[:, None, :]
    outs = [y_p, y_s]
    for W in WINS:
        outs.append(np.stack([res[4 * b + 3]["kvp%d" % W] for b in range(2)]).reshape(1, 2, W, 2, 4, 64))
    outs.append(np.stack([res[4 * b + 3]["stp_re"] for b in range(2)]).reshape(1, 2, 32, 64))
    outs.append(np.stack([res[4 * b + 3]["stp_im"] for b in range(2)]).reshape(1, 2, 32, 64))
    for W in WINS:
        outs.append(np.concatenate([res[c]["kvs%d" % W] for c in range(NCORES)]).reshape(1, 32, W, 2, 4, 64))
    outs.append(np.concatenate([res[c]["sts_re"] for c in range(NCORES)]).reshape(1, 32, 32, 64))
    outs.append(np.concatenate([res[c]["sts_im"] for c in range(NCORES)]).reshape(1, 32, 32, 64))
    return tuple(outs)
```

```python
import numpy as np
from contextlib import ExitStack
import concourse.bass as bass
import concourse.mybir as mybir
from concourse.bass_utils import run_bass_kernel_spmd

F32 = mybir.dt.float32
BF16 = mybir.dt.bfloat16
ALU = mybir.AluOpType
AF = mybir.ActivationFunctionType
AX = mybir.AxisListType

NCORES = 8
D = 1024
DFF = 2816
NF = DFF // 128
SEQ = 8192
NS = 4
ST = 512
QTR = 2048
EPS = 1e-6


class Prog:
    ENG = ["tensor", "vector", "scalar", "gpsimd", "sync"]

    def __init__(self, nc, stack):
        self.nc = nc
        self.stack = stack
        self.q = {e: [] for e in self.ENG}
        self.cnt = {e: 0 for e in self.ENG}
        self.esem = {e: stack.enter_context(nc.semaphore("es_" + e)) for e in self.ENG if e != "sync"}
        self.lastw = {}
        self.readers = {}
        self.waited = {e: {} for e in self.ENG}
        self.dpool = {}
        self.dpi = {}
        for e, n in (("sync", 12), ("gpsimd", 6), ("scalar", 4)):
            self.dpool[e] = [[stack.enter_context(nc.semaphore("ds_%s%d" % (e, i))), 0] for i in range(n)]
            self.dpi[e] = 0
        self.nops = 0

    def _need(self, eng, tk):
        sem, val = tk
        if val <= 0:
            return
        w = self.waited[eng]
        if w.get(id(sem), 0) >= val:
            return
        w[id(sem)] = val
        self.q[eng].append(lambda e, sem=sem, val=val: e.wait_ge(sem, val))

    def _deps(self, eng, reads, writes):
        for k in reads:
            t = self.lastw.get(k)
            if t is not None:
                self._need(eng, t)
        for k in writes:
            t = self.lastw.get(k)
            if t is not None:
                self._need(eng, t)
            for t in self.readers.get(k, ()):
                self._need(eng, t)

    def _record(self, tk, reads, writes):
        for k in reads:
            self.readers.setdefault(k, []).append(tk)
        for k in writes:
            self.lastw[k] = tk
            self.readers[k] = []

    def op(self, eng, fn, reads=(), writes=()):
        self._deps(eng, reads, writes)
        self.cnt[eng] += 1
        v = self.cnt[eng]
        sem = self.esem[eng]
        self.q[eng].append(lambda e, fn=fn, sem=sem: fn(e).then_inc(sem, 1))
        tk = (sem, v)
        if eng == "tensor":
            self.waited[eng][id(sem)] = v
        self._record(tk, reads, writes)
        self.nops += 1
        return tk

    def dma(self, eng, out, in_, reads=(), writes=(), **kw):
        pool = self.dpool[eng]
        i = self.dpi[eng]
        self.dpi[eng] = (i + 1) % len(pool)
        sem, cur = pool[i]
        self._deps(eng, reads, writes)
        self._need(eng, (sem, cur))
        pool[i][1] = cur + 16
        self.q[eng].append(lambda e, out=out, in_=in_, sem=sem, kw=kw: e.dma_start(out=out, in_=in_, **kw).then_inc(sem, 16))
        tk = (sem, cur + 16)
        self._record(tk, reads, writes)
        self.nops += 1
        return tk

    def barrier(self):
        for e in self.ENG:
            for pe in self.dpool:
                for sem, cur in self.dpool[pe]:
                    self._need(e, (sem, cur))
            for ce in self.esem:
                if ce != e:
                    self._need(e, (self.esem[ce], self.cnt[ce]))

    def flush(self):
        nc = self.nc
        q = self.q
        self.q = {e: [] for e in self.ENG}
        with nc.Block() as block:
            @block.tensor
            def _(e):
                for f in q["tensor"]:
                    f(e)

            @block.vector
            def _(e):
                for f in q["vector"]:
                    f(e)

            @block.scalar
            def _(e):
                for f in q["scalar"]:
                    f(e)

            @block.gpsimd
            def _(e):
                for f in q["gpsimd"]:
                    f(e)

            @block.sync
            def _(e):
                for f in q["sync"]:
                    f(e)

    def finish(self):
        self.barrier()
        self.flush()


def tiles_of(total):
    return [(t0, min(ST, total - t0)) for t0 in range(0, total, ST)]


class K:
    def __init__(self, debug=()):
        self.debug = set(debug)
        self.nc = bass.Bass("TRN2", target_bir_lowering=False)
        self.stack = ExitStack()
        self.P = Prog(self.nc, self.stack)
        self.ins = {}
        self.outs = {}
        self._uid = 0

    def din(self, name, shape, dt=F32):
        t = self.nc.dram_tensor(name, list(shape), dt, kind="ExternalInput").ap()
        self.ins[name] = t
        return t

    def dout(self, name, shape, dt=F32):
        t = self.nc.dram_tensor(name, list(shape), dt, kind="ExternalOutput").ap()
        self.outs[name] = t
        return t

    def dscr(self, name, shape, dt=F32):
        kind = "ExternalOutput" if name in self.debug else "Internal"
        t = self.nc.dram_tensor(name, list(shape), dt, kind=kind).ap()
        if name in self.debug:
            self.outs[name] = t
        return t

    def sb(self, name, shape, dt=F32):
        return self.stack.enter_context(self.nc.sbuf_tensor(name, list(shape), dt))

    def ps(self, name, shape, dt=F32):
        return self.stack.enter_context(self.nc.psum_tensor(name, list(shape), dt))

    def sbp(self, name, shape, dt=F32):
        self._uid += 1
        return self.ph.enter_context(self.nc.sbuf_tensor("%s_%d" % (name, self._uid), list(shape), dt))

    def build(self):
        nc, P = self.nc, self.P
        NT = SEQ + NS
        self.NT = NT
        xp = self.din("xp", [SEQ, D])
        xs = self.din("xs", [NS, D])
        w = {}
        for nm, shp in (("g_ffn1", [D]), ("w1_gate", [D, DFF]), ("w1_up", [D, DFF]), ("w1_down", [DFF, D]),
                        ("g_ffn2", [D]), ("w2_gate", [D, DFF]), ("w2_up", [D, DFF]), ("w2_down", [DFF, D]),
                        ("g_mix", [D]), ("w_in", [D, 4864]), ("g_q", [1, 192]), ("g_k", [1, 192]),
                        ("ssm_a_re", [16, 128]), ("ssm_a_im", [16, 128]), ("ssm_ldt", [16, 128]),
                        ("Bm_re", [16, 128, 128]), ("Bm_im", [16, 128, 128]), ("BmT_re", [16, 128, 128]), ("BmT_im", [16, 128, 128]), ("Cm_re", [16, 128, 128]), ("Cm_im", [16, 128, 128]),
                        ("ssm_d", [512]), ("w_glu", [512, 512]), ("b_glu", [512]),
                        ("mask_cur", [128, 256]), ("mask_prev", [128, 256]), ("mask_halo", [128, 256]), ("tvals", [128, ST]),
                        ("w_ssm_proj", [512, D]), ("w_attn_proj", [256, D]), ("w_o", [D, D])):
            w[nm] = self.din(nm, shp)
        self.w = w
        ident = self.din("ident", [128, 128])
        w["ident"] = ident
        self.cache = [self.din("c%d" % W, [NS, W, 512]) for W in (128, 512, 2048)]
        yp = self.dout("yp", [QTR, D])
        ys = self.dout("ys", [NS, D])
        self.kvp = [self.dout("kvp%d" % W, [W, 512]) for W in (128, 512, 2048)]
        self.kvs = [self.dout("kvs%d" % W, [NS, W, 512]) for W in (128, 512, 2048)]
        self.st_in = [self.din("st_re", [NS, 2048]), self.din("st_im", [NS, 2048])]
        self.st_out_p = [self.dout("stp_re", [16, 128]), self.dout("stp_im", [16, 128])]
        self.st_out_s = [self.dout("sts_re", [NS, 2048]), self.dout("sts_im", [NS, 2048])]
        self.yssT_s = self.dscr("yssT_s", [4, 128, NT], BF16)
        self.yatT_s = self.dscr("yatT_s", [2, 128, NT], BF16)
        self.utok_s = self.dscr("utok_s", [SEQ, 512], BF16)
        self.car = [self.sb("car_re", [128, 16], F32), self.sb("car_im", [128, 16], F32)]
        x1T = self.dscr("x1T", [8, 128, NT])
        self.x1T = x1T
        self.hT_s = self.dscr("hT_s", [8, 128, NT], BF16)
        self.uT_s = self.dscr("uT_s", [4, 128, NT], BF16)
        self.kv_s = [self.dscr("kv_s%d" % g, [NT, 512]) for g in range(3)]
        self.q_s = [self.dscr("q_s%d" % g, [NT, 256]) for g in range(3)]

        self.identF = self.sb("identF", [128, 128], F32)
        P.dma("sync", self.identF[:], ident[:, :], writes=["identF"])
        self.onesB = self.sb("onesB", [128, 128], BF16)
        P.op("vector", lambda e: e.memset(self.onesB[:], 1.0), writes=["onesB"])
        self.epsC = self.sb("epsC", [128, 1], F32)
        P.op("vector", lambda e: e.memset(self.epsC[:], EPS), writes=["epsC"])
        self.bank = [self.ps("bank%d" % i, [128, 512], F32) for i in range(8)]
        self.srcs = [("p", t0, n) for (t0, n) in tiles_of(SEQ)] + [("s", 0, NS)]
        self.own0 = SEQ - QTR
        self.own_tis = [ti for ti, (kind, t0, n) in enumerate(self.srcs) if kind == "s" or t0 >= self.own0]

        for g, W in enumerate((128, 512, 2048)):
            for b in range(NS):
                P.dma("sync", self.kvs[g][b, 0:W - 1, :], self.cache[g][b, 1:W, :], writes=[("kvs", g, b)])

        with ExitStack() as ph:
            self.ph = ph
            self.alloc_ffn()
            self.ffn_phase(1, w["g_ffn1"], w["w1_gate"], w["w1_up"], w["w1_down"], self.srcs,
                           src_tok=(xp, xs), src_T=None, dst_T=x1T, dst_tok=None)
            P.barrier()
            P.flush()
        with ExitStack() as ph:
            self.ph = ph
            self.phase_b1()
            P.barrier()
            P.flush()
        with ExitStack() as ph:
            self.ph = ph
            self.phase_b2p()
            P.barrier()
            P.flush()
        with ExitStack() as ph:
            self.ph = ph
            self.phase_b2()
            P.barrier()
            P.flush()
        with ExitStack() as ph:
            self.ph = ph
            self.phase_b3()
            P.barrier()
            P.flush()
        with ExitStack() as ph:
            self.ph = ph
            self.phase_b4()
            P.barrier()
            P.flush()
        with ExitStack() as ph:
            self.ph = ph
            self.alloc_ffn()
            self.ffn_phase(2, w["g_ffn2"], w["w2_gate"], w["w2_up"], w["w2_down"], self.srcs,
                           src_tok=None, src_T=x1T, dst_T=None, dst_tok=(yp, ys), only=self.own_tis)
            P.finish()
        return nc

    def alloc_ffn(self):
        self.wg = self.sbp("wg", [128, 8, DFF], BF16)
        self.wu = self.sbp("wu", [128, 8, DFF], BF16)
        self.wd = self.sbp("wd", [128, NF, D], BF16)
        self.gcol = self.sbp("gcol", [128, 8], F32)
        self.xin = [self.sbp("xin%d" % i, [128, D], F32) for i in range(2)]
        self.xT = self.sbp("xT", [128, 8, ST], F32)
        self.sq = self.sbp("sq", [128, 8, ST], BF16)
        self.hT = self.sbp("hT", [128, 8, ST], BF16)
        self.rstd = self.sbp("rstd", [128, ST], F32)
        self.aT = self.sbp("aT", [128, NF, ST], BF16)
        self.sil = [self.sbp("sil%d" % i, [128, ST], F32) for i in range(2)]

    def rmsnorm(self, n, nb=6):
        P = self.P
        xT, sq, hT, rstd, bank = self.xT, self.sq, self.hT, self.rstd, self.bank
        for k in range(8):
            P.op("scalar", lambda e, k=k: e.activation(sq[:, k, :n], xT[:, k, :n], AF.Square),
                 reads=[("xT", k)], writes=[("sq", k)])

        def nrm(e):
            last = None
            for k in range(8):
                last = e.matmul(bank[nb][:, :n], self.onesB[:], sq[:, k, :n], start=(k == 0), stop=(k == 7))
            return last
        P.op("tensor", nrm, reads=[("sq", k) for k in range(8)] + ["onesB"], writes=[("bank", nb)])
        P.op("scalar", lambda e: e.activation(rstd[:, :n], bank[nb][:, :n], AF.Sqrt, bias=self.epsC[:], scale=1.0 / D),
             reads=[("bank", nb), "epsC"], writes=["rstd"])
        P.op("vector", lambda e: e.reciprocal(rstd[:, :n], rstd[:, :n]), reads=["rstd"], writes=["rstd"])
        for k in range(8):
            P.op("vector", lambda e, k=k: e.scalar_tensor_tensor(
                hT[:, k, :n], xT[:, k, :n], self.gcol[:, k:k + 1], rstd[:, :n], ALU.mult, ALU.mult),
                reads=[("xT", k), "gcol", "rstd"], writes=[("hT", k)])

    def phase_b1(self):
        P, nc, w = self.P, self.nc, self.w
        NW = 2816
        self.winA = self.sbp("winA", [128, 8, NW], BF16)
        for k in range(8):
            P.dma("gpsimd", self.winA[:, k, :], w["w_in"][k * 128:(k + 1) * 128, 0:NW], writes=[("winA", k)], max_dma_last_dim=4096)
        self.gcol = self.sbp("gcol", [128, 8], F32)
        P.dma("sync", self.gcol[:], w["g_mix"].rearrange("(k p) -> p k", p=128), writes=["gcol"], allow_slow_non_contiguous=True)
        self.gqb = self.sbp("gqb", [128, 192], F32)
        self.gkb = self.sbp("gkb", [128, 192], F32)
        P.dma("sync", self.gqb[:], w["g_q"].partition_broadcast(128), writes=["gqb"])
        P.dma("sync", self.gkb[:], w["g_k"].partition_broadcast(128), writes=["gkb"])
        P.op("vector", lambda e: e.tensor_scalar(self.gqb[:], self.gqb[:], 0.125, None, ALU.mult), reads=["gqb"], writes=["gqb"])
        self.xT = self.sbp("xT", [128, 8, ST], F32)
        self.sq = self.sbp("sq", [128, 8, ST], BF16)
        self.hT = self.sbp("hT", [128, 8, ST], BF16)
        self.rstd = self.sbp("rstd", [128, ST], F32)
        self.uTt = self.sbp("uTt", [128, 4, ST], BF16)
        self.utkb = [self.sbp("utkb%d" % i, [128, 512], BF16) for i in range(2)]
        self.sqq = [self.sbp("sqq%d" % i, [128, 512], F32) for i in range(3)]
        self.ssum = [self.sbp("ssum%d" % i, [128, 8], F32) for i in range(3)]
        self.kvt = [self.sbp("kvt%d" % i, [128, 512], F32) for i in range(3)]
        self.qt = [self.sbp("qt%d" % i, [128, 256], F32) for i in range(3)]
        self.unit = 0
        for ti, (kind, t0, n) in enumerate(self.srcs):
            self._b1_tile(ti, kind, t0, n)
        allkv = lambda g: [("kv_s", g, ti) for ti in range(len(self.srcs))]
        for g, W in enumerate((128, 512, 2048)):
            P.dma("sync", self.kvp[g][:, :], self.kv_s[g][SEQ - W:SEQ, :], reads=allkv(g), writes=[("kvp", g)])
            for b in range(NS):
                P.dma("sync", self.kvs[g][b, W - 1:W, :], self.kv_s[g][SEQ + b:SEQ + b + 1, :], reads=allkv(g), writes=[("kvs", g, b)])

    def _b1_tile(self, ti, kind, t0, n):
        P = self.P
        xT, hT, bank, winA = self.xT, self.hT, self.bank, self.winA
        gt0 = t0 if kind == "p" else SEQ
        nsub = (n + 127) // 128
        P.dma("sync", xT[:, :, :n], self.x1T[:, :, gt0:gt0 + n].rearrange("k p n -> p k n"),
              reads=[("x1T", ti)], writes=[("xT", k) for k in range(8)])
        self.rmsnorm(n, nb=0)
        hk = [("hT", k) for k in range(8)]
        wk = [("winA", k) for k in range(8)]
        need_h = ti in self.own_tis
        need_kv = kind == "s" or t0 >= self.own0 - QTR
        if need_h:
            P.dma("sync", self.hT_s[:, :, gt0:gt0 + n].rearrange("k p n -> p k n"), hT[:, :, :n], reads=hk, writes=[("hT_s", ti)])
        if not need_h:
            for s in range(nsub):
                bu = bank[s % 2]

                def mmut(e, s=s, bu=bu):
                    last = None
                    for k in range(8):
                        last = e.matmul(bu[:, 0:512], hT[:, k, s * 128:(s + 1) * 128], winA[:, k, 0:512], start=(k == 0), stop=(k == 7))
                    return last
                P.op("tensor", mmut, reads=hk + wk, writes=[("bank", s % 2)])
                ub = self.utkb[s % 2]
                P.op("scalar", lambda e, bu=bu, ub=ub: e.copy(ub[:], bu[:, 0:512]), reads=[("bank", s % 2)], writes=[("utkb", s % 2)])
                P.dma("sync", self.utok_s[t0 + s * 128:t0 + (s + 1) * 128, :], ub[:], reads=[("utkb", s % 2)], writes=[("utok_s", ti)])
        for m in (range(4) if need_h else ()):
            bu = bank[m % 2]

            def mmu(e, m=m, bu=bu):
                last = None
                for k in range(8):
                    last = e.matmul(bu[:, :n], winA[:, k, m * 128:(m + 1) * 128], hT[:, k, :n], start=(k == 0), stop=(k == 7))
                return last
            P.op("tensor", mmu, reads=hk + wk, writes=[("bank", m % 2)])
            P.op("scalar", lambda e, m=m, bu=bu: e.copy(self.uTt[:, m, :n], bu[:, :n]), reads=[("bank", m % 2)], writes=[("uTt", m)])
        if need_h:
            P.dma("sync", self.uT_s[:, :, gt0:gt0 + n].rearrange("k p n -> p k n"), self.uTt[:, :, :n],
                  reads=[("uTt", m) for m in range(4)], writes=[("uT_s", ti)])
        if not need_kv:
            return
        for s in range(nsub):
            r = min(128, n - s * 128)
            for g in range(3):
                self._b1_qkv(ti, gt0 + s * 128, s, r, g)

    def _b1_qkv(self, ti, row0, s, r, g):
        P = self.P
        hT, bank, winA = self.hT, self.bank, self.winA
        u = self.unit
        self.unit += 1
        bA = bank[2 + (u % 3)]
        bB = bank[5 + (u % 3)]
        kA, kB = ("bank", 2 + u % 3), ("bank", 5 + u % 3)
        sqq, ssum, kvt, qt = self.sqq[u % 3], self.ssum[u % 3], self.kvt[u % 3], self.qt[u % 3]
        ks = lambda nm: (nm, u % 3)
        hk = [("hT", k) for k in range(8)]
        wk = [("winA", k) for k in range(8)]

        def mmkv(e):
            last = None
            for part, c0 in ((0, 1280 + 256 * g), (1, 2048 + 256 * g)):
                for k in range(8):
                    last = e.matmul(bA[:r, part * 256:(part + 1) * 256], hT[:, k, s * 128:s * 128 + r], winA[:, k, c0:c0 + 256],
                                    start=(k == 0), stop=(k == 7))
            return last

        def mmq(e):
            last = None
            c0 = 512 + 256 * g
            for k in range(8):
                last = e.matmul(bB[:r, 0:256], hT[:, k, s * 128:s * 128 + r], winA[:, k, c0:c0 + 256], start=(k == 0), stop=(k == 7))
            return last
        P.op("tensor", mmkv, reads=hk + wk, writes=[kA])
        P.op("tensor", mmq, reads=hk + wk, writes=[kB])
        P.op("scalar", lambda e: e.activation(sqq[:r, 0:256], bB[:r, 0:256], AF.Square), reads=[kB], writes=[ks("sqq")])
        P.op("scalar", lambda e: e.activation(sqq[:r, 256:512], bA[:r, 0:256], AF.Square), reads=[kA], writes=[ks("sqq")])
        P.op("vector", lambda e: e.tensor_reduce(ssum[:r, :], sqq[:r, :].rearrange("p (h d) -> p h d", d=64), AX.X, ALU.add),
             reads=[ks("sqq")], writes=[ks("ssum")])
        P.op("scalar", lambda e: e.activation(ssum[:r, :], ssum[:r, :], AF.Sqrt, bias=self.epsC[:r, :], scale=1.0 / 64),
             reads=[ks("ssum"), "epsC"], writes=[ks("ssum")])
        P.op("vector", lambda e: e.reciprocal(ssum[:r, :], ssum[:r, :]), reads=[ks("ssum")], writes=[ks("ssum")])
        v3 = lambda ap: ap.rearrange("p (h d) -> p h d", d=64)
        P.op("vector", lambda e: e.tensor_tensor(v3(kvt[:r, 0:256]), v3(bA[:r, 0:256]),
                                                 ssum[:r, 4:8].unsqueeze(2).broadcast_to([r, 4, 64]), ALU.mult),
             reads=[kA, ks("ssum")], writes=[ks("kvt")])
        P.op("vector", lambda e: e.tensor_tensor(v3(kvt[:r, 0:256]), v3(kvt[:r, 0:256]),
                                                 self.gkb[:r, g * 64:(g + 1) * 64].unsqueeze(1).broadcast_to([r, 4, 64]), ALU.mult),
             reads=[ks("kvt"), "gkb"], writes=[ks("kvt")])
        P.op("scalar", lambda e: e.copy(kvt[:r, 256:512], bA[:r, 256:512]), reads=[kA], writes=[ks("kvt")])
        P.op("vector", lambda e: e.tensor_tensor(v3(qt[:r, :]), v3(bB[:r, 0:256]),
                                                 ssum[:r, 0:4].unsqueeze(2).broadcast_to([r, 4, 64]), ALU.mult),
             reads=[kB, ks("ssum")], writes=[ks("qt")])
        P.op("vector", lambda e: e.tensor_tensor(v3(qt[:r, :]), v3(qt[:r, :]),
                                                 self.gqb[:r, g * 64:(g + 1) * 64].unsqueeze(1).broadcast_to([r, 4, 64]), ALU.mult),
             reads=[ks("qt"), "gqb"], writes=[ks("qt")])
        P.dma("sync", self.kv_s[g][row0:row0 + r, :], kvt[:r, :], reads=[ks("kvt")], writes=[("kv_s", g, ti)])
        P.dma("sync", self.q_s[g][row0:row0 + r, :], qt[:r, :], reads=[ks("qt")], writes=[("q_s", g, ti)])

    def _ssm_setup(self):
        P, nc, w = self.P, self.nc, self.w
        TS = 16
        T = {}

        def tl(nm, shape=(128, TS), dt=F32):
            T[nm] = self.sbp(nm, list(shape), dt)
            return T[nm]
        for nm in ("a_re", "a_im", "ldt"):
            tl(nm)
            P.dma("sync", T[nm][:], w["ssm_" + nm].rearrange("s q -> q s"), writes=[nm], allow_slow_non_contiguous=True)
        V = lambda fn, r, wr: P.op("vector", fn, reads=r, writes=wr)
        A = lambda fn, r, wr: P.op("scalar", fn, reads=r, writes=wr)
        tt = lambda o, a, b, op: V(lambda e: e.tensor_tensor(T[o][:], T[a][:], T[b][:], op), [a, b], [o])
        for nm in ("dt", "ar", "ai", "mag", "rs", "rc", "sn", "cs", "abr", "abi", "nabi", "sq1", "sq2", "inv", "em1", "t1", "t2", "f_re", "f_im"):
            tl(nm)
        A(lambda e: e.activation(T["dt"][:], T["ldt"][:], AF.Exp), ["ldt"], ["dt"])
        tt("ar", "a_re", "dt", ALU.mult)
        tt("ai", "a_im", "dt", ALU.mult)
        A(lambda e: e.activation(T["mag"][:], T["ar"][:], AF.Exp), ["ar"], ["mag"])
        PI = float(np.pi)
        tl("rtmp")
        T["rint"] = self.sbp("rint", [128, TS], mybir.dt.int32)
        tl("rmask")

        def reduce_generic(dst_ap, src_ap, off, tmp_ap, int_ap, mask_ap, kd):
            V(lambda e: e.tensor_scalar(dst_ap, src_ap, off, None, ALU.add), [kd, "ai", "mm0"], [kd])
            V(lambda e: e.tensor_scalar(tmp_ap, dst_ap, 1.0 / (2 * PI), None, ALU.mult), [kd], ["mm1"])
            V(lambda e: e.tensor_copy(int_ap, tmp_ap), ["mm1"], ["g_int"])
            V(lambda e: e.tensor_copy(tmp_ap, int_ap), ["g_int"], ["mm1"])
            V(lambda e: e.scalar_tensor_tensor(dst_ap, tmp_ap, -2 * PI, dst_ap, ALU.mult, ALU.add), ["mm1", kd], [kd])
            V(lambda e: e.tensor_scalar(mask_ap, dst_ap, PI, None, ALU.is_gt), [kd], ["mm2"])
            V(lambda e: e.scalar_tensor_tensor(dst_ap, mask_ap, -2 * PI, dst_ap, ALU.mult, ALU.add), ["mm2", kd], [kd])
            V(lambda e: e.tensor_scalar(mask_ap, dst_ap, -PI, None, ALU.is_lt), [kd], ["mm2"])
            V(lambda e: e.scalar_tensor_tensor(dst_ap, mask_ap, 2 * PI, dst_ap, ALU.mult, ALU.add), ["mm2", kd], [kd])
            V(lambda e: e.tensor_scalar(dst_ap, dst_ap, PI, -PI, ALU.min, ALU.max), [kd], [kd])

        def reduce_angle(dst, off):
            reduce_generic(T[dst][:], T["ai"][:], off, T["rtmp"][:], T["rint"][:], T["rmask"][:], dst)
        reduce_angle("rs", 0.0)
        reduce_angle("rc", 0.5 * PI)
        A(lambda e: e.activation(T["sn"][:], T["rs"][:], AF.Sin), ["rs"], ["sn"])
        A(lambda e: e.activation(T["cs"][:], T["rc"][:], AF.Sin), ["rc"], ["cs"])
        tt("abr", "mag", "cs", ALU.mult)
        tt("abi", "mag", "sn", ALU.mult)
        tt("sq1", "a_re", "a_re", ALU.mult)
        tt("sq2", "a_im", "a_im", ALU.mult)
        tt("inv", "sq1", "sq2", ALU.add)
        V(lambda e: e.reciprocal(T["inv"][:], T["inv"][:]), ["inv"], ["inv"])
        V(lambda e: e.tensor_scalar(T["em1"][:], T["abr"][:], -1.0, None, ALU.add), ["abr"], ["em1"])
        tt("t1", "em1", "a_re", ALU.mult)
        tt("t2", "abi", "a_im", ALU.mult)
        tt("t1", "t1", "t2", ALU.add)
        tt("f_re", "t1", "inv", ALU.mult)
        tt("t1", "abi", "a_re", ALU.mult)
        tt("t2", "em1", "a_im", ALU.mult)
        tt("t1", "t1", "t2", ALU.subtract)
        tt("f_im", "t1", "inv", ALU.mult)
        return T, tl, reduce_generic

    def phase_b2p(self):
        P, w, bank = self.P, self.w, self.bank
        TS, n = 16, ST
        T, tl, reduce_generic = self._ssm_setup()
        PI = float(np.pi)
        V = lambda fn, r, wr: P.op("vector", fn, reads=r, writes=wr)
        A = lambda fn, r, wr: P.op("scalar", fn, reads=r, writes=wr)
        G = lambda fn, r, wr: P.op("gpsimd", fn, reads=r, writes=wr)
        car = self.car
        V(lambda e: e.memset(car[0][:], 0.0), [], ["car"])
        V(lambda e: e.memset(car[1][:], 0.0), [], ["car"])
        tv, rev = tl("tvals", (128, ST)), tl("rev", (128, ST))
        P.dma("sync", tv[:], w["tvals"][:, :], writes=["tvals"])
        V(lambda e: e.tensor_scalar(rev[:], tv[:], -1.0, float(ST), ALU.mult, ALU.add), ["tvals"], ["rev"])
        Et = [tl("Et_re", (128, ST)), tl("Et_im", (128, ST))]
        ang, g_tmp, g_msk = tl("angp", (128, ST)), tl("g_tmpp", (128, ST)), tl("g_mskp", (128, ST))
        g_int = self.sbp("g_intp", [128, ST], mybir.dt.int32)
        En = [tl("En_re"), tl("En_im")]
        Kc = [tl("Kc_re"), tl("Kc_im")]
        kt = tl("kt")
        Rt, Yr, Yi, W0, W1, t1 = [tl(nm, (128, ST)) for nm in ("Rt", "Yr", "Yi", "W0", "W1", "wt1")]
        Wt = [tl("Wt_re", (128, 4, TS, 128), BF16), tl("Wt_im", (128, 4, TS, 128), BF16)]
        BmT = [tl("BmT_re", (128, TS, 128), BF16), tl("BmT_im", (128, TS, 128), BF16)]
        for t_, nm in ((BmT[0], "BmT_re"), (BmT[1], "BmT_im")):
            P.dma("gpsimd", t_[:], w[nm].rearrange("s r c -> r s c"), writes=[nm])
        for s_ in range(TS):
            V(lambda e, s_=s_: e.tensor_scalar(ang[:], tv[:], T["ai"][:, s_:s_ + 1], None, ALU.mult), ["tvals", "ai"], ["mm0"])
            for j, off in ((1, 0.0), (0, 0.5 * PI)):
                reduce_generic(Et[j][:], ang[:], off, g_tmp[:], g_int[:], g_msk[:], ("Et", j))
                A(lambda e, j=j: e.activation(Et[j][:], Et[j][:], AF.Sin), [("Et", j)], [("Et", j)])
            for j in range(2):
                V(lambda e, j=j, s_=s_: e.tensor_copy(En[j][:, s_:s_ + 1], Et[j][:, ST - 1:ST]), [("Et", j)], ["En"])
            fr, fi = T["f_re"][:, s_:s_ + 1], T["f_im"][:, s_:s_ + 1]
            er, ei = En[0][:, s_:s_ + 1], En[1][:, s_:s_ + 1]
            kr, ki = Kc[0][:, s_:s_ + 1], Kc[1][:, s_:s_ + 1]
            V(lambda e, s_=s_, fi=fi, ei=ei: e.tensor_tensor(kt[:, 0:1], fi, ei, ALU.mult), ["f_im", "En"], ["kt"])
            V(lambda e, fr=fr, er=er, kr=kr: e.scalar_tensor_tensor(kr, er, fr, kt[:, 0:1], ALU.mult, ALU.subtract), ["f_re", "En", "kt"], ["Kc"])
            V(lambda e, s_=s_, fi=fi, er=er: e.tensor_tensor(kt[:, 0:1], fi, er, ALU.mult), ["f_im", "En"], ["kt"])
            V(lambda e, fr=fr, ei=ei, ki=ki: e.scalar_tensor_tensor(ki, ei, fr, kt[:, 0:1], ALU.mult, ALU.add), ["f_re", "En", "kt"], ["Kc"])
            V(lambda e, s_=s_: e.tensor_scalar(Rt[:], rev[:], T["ar"][:, s_:s_ + 1], None, ALU.mult), ["rev", "ar"], ["Rt"])
            A(lambda e: e.activation(Rt[:], Rt[:], AF.Exp), ["Rt"], ["Rt"])
            V(lambda e: e.tensor_tensor(Yr[:], Rt[:], Et[0][:], ALU.mult), ["Rt", ("Et", 0)], ["Yr"])
            G(lambda e: e.tensor_tensor(Yi[:], Rt[:], Et[1][:], ALU.mult), ["Rt", ("Et", 1)], ["Yi"])
            V(lambda e, ki=ki: e.tensor_scalar(t1[:], Yi[:], ki, None, ALU.mult), ["Yi", "Kc"], ["wt1"])
            V(lambda e, kr=kr: e.scalar_tensor_tensor(W0[:], Yr[:], kr, t1[:], ALU.mult, ALU.add), ["Yr", "Kc", "wt1"], ["W0"])
            V(lambda e, kr=kr: e.tensor_scalar(t1[:], Yi[:], kr, None, ALU.mult), ["Yi", "Kc"], ["wt1"])
            V(lambda e, ki=ki: e.scalar_tensor_tensor(W1[:], Yr[:], ki, t1[:], ALU.mult, ALU.subtract), ["Yr", "Kc", "wt1"], ["W1"])
            for j, (Wsrc, kW) in enumerate(((W0, "W0"), (W1, "W1"))):
                tb = bank[6 + j]
                for tc in range(4):
                    P.op("tensor", lambda e, tb=tb, tc=tc, Wsrc=Wsrc: e.transpose(tb[:, tc * 128:(tc + 1) * 128], Wsrc[:, tc * 128:(tc + 1) * 128], self.identF[:, :]),
                         reads=[kW, "identF"], writes=[("bank", 6 + j)])
                eng = "scalar" if j == 0 else "vector"
                P.op(eng, (lambda e, j=j, s_=s_, tb=tb: e.copy(Wt[j][:, :, s_, :], tb[:, :].rearrange("p (a n) -> p a n", a=4))) if j == 0 else
                     (lambda e, j=j, s_=s_, tb=tb: e.tensor_copy(Wt[j][:, :, s_, :], tb[:, :].rearrange("p (a n) -> p a n", a=4))),
                     reads=[("bank", 6 + j)], writes=["Wt"])
        rho_n, An = tl("rho_n"), [tl("An_re"), tl("An_im")]
        A(lambda e: e.activation(rho_n[:], T["ar"][:], AF.Exp, scale=float(ST)), ["ar"], ["rho_n"])
        for j in range(2):
            V(lambda e, j=j: e.tensor_tensor(An[j][:], rho_n[:], En[j][:], ALU.mult), ["rho_n", "En"], ["An"])
        utk = [tl("utk%d" % i, (128, 4, 512), BF16) for i in range(2)]
        junk = tl("junk", (128, 128))
        Dt = [[tl("D%d%d" % (j, k)) for k in range(2)] for j in range(2)]
        nr, ni, tt_ = tl("nr"), tl("ni"), tl("ttt")
        cnt = 0
        for ti, (kind, t0, n_) in enumerate(self.srcs):
            if kind != "p" or t0 >= self.own0:
                continue
            uk = utk[cnt % 2]
            kuk = ("utk", cnt % 2)
            cnt += 1
            P.dma("sync", uk[:], self.utok_s[t0:t0 + ST, :].rearrange("(a t) c -> t a c", t=128), reads=[("utok_s", ti)], writes=[kuk])
            for s_ in range(TS):
                c = s_ // 4
                bk = bank[s_ % 4]
                kb = ("bank", s_ % 4)

                def mmM(e, s_=s_, c=c, bk=bk, uk=uk):
                    last = None
                    for j in range(2):
                        for tc in range(4):
                            last = e.matmul(bk[:, j * 128:(j + 1) * 128], Wt[j][:, tc, s_, :], uk[:, tc, c * 128:(c + 1) * 128],
                                            start=(tc == 0), stop=(tc == 3))
                    return last
                P.op("tensor", mmM, reads=["Wt", kuk], writes=[kb])
                for j in range(2):
                    for k in range(2):
                        V(lambda e, j=j, k=k, s_=s_, bk=bk: e.scalar_tensor_tensor(junk[:], bk[:, j * 128:(j + 1) * 128], 1.0, BmT[k][:, s_, :],
                                                                                   ALU.mult, ALU.mult, accum_out=Dt[j][k][:, s_:s_ + 1]),
                          [kb, "BmT_re", "BmT_im", "junk"], ["junk", ("D", j, k)])
            dk = [("D", j, k) for j in range(2) for k in range(2)]
            V(lambda e: e.tensor_tensor(nr[:], An[0][:], car[0][:], ALU.mult), ["An", "car"], ["nr"])
            V(lambda e: e.tensor_tensor(tt_[:], An[1][:], car[1][:], ALU.mult), ["An", "car"], ["ttt"])
            V(lambda e: e.tensor_tensor(nr[:], nr[:], tt_[:], ALU.subtract), ["nr", "ttt"], ["nr"])
            V(lambda e: e.tensor_tensor(nr[:], nr[:], Dt[0][0][:], ALU.add), ["nr"] + dk, ["nr"])
            V(lambda e: e.tensor_tensor(nr[:], nr[:], Dt[1][1][:], ALU.subtract), ["nr"] + dk, ["nr"])
            V(lambda e: e.tensor_tensor(ni[:], An[0][:], car[1][:], ALU.mult), ["An", "car"], ["ni"])
            V(lambda e: e.tensor_tensor(tt_[:], An[1][:], car[0][:], ALU.mult), ["An", "car", "nr"], ["ttt"])
            V(lambda e: e.tensor_tensor(ni[:], ni[:], tt_[:], ALU.add), ["ni", "ttt"], ["ni"])
            V(lambda e: e.tensor_tensor(ni[:], ni[:], Dt[0][1][:], ALU.add), ["ni"] + dk, ["ni"])
            V(lambda e: e.tensor_tensor(ni[:], ni[:], Dt[1][0][:], ALU.add), ["ni"] + dk, ["ni"])
            V(lambda e: e.tensor_copy(car[0][:], nr[:]), ["nr", "car"], ["car"])
            V(lambda e: e.tensor_copy(car[1][:], ni[:]), ["ni", "car"], ["car"])

    def phase_b2(self):
        P, nc, w = self.P, self.nc, self.w
        TS = 16
        T, tl, reduce_generic = self._ssm_setup()
        PI = float(np.pi)
        V = lambda fn, r, wr: P.op("vector", fn, reads=r, writes=wr)
        A = lambda fn, r, wr: P.op("scalar", fn, reads=r, writes=wr)
        tv = tl("tvals", (128, ST))
        P.dma("sync", tv[:], w["tvals"][:, :], writes=["tvals"])
        E = [tl("E_re", (128, TS, ST)), tl("E_im", (128, TS, ST))]
        FE = [tl("FE_re", (128, TS, ST)), tl("FE_im", (128, TS, ST))]
        mm_ = [tl("mm%d" % i, (128, ST)) for i in range(4)]
        ang, g_tmp, g_msk = mm_[0], mm_[1], mm_[2]
        g_int = self.sbp("g_int", [128, ST], mybir.dt.int32)
        for s_ in range(TS):
            V(lambda e, s_=s_: e.tensor_scalar(ang[:], tv[:], T["ai"][:, s_:s_ + 1], None, ALU.mult), ["tvals", "ai"], ["mm0"])
            for j, off in ((1, 0.0), (0, 0.5 * PI)):
                reduce_generic(E[j][:, s_, :], ang[:], off, g_tmp[:], g_int[:], g_msk[:], ("E", j, s_))
                A(lambda e, j=j, s_=s_: e.activation(E[j][:, s_, :], E[j][:, s_, :], AF.Sin), [("E", j, s_)], [("E", j, s_)])
            fr, fi = T["f_re"][:, s_:s_ + 1], T["f_im"][:, s_:s_ + 1]
            V(lambda e, s_=s_, fi=fi: e.tensor_scalar(g_tmp[:], E[1][:, s_, :], fi, None, ALU.mult), [("E", 1, s_), "f_im"], ["mm1"])
            V(lambda e, s_=s_, fr=fr: e.scalar_tensor_tensor(FE[0][:, s_, :], E[0][:, s_, :], fr, g_tmp[:], ALU.mult, ALU.add), [("E", 0, s_), "f_re", "mm1"], ["FE"])
            V(lambda e, s_=s_, fr=fr: e.tensor_scalar(g_tmp[:], E[1][:, s_, :], fr, None, ALU.mult), [("E", 1, s_), "f_re"], ["mm1"])
            V(lambda e, s_=s_, fi=fi: e.scalar_tensor_tensor(FE[1][:, s_, :], E[0][:, s_, :], fi, g_tmp[:], ALU.mult, ALU.subtract), [("E", 0, s_), "f_im", "mm1"], ["FE"])
        zs = [tl("zs0", (128, ST)), tl("zs1", (128, ST))]
        NL = 1
        PR, PIm, NPI = tl("PR", (128, NL, TS)), tl("PIm", (128, NL, TS)), tl("NPI", (128, NL, TS))
        V(lambda e: e.tensor_copy(PR[:, 0, :], T["abr"][:]), ["abr"], ["PR"])
        V(lambda e: e.tensor_copy(PIm[:, 0, :], T["abi"][:]), ["abi"], ["PIm"])
        for k in range(1, NL):
            V(lambda e, k=k: e.tensor_tensor(T["t1"][:], PR[:, k - 1, :], PR[:, k - 1, :], ALU.mult), ["PR"], ["t1"])
            V(lambda e, k=k: e.tensor_tensor(T["t2"][:], PIm[:, k - 1, :], PIm[:, k - 1, :], ALU.mult), ["PIm"], ["t2"])
            V(lambda e, k=k: e.tensor_tensor(PIm[:, k, :], PR[:, k - 1, :], PIm[:, k - 1, :], ALU.mult), ["PR", "PIm"], ["PIm"])
            V(lambda e, k=k: e.tensor_scalar(PIm[:, k, :], PIm[:, k, :], 2.0, None, ALU.mult), ["PIm"], ["PIm"])
            V(lambda e, k=k: e.tensor_tensor(PR[:, k, :], T["t1"][:], T["t2"][:], ALU.subtract), ["t1", "t2"], ["PR"])
        V(lambda e: e.tensor_scalar(NPI[:], PIm[:], -1.0, None, ALU.mult), ["PIm"], ["NPI"])
        Bm = [tl("Bm_re", (128, TS, 128), BF16), tl("Bm_im", (128, TS, 128), BF16)]
        Cm = [tl("Cm_re", (128, TS, 128), BF16), tl("Cm_im", (128, TS, 128), BF16)]
        for t_, nm in ((Bm[0], "Bm_re"), (Bm[1], "Bm_im"), (Cm[0], "Cm_re"), (Cm[1], "Cm_im")):
            P.dma("gpsimd", t_[:], w[nm].rearrange("s r c -> r s c"), writes=[nm])
        wglu = tl("wglu", (128, 4, 512), BF16)
        P.dma("gpsimd", wglu[:], w["w_glu"].rearrange("(k p) f -> p k f", p=128), writes=["wglu"])
        dcol, bcol = tl("dcol", (128, 4)), tl("bcol", (128, 4))
        P.dma("sync", dcol[:], w["ssm_d"].rearrange("(c p) -> p c", p=128), writes=["dcol"], allow_slow_non_contiguous=True)
        P.dma("sync", bcol[:], w["b_glu"].rearrange("(c p) -> p c", p=128), writes=["bcol"], allow_slow_non_contiguous=True)
        car = self.car
        h0s = [tl("h0s_re", (128, NS, TS)), tl("h0s_im", (128, NS, TS))]
        for j in range(2):
            for b in range(NS):
                P.dma("sync", h0s[j][:, b, :], self.st_in[j][b].rearrange("(s q) -> q s", q=128), writes=["h0s"], allow_slow_non_contiguous=True)
        sts = [tl("sts_re", (128, NS, TS)), tl("sts_im", (128, NS, TS))]
        uT = tl("uT", (128, 4, ST), BF16)
        pp = [[tl("pp%d%d" % (i, j), (128, ST)) for j in range(2)] for i in range(2)]
        tmp = [mm_[2], mm_[3]]
        sbr, sbi = tl("sbr", (128, ST), BF16), tl("sbi", (128, ST), BF16)
        yraw, x2, tg = tl("yraw", (128, ST)), tl("x2", (128, ST)), tl("tg", (128, ST))
        ygf, ygb, yss = tl("ygf", (128, 4, ST)), tl("ygb", (128, 4, ST), BF16), tl("yss", (128, 4, ST), BF16)
        bank = self.bank
        self._b2 = dict(E=E, FE=FE, zs=zs, mm=mm_, T=T, PR=PR, PIm=PIm, NPI=NPI, Bm=Bm, Cm=Cm, wglu=wglu, dcol=dcol, bcol=bcol, car=car, h0s=h0s, sts=sts,
                        uT=uT, pp=pp, tmp=tmp, sbr=sbr, sbi=sbi, yraw=yraw, x2=x2, tg=tg, ygf=ygf, ygb=ygb, yss=yss)
        for ti, (kind, t0, n) in enumerate(self.srcs):
            if kind == "s" or t0 >= self.own0:
                self._b2_tile(ti, kind, t0, n)
        P.dma("sync", self.st_out_p[0].rearrange("s q -> q s"), car[0][:], reads=["car"], writes=["stp0"], allow_slow_non_contiguous=True)
        P.dma("sync", self.st_out_p[1].rearrange("s q -> q s"), car[1][:], reads=["car"], writes=["stp1"], allow_slow_non_contiguous=True)
        for j in range(2):
            for b in range(NS):
                P.dma("sync", self.st_out_s[j][b].rearrange("(s q) -> q s", q=128), sts[j][:, b, :], reads=["sts"], writes=[("stso", j, b)], allow_slow_non_contiguous=True)

    def _b2_tile(self, ti, kind, t0, n):
        P = self.P
        B = self._b2
        T, PR, PIm, NPI, Bm, Cm, car = B["T"], B["PR"], B["PIm"], B["NPI"], B["Bm"], B["Cm"], B["car"]
        uT, pp, tmp, sbr, sbi = B["uT"], B["pp"], B["tmp"], B["sbr"], B["sbi"]
        bank = self.bank
        gt0 = t0 if kind == "p" else SEQ
        V = lambda fn, r, wr: P.op("vector", fn, reads=r, writes=wr)
        is_own = kind == "s" or t0 >= self.own0
        P.dma("sync", uT[:, :, :n], self.uT_s[:, :, gt0:gt0 + n].rearrange("k p n -> p k n"), reads=[("uT_s", ti)], writes=["uT"])
        for c in range(4):
            for s4 in range(4):
                s = 4 * c + s4
                zb = (bank[0], bank[1]) if s % 2 == 0 else (bank[4], bank[5])
                zk = (("bank", 0), ("bank", 1)) if s % 2 == 0 else (("bank", 4), ("bank", 5))
                for j in range(2):
                    P.op("tensor", lambda e, j=j, s=s, c=c, zb=zb: e.matmul(zb[j][:, :n], Bm[j][:, s, :], uT[:, c, :n], start=True, stop=True),
                         reads=["uT", "Bm_re", "Bm_im"], writes=[zk[j]])
                fre, fim = T["f_re"][:, s:s + 1], T["f_im"][:, s:s + 1]
                cur = pp[0]
                if kind == "p":
                    E, FE, zs, mm = B["E"], B["FE"], B["zs"], B["mm"]
                    G = lambda fn, r, wr: P.op("gpsimd", fn, reads=r, writes=wr)
                    V(lambda e, s=s, zb=zb: e.tensor_tensor(mm[0][:, :n], FE[0][:, s, :n], zb[0][:, :n], ALU.mult), ["FE", zk[0]], ["mm0"])
                    V(lambda e, s=s, zb=zb: e.tensor_tensor(mm[1][:, :n], FE[1][:, s, :n], zb[1][:, :n], ALU.mult), ["FE", zk[1]], ["mm1"])
                    G(lambda e: e.tensor_tensor(mm[0][:, :n], mm[0][:, :n], mm[1][:, :n], ALU.subtract), ["mm0", "mm1"], ["mm0"])
                    V(lambda e, s=s, zb=zb: e.tensor_tensor(mm[2][:, :n], FE[0][:, s, :n], zb[1][:, :n], ALU.mult), ["FE", zk[1]], ["mm2"])
                    V(lambda e, s=s, zb=zb: e.tensor_tensor(mm[3][:, :n], FE[1][:, s, :n], zb[0][:, :n], ALU.mult), ["FE", zk[0]], ["mm3"])
                    G(lambda e: e.tensor_tensor(mm[2][:, :n], mm[2][:, :n], mm[3][:, :n], ALU.add), ["mm2", "mm3"], ["mm2"])
                    rho = T["mag"][:, s:s + 1].broadcast_to([128, n])
                    V(lambda e, s=s, rho=rho: e.tensor_tensor_scan(pp[1][0][:, :n], rho, mm[0][:, :n], car[0][:, s:s + 1], ALU.mult, ALU.add),
                      ["mm0", "car", "mag"], ["pp10"])
                    V(lambda e, s=s, rho=rho: e.tensor_tensor_scan(pp[1][1][:, :n], rho, mm[2][:, :n], car[1][:, s:s + 1], ALU.mult, ALU.add),
                      ["mm2", "car", "mag"], ["pp11"])
                    wr_, wi_ = pp[1][0], pp[1][1]
                    if not is_own:
                        cl = slice(n - 1, n)
                        V(lambda e, s=s: e.tensor_tensor(mm[0][:, 0:1], wi_[:, cl], E[1][:, s, cl], ALU.mult), [("E", 1, s), "pp11"], ["mm0"])
                        V(lambda e, s=s: e.scalar_tensor_tensor(car[0][:, s:s + 1], wr_[:, cl], E[0][:, s, cl], mm[0][:, 0:1], ALU.mult, ALU.subtract),
                          [("E", 0, s), "pp10", "mm0"], ["car"])
                        V(lambda e, s=s: e.tensor_tensor(mm[2][:, 0:1], wr_[:, cl], E[1][:, s, cl], ALU.mult), [("E", 1, s), "pp10"], ["mm2"])
                        V(lambda e, s=s: e.scalar_tensor_tensor(car[1][:, s:s + 1], wi_[:, cl], E[0][:, s, cl], mm[2][:, 0:1], ALU.mult, ALU.add),
                          [("E", 0, s), "pp11", "mm2"], ["car"])
                        continue
                    G(lambda e, s=s: e.tensor_tensor(mm[0][:, :n], E[0][:, s, :n], wr_[:, :n], ALU.mult), [("E", 0, s), "pp10"], ["mm0"])
                    G(lambda e, s=s: e.tensor_tensor(mm[1][:, :n], E[1][:, s, :n], wi_[:, :n], ALU.mult), [("E", 1, s), "pp11"], ["mm1"])
                    G(lambda e: e.tensor_tensor(cur[0][:, :n], mm[0][:, :n], mm[1][:, :n], ALU.subtract), ["mm0", "mm1"], ["pp00"])
                    V(lambda e, s=s: e.tensor_tensor(mm[2][:, :n], E[0][:, s, :n], wi_[:, :n], ALU.mult), [("E", 0, s), "pp11"], ["mm2"])
                    V(lambda e, s=s: e.tensor_tensor(mm[3][:, :n], E[1][:, s, :n], wr_[:, :n], ALU.mult), [("E", 1, s), "pp10"], ["mm3"])
                    V(lambda e: e.tensor_tensor(cur[1][:, :n], mm[2][:, :n], mm[3][:, :n], ALU.add), ["mm2", "mm3"], ["pp01"])
                    ci = 0
                else:
                    V(lambda e, zb=zb, fim=fim: e.tensor_scalar(tmp[0][:, :n], zb[1][:, :n], fim, None, ALU.mult), [zk[1], "f_im"], ["mm2"])
                    V(lambda e, zb=zb, fre=fre, cur=cur: e.scalar_tensor_tensor(cur[0][:, :n], zb[0][:, :n], fre, tmp[0][:, :n], ALU.mult, ALU.subtract),
                      [zk[0], "f_re", "mm2"], ["pp00"])
                    V(lambda e, zb=zb, fim=fim: e.tensor_scalar(tmp[1][:, :n], zb[0][:, :n], fim, None, ALU.mult), [zk[0], "f_im"], ["mm3"])
                    V(lambda e, zb=zb, fre=fre, cur=cur: e.scalar_tensor_tensor(cur[1][:, :n], zb[1][:, :n], fre, tmp[1][:, :n], ALU.mult, ALU.add),
                      [zk[1], "f_re", "mm3"], ["pp01"])
                    a0, b0, nb0 = PR[:, 0, s:s + 1], PIm[:, 0, s:s + 1], NPI[:, 0, s:s + 1]
                    hr, hi = B["h0s"][0][:, :, s], B["h0s"][1][:, :, s]
                    w0 = n
                    V(lambda e, cur=cur, hr=hr, a0=a0: e.scalar_tensor_tensor(cur[0][:, :w0], hr, a0, cur[0][:, :w0], ALU.mult, ALU.add), ["pp00", "h0s", "PR"], ["pp00"])
                    V(lambda e, cur=cur, hi=hi, nb0=nb0: e.scalar_tensor_tensor(cur[0][:, :w0], hi, nb0, cur[0][:, :w0], ALU.mult, ALU.add), ["pp00", "h0s", "NPI"], ["pp00"])
                    V(lambda e, cur=cur, hi=hi, a0=a0: e.scalar_tensor_tensor(cur[1][:, :w0], hi, a0, cur[1][:, :w0], ALU.mult, ALU.add), ["pp01", "h0s", "PR"], ["pp01"])
                    V(lambda e, cur=cur, hr=hr, b0=b0: e.scalar_tensor_tensor(cur[1][:, :w0], hr, b0, cur[1][:, :w0], ALU.mult, ALU.add), ["pp01", "h0s", "PIm"], ["pp01"])
                    ci = 0
                X = pp[ci]
                kx = ["pp%d0" % ci, "pp%d1" % ci]
                if kind == "p":
                    P.op("scalar", lambda e, X=X, s=s: e.copy(car[0][:, s:s + 1], X[0][:, n - 1:n]), reads=[kx[0]], writes=["car"])
                    P.op("scalar", lambda e, X=X, s=s: e.copy(car[1][:, s:s + 1], X[1][:, n - 1:n]), reads=[kx[1]], writes=["car"])
                else:
                    P.op("scalar", lambda e, X=X, s=s: e.copy(B["sts"][0][:, :, s], X[0][:, :n]), reads=[kx[0]], writes=["sts"])
                    P.op("scalar", lambda e, X=X, s=s: e.copy(B["sts"][1][:, :, s], X[1][:, :n]), reads=[kx[1]], writes=["sts"])
                P.op("scalar", lambda e, X=X: e.copy(sbr[:, :n], X[0][:, :n]), reads=[kx[0]], writes=["sbr"])
                P.op("scalar", lambda e, X=X: e.mul(sbi[:, :n], X[1][:, :n], -1.0), reads=[kx[1]], writes=["sbi"])

                def mmy(e, s=s, s4=s4):
                    e.matmul(bank[7][:, :n], Cm[0][:, s, :], sbr[:, :n], start=(s4 == 0), stop=False)
                    return e.matmul(bank[7][:, :n], Cm[1][:, s, :], sbi[:, :n], start=False, stop=(s4 == 3))
                P.op("tensor", mmy, reads=["sbr", "sbi", "Cm_re", "Cm_im"], writes=[("bank", 7)])
            if not is_own:
                continue
            yraw, x2, tg, ygf, ygb = B["yraw"], B["x2"], B["tg"], B["ygf"], B["ygb"]
            V(lambda e, c=c: e.scalar_tensor_tensor(yraw[:, :n], uT[:, c, :n], B["dcol"][:, c:c + 1], bank[7][:, :n], ALU.mult, ALU.add),
              ["uT", "dcol", ("bank", 7)], ["yraw"])
            P.op("scalar", lambda e: e.activation(x2[:, :n], yraw[:, :n], AF.Square), reads=["yraw"], writes=["x2"])
            V(lambda e: e.tensor_scalar(x2[:, :n], x2[:, :n], 0.044715, 1.0, ALU.mult, ALU.add), ["x2"], ["x2"])
            V(lambda e: e.tensor_tensor(tg[:, :n], x2[:, :n], yraw[:, :n], ALU.mult), ["x2", "yraw"], ["tg"])
            P.op("scalar", lambda e: e.activation(tg[:, :n], tg[:, :n], AF.Sigmoid, scale=1.5957691216057308), reads=["tg"], writes=["tg"])
            V(lambda e, c=c: e.tensor_tensor(ygf[:, c, :n], yraw[:, :n], tg[:, :n], ALU.mult), ["tg", "yraw"], [("ygf", c)])
            P.op("gpsimd", lambda e, c=c: e.tensor_copy(ygb[:, c, :n], ygf[:, c, :n]), reads=[("ygf", c)], writes=[("ygb", c)])
        if not is_own:
            return
        wglu, yss = B["wglu"], B["yss"]
        for m in range(4):
            bg = bank[2 + m % 2]

            def mmg(e, m=m, bg=bg):
                last = None
                for k in range(4):
                    last = e.matmul(bg[:, :n], wglu[:, k, m * 128:(m + 1) * 128], ygb[:, k, :n], start=(k == 0), stop=(k == 3))
                return last
            P.op("tensor", mmg, reads=[("ygb", k) for k in range(4)] + ["wglu"], writes=[("bank", 2 + m % 2)])
            P.op("scalar", lambda e, m=m, bg=bg: e.activation(tg[:, :n], bg[:, :n], AF.Sigmoid, bias=B["bcol"][:, m:m + 1]),
                 reads=[("bank", 2 + m % 2), "bcol"], writes=["tg"])
            V(lambda e, m=m: e.tensor_tensor(yss[:, m, :n], ygf[:, m, :n], tg[:, :n], ALU.mult), ["tg", ("ygf", m)], [("yss", m)])
        P.dma("sync", self.yssT_s[:, :, gt0:gt0 + n].rearrange("k p n -> p k n"), yss[:, :, :n],
              reads=[("yss", m) for m in range(4)], writes=[("yssT_s", ti)])

    def phase_b3(self):
        P, w = self.P, self.w
        tl = self.sbp
        A = {}
        A["mask"] = [tl("mask_cur", [128, 256], BF16), tl("mask_prev", [128, 256], BF16), tl("mask_halo", [128, 256], BF16)]
        P.dma("gpsimd", A["mask"][0][:], w["mask_cur"][:, :], writes=["mask"])
        P.dma("gpsimd", A["mask"][1][:], w["mask_prev"][:, :], writes=["mask"])
        P.dma("gpsimd", A["mask"][2][:], w["mask_halo"][:, :], writes=["mask"])
        A["identB"] = tl("identB", [128, 128], BF16)
        P.dma("gpsimd", A["identB"][:], w["ident"][:, :], writes=["identB"])
        A["kv"] = [tl("kvA%d" % i, [128, 512], F32) for i in range(2)]
        A["qf"] = [tl("qf%d" % i, [128, 256], F32) for i in range(2)]
        A["kT"] = [tl("kT%d" % i, [128, 2, 128], BF16) for i in range(3)]
        A["Vz"] = [tl("Vz%d" % i, [128, 4, 128], BF16) for i in range(3)]
        A["qTz"] = [[tl("qTz%d%d" % (i, hh), [128, 2, 128], BF16) for hh in range(2)] for i in range(2)]
        A["onesZ"] = [tl("onesZ%d" % hh, [128, 128], BF16) for hh in range(2)]
        for i in range(3):
            P.op("vector", lambda e, i=i: e.memset(A["Vz"][i][:], 0.0), writes=[("Vb", i)])
        for i in range(2):
            for hh in range(2):
                P.op("vector", lambda e, i=i, hh=hh: e.memset(A["qTz"][i][hh][:], 0.0), writes=[("qT", i)])
        for hh in range(2):
            P.op("vector", lambda e, hh=hh: e.memset(A["onesZ"][hh][:], 0.0), writes=["onesZ"])
            P.op("vector", lambda e, hh=hh: e.memset(A["onesZ"][hh][:, 64 * hh:64 * hh + 64], 1.0), reads=["onesZ"], writes=["onesZ"])
        A["Pe"] = [tl("Pe%d" % i, [128, 512], BF16) for i in range(2)]
        A["Pm"] = [tl("Pm%d" % i, [128, 512], BF16) for i in range(2)]
        BLK = 2048
        A["accN"] = tl("accN", [128, 2, BLK], F32)
        A["accD"] = tl("accD", [128, 2, BLK], F32)
        A["yat"] = tl("yat", [128, 2, BLK], BF16)
        self._b3 = A
        self.pi = 0
        self.ui = 0
        V = lambda fn, r, wr: P.op("vector", fn, reads=r, writes=wr)
        groups = ((128, 1), (512, 4), (2048, 16))
        nblk = SEQ // BLK
        for bb in range(nblk - 1, nblk):
            V(lambda e: e.memset(A["accN"][:], 0.0), [], ["accN"])
            V(lambda e: e.memset(A["accD"][:], 0.0), [], ["accD"])
            for g, (W, dl) in enumerate(groups):
                span = 128 * dl
                for r in range(dl):
                    prev = None
                    for bk in range(BLK // span):
                        base = bb * BLK + bk * span
                        if prev is None and base >= span:
                            prev = self._b3_prep(g, base - span, r, dl, 128)
                        cur = self._b3_prep(g, base, r, dl, 128)
                        qT = self._b3_q(g, base, r, dl, 128)
                        pm = 2 if (base - span) < self.own0 else 1
                        ksets = [(cur, 128, 0)] + ([(prev, 128, pm)] if prev is not None else [])
                        c0 = bk * span + r
                        views = lambda acc, pr, c0=c0, dl=dl, span=span: acc[:, pr, c0:c0 + 127 * dl + 1:dl] if dl > 1 else acc[:, pr, c0:c0 + 128]
                        self._b3_unit(qT, 128, ksets, views)
                        prev = cur
            self._b3_norm(bb * BLK, BLK, A["accN"], A["accD"], A["yat"])
        V(lambda e: e.memset(A["accN"][:], 0.0), [], ["accN"])
        V(lambda e: e.memset(A["accD"][:], 0.0), [], ["accD"])
        for b in range(NS):
            for g, (W, dl) in enumerate(groups):
                cset = self._b3_prep(g, None, 0, dl, 128, cache=(g, b))
                sset = self._b3_prep(g, SEQ + b, 0, 1, 1)
                qT = self._b3_q(g, SEQ + b, 0, 1, 1)
                views = lambda acc, pr, b=b: acc[:, pr, b:b + 1]
                self._b3_unit(qT, 1, [(cset, 128, None), (sset, 1, None)], views)
        self._b3_norm(SEQ, NS, A["accN"], A["accD"], A["yat"])

    def _b3_prep(self, g, base, r, dl, nk, cache=None):
        P, A, bank = self.P, self._b3, self.bank
        i = self.pi % 3
        j = self.pi % 2
        self.pi += 1
        kv, kT, Vz = A["kv"][j], A["kT"][i], A["Vz"][i]
        if cache is not None:
            cg, b = cache
            W = (128, 512, 2048)[cg]
            src = self.cache[cg][b, :, :].rearrange("(m d) f -> d m f", d=dl)[0]
            rd = []
        elif dl > 1:
            src = self.kv_s[g][base:base + 128 * dl, :].rearrange("(m d) f -> d m f", d=dl)[r]
            rd = [("kv_s", g, ti) for ti in range(len(self.srcs))]
        else:
            src = self.kv_s[g][base:base + nk, :]
            rd = [("kv_s", g, ti) for ti in range(len(self.srcs))]
        P.dma("sync", kv[:nk, :], src, reads=rd, writes=[("kvA", j)])
        tb = bank[6 + j]
        for pr in range(2):
            P.op("tensor", lambda e, pr=pr: e.transpose(tb[:, pr * 128:pr * 128 + nk], kv[:nk, pr * 128:(pr + 1) * 128], self.identF[:nk, :nk]),
                 reads=[("kvA", j), "identF"], writes=[("bank", 6 + j)])
        P.op("scalar", lambda e: e.copy(kT[:, :, :nk], tb[:, 0:256].rearrange("p (a n) -> p a n", a=2)[:, :, :nk]),
             reads=[("bank", 6 + j)], writes=[("kT", i)])
        v4 = kv[:nk, 256:512].rearrange("p (h d) -> p h d", d=64)
        P.op("gpsimd", lambda e: e.tensor_copy(Vz[:nk, 0::2, 0:64], v4[:, 0::2, :]), reads=[("kvA", j)], writes=[("Vb", i)])
        P.op("gpsimd", lambda e: e.tensor_copy(Vz[:nk, 1::2, 64:128], v4[:, 1::2, :]), reads=[("kvA", j)], writes=[("Vb", i)])
        return i

    def _b3_q(self, g, base, r, dl, nq):
        P, A, bank = self.P, self._b3, self.bank
        j = self.ui % 2
        qf, qTz = A["qf"][j], A["qTz"][j]
        if dl > 1:
            src = self.q_s[g][base:base + 128 * dl, :].rearrange("(m d) f -> d m f", d=dl)[r]
        else:
            src = self.q_s[g][base:base + nq, :]
        P.dma("sync", qf[:nq, :], src, reads=[("q_s", g, ti) for ti in range(len(self.srcs))], writes=[("qf", j)])
        tb = bank[6 + j]
        for pr in range(2):
            P.op("tensor", lambda e, pr=pr: e.transpose(tb[:, 256 + pr * 128:256 + pr * 128 + nq], qf[:nq, pr * 128:(pr + 1) * 128], self.identF[:nq, :nq]),
                 reads=[("qf", j), "identF"], writes=[("bank", 6 + j)])
        P.op("vector", lambda e: e.tensor_copy(qTz[0][0:64, :, :nq], tb[0:64, 256:512].rearrange("p (a n) -> p a n", a=2)[:, :, :nq]),
             reads=[("bank", 6 + j)], writes=[("qT", j)])
        P.op("vector", lambda e: e.tensor_copy(qTz[1][64:128, :, :nq], tb[64:128, 256:512].rearrange("p (a n) -> p a n", a=2)[:, :, :nq]),
             reads=[("bank", 6 + j)], writes=[("qT", j)])
        return j

    def _b3_unit(self, qi, nq, ksets, views):
        P, A, bank = self.P, self._b3, self.bank
        qTz = A["qTz"][qi]
        V = lambda fn, r, wr: P.op("vector", fn, reads=r, writes=wr)
        for pr in range(2):
            u = self.ui
            self.ui += 1
            j = u % 2
            bS, bN, bD = bank[0 + j], bank[2 + j], bank[4 + j]
            kS, kN, kD = ("bank", j), ("bank", 2 + j), ("bank", 4 + j)
            Pe, Pm = A["Pe"][j], A["Pm"][j]

            def mms(e, pr=pr):
                last = None
                for si, (ki, nk, mk) in enumerate(ksets):
                    kT = A["kT"][ki]
                    for hh in range(2):
                        slot = si * 2 + hh
                        last = e.matmul(bS[:nk, slot * nq:(slot + 1) * nq], kT[:, pr, :nk],
                                        qTz[hh][:, pr, :nq], start=True, stop=True)
                return last
            P.op("tensor", mms, reads=[("kT", ki) for ki, _, _ in ksets] + [("qT", qi)], writes=[kS])
            for si, (ki, nk, mk) in enumerate(ksets):
                lo, hi = si * 2 * nq, (si + 1) * 2 * nq
                P.op("scalar", lambda e, nk=nk, lo=lo, hi=hi: e.activation(Pe[:nk, lo:hi], bS[:nk, lo:hi], AF.Exp),
                     reads=[kS], writes=[("Pe", j, si)])
                if mk is not None:
                    P.op("gpsimd", lambda e, nk=nk, lo=lo, hi=hi, mk=mk: e.tensor_tensor(Pm[:nk, lo:hi], Pe[:nk, lo:hi], A["mask"][mk][:nk, :], ALU.mult),
                         reads=[("Pe", j, si), "mask"], writes=[("Pm", j, si)])
                else:
                    P.op("gpsimd", lambda e, nk=nk, lo=lo, hi=hi: e.tensor_copy(Pm[:nk, lo:hi], Pe[:nk, lo:hi]),
                         reads=[("Pe", j, si)], writes=[("Pm", j, si)])

            def mmav(e, pr=pr):
                last = None
                ns = len(ksets)
                tot = 2 * ns
                for lhs_of, bO in ((lambda ki, nk, hh: A["Vz"][ki][:nk, 2 * pr + hh, :], bN), (lambda ki, nk, hh: A["onesZ"][hh][:nk, :], bD)):
                    c = 0
                    for hh in range(2):
                        for si, (ki, nk, mk) in enumerate(ksets):
                            slot = si * 2 + hh
                            last = e.matmul(bO[:, :nq], lhs_of(ki, nk, hh), Pm[:nk, slot * nq:(slot + 1) * nq],
                                            start=(c == 0), stop=(c == tot - 1))
                            c += 1
                return last
            P.op("tensor", mmav, reads=[("Vb", ki) for ki, _, _ in ksets] + [("Pm", j, si) for si in range(len(ksets))] + ["onesZ"],
                 writes=[kN, kD])
            vn, vd = views(A["accN"], pr), views(A["accD"], pr)
            V(lambda e, vn=vn: e.tensor_tensor(vn, vn, bN[:, :nq], ALU.add), [kN, "accN"], ["accN"])
            V(lambda e, vd=vd: e.tensor_tensor(vd, vd, bD[:, :nq], ALU.add), [kD, "accD"], ["accD"])

    def _b3_norm(self, col0, n, accN, accD, yat):
        P = self.P
        V = lambda fn, r, wr: P.op("vector", fn, reads=r, writes=wr)
        V(lambda e: e.reciprocal(accD[:, :, :n], accD[:, :, :n]), ["accD"], ["accD"])
        V(lambda e: e.tensor_tensor(yat[:, :, :n], accN[:, :, :n], accD[:, :, :n], ALU.mult), ["accN", "accD"], ["yat"])
        P.dma("sync", self.yatT_s[:, :, col0:col0 + n].rearrange("k p n -> p k n"), yat[:, :, :n], reads=["yat"], writes=[("yatT_s", col0)])

    def phase_b4(self):
        P, w = self.P, self.w
        tl = self.sbp
        wing = tl("wing", [128, 8, 2048], BF16)
        for k in range(8):
            P.dma("gpsimd", wing[:, k, :], w["w_in"][k * 128:(k + 1) * 128, 2816:4864], writes=[("wing", k)], max_dma_last_dim=4096)
        wsp, wap, wo = tl("wsp", [128, 4, D], BF16), tl("wap", [128, 2, D], BF16), tl("wo", [128, 8, D], BF16)
        P.dma("gpsimd", wsp[:], w["w_ssm_proj"].rearrange("(k p) f -> p k f", p=128), writes=["wsp"])
        P.dma("gpsimd", wap[:], w["w_attn_proj"].rearrange("(k p) f -> p k f", p=128), writes=["wap"])
        P.dma("gpsimd", wo[:], w["w_o"].rearrange("(k p) f -> p k f", p=128), writes=["wo"])
        B = dict(wing=wing, wsp=wsp, wap=wap, wo=wo,
                 hT=tl("hT4", [128, 8, ST], BF16), yss=tl("yss4", [128, 4, ST], BF16), yat=tl("yat4", [128, 2, ST], BF16),
                 xT=tl("xT4", [128, 8, ST], F32), mixed=tl("mixed", [128, 8, ST], BF16),
                 sg=[tl("sg%d" % i, [128, ST], F32) for i in range(2)], tm=[tl("tm%d" % i, [128, ST], F32) for i in range(2)])
        self._b4 = B
        for ti, (kind, t0, n) in enumerate(self.srcs):
            if ti in self.own_tis:
                self._b4_tile(ti, kind, t0, n)

    def _b4_tile(self, ti, kind, t0, n):
        P, B, bank = self.P, self._b4, self.bank
        gt0 = t0 if kind == "p" else SEQ
        hT, yss, yat, xT, mixed, sg, tm = B["hT"], B["yss"], B["yat"], B["xT"], B["mixed"], B["sg"], B["tm"]
        wing, wsp, wap, wo = B["wing"], B["wsp"], B["wap"], B["wo"]
        V = lambda fn, r, wr: P.op("vector", fn, reads=r, writes=wr)
        fm = lambda t: t[:, :, gt0:gt0 + n].rearrange("k p n -> p k n")
        P.dma("sync", hT[:, :, :n], fm(self.hT_s), writes=["hT4"])
        P.dma("sync", yss[:, :, :n], fm(self.yssT_s), writes=["yss4"])
        P.dma("sync", yat[:, :, :n], fm(self.yatT_s), writes=["yat4"])
        P.dma("sync", xT[:, :, :n], fm(self.x1T), reads=[("x1T", ti)], writes=["xT4"])
        wk = [("wing", k) for k in range(8)]
        for m in range(8):
            for br, (src, nk_, wp, key, coff) in enumerate(((yss, 4, wsp, "yss4", 0), (yat, 2, wap, "yat4", 1024))):
                bP, bG = bank[2 * br], bank[2 * br + 1]
                kP, kG = ("bank", 2 * br), ("bank", 2 * br + 1)

                def mmp(e, m=m, src=src, nk_=nk_, wp=wp, bP=bP):
                    last = None
                    for k in range(nk_):
                        last = e.matmul(bP[:, :n], wp[:, k, m * 128:(m + 1) * 128], src[:, k, :n], start=(k == 0), stop=(k == nk_ - 1))
                    return last

                def mmg(e, m=m, coff=coff, bG=bG):
                    last = None
                    for k in range(8):
                        last = e.matmul(bG[:, :n], wing[:, k, coff + m * 128:coff + (m + 1) * 128], hT[:, k, :n], start=(k == 0), stop=(k == 7))
                    return last
                P.op("tensor", mmp, reads=[key, "wsp", "wap"], writes=[kP])
                P.op("tensor", mmg, reads=["hT4"] + wk, writes=[kG])
                P.op("scalar", lambda e, br=br, bG=bG: e.activation(sg[br][:, :n], bG[:, :n], AF.Sigmoid), reads=[kG], writes=[("sg", br)])
                V(lambda e, br=br, bP=bP: e.tensor_tensor(tm[br][:, :n], sg[br][:, :n], bP[:, :n], ALU.mult), [("sg", br), kP], [("tm", br)])
            P.op("gpsimd", lambda e, m=m: e.tensor_tensor(mixed[:, m, :n], tm[0][:, :n], tm[1][:, :n], ALU.add),
                 reads=[("tm", 0), ("tm", 1)], writes=[("mixed", m)])
        for m in range(8):
            bo = bank[4 + m % 2]

            def mmo(e, m=m, bo=bo):
                last = None
                for k in range(8):
                    last = e.matmul(bo[:, :n], wo[:, k, m * 128:(m + 1) * 128], mixed[:, k, :n], start=(k == 0), stop=(k == 7))
                return last
            P.op("tensor", mmo, reads=[("mixed", k) for k in range(8)] + ["wo"], writes=[("bank", 4 + m % 2)])
            V(lambda e, m=m, bo=bo: e.tensor_tensor(xT[:, m, :n], xT[:, m, :n], bo[:, :n], ALU.add), [("bank", 4 + m % 2), "xT4"], ["xT4"])
        P.dma("sync", fm(self.x1T), xT[:, :, :n], reads=["xT4"], writes=[("x1T", ti)])

    def load_ffn_weights(self, tag, g, wgate, wup, wdown):
        P = self.P
        P.dma("sync", self.gcol[:], g.rearrange("(k p) -> p k", p=128), writes=["gcol"], allow_slow_non_contiguous=True)
        for k in range(8):
            P.dma("gpsimd", self.wg[:, k, :], wgate[k * 128:(k + 1) * 128, :], writes=[("wg", k)], max_dma_last_dim=4096)
            P.dma("gpsimd", self.wu[:, k, :], wup[k * 128:(k + 1) * 128, :], writes=[("wu", k)], max_dma_last_dim=4096)
        for f in range(NF):
            P.dma("gpsimd", self.wd[:, f, :], wdown[f * 128:(f + 1) * 128, :], writes=[("wd", f)], max_dma_last_dim=4096)

    def ffn_phase(self, tag, g, wgate, wup, wdown, srcs, src_tok, src_T, dst_T, dst_tok, only=None):
        P, nc = self.P, self.nc
        self.load_ffn_weights(tag, g, wgate, wup, wdown)
        xT, sq, hT, rstd, aT = self.xT, self.sq, self.hT, self.rstd, self.aT
        bank = self.bank
        for ti, (kind, t0, n) in enumerate(srcs):
            if only is not None and ti not in only:
                continue
            self._ffn_tile(ti, kind, t0, n, src_tok, src_T, dst_T, dst_tok)

    def _ffn_tile(self, ti, kind, t0, n, src_tok, src_T, dst_T, dst_tok):
        P, nc = self.P, self.nc
        xT, sq, hT, rstd, aT = self.xT, self.sq, self.hT, self.rstd, self.aT
        bank = self.bank
        if True:
            gt0 = t0 if kind == "p" else SEQ
            nsub = (n + 127) // 128
            if src_tok is not None:
                src = src_tok[0] if kind == "p" else src_tok[1]
                for s in range(nsub):
                    r = min(128, n - s * 128)
                    xin = self.xin[s % 2]
                    P.dma("sync", xin[:r, :], src[t0 + s * 128: t0 + s * 128 + r, :], writes=[("xin", s % 2)])
                    for k in range(8):
                        b = bank[k]
                        P.op("tensor", lambda e, b=b, xin=xin, k=k, s=s, r=r: e.transpose(
                            b[:, s * 128: s * 128 + r], xin[:r, k * 128:(k + 1) * 128], self.identF[:r, :r]),
                            reads=[("xin", s % 2), "identF"], writes=[("bank", k)])
                for k in range(8):
                    P.op("scalar" if k % 2 else "vector",
                         (lambda e, k=k: e.copy(xT[:, k, :n], bank[k][:, :n])) if k % 2 else
                         (lambda e, k=k: e.tensor_copy(xT[:, k, :n], bank[k][:, :n])),
                         reads=[("bank", k)], writes=[("xT", k)])
            else:
                P.dma("sync", xT[:, :, :n], src_T[:, :, gt0:gt0 + n].rearrange("k p n -> p k n"),
                      writes=[("xT", k) for k in range(8)])
            for k in range(8):
                P.op("scalar", lambda e, k=k: e.activation(sq[:, k, :n], xT[:, k, :n], AF.Square),
                     reads=[("xT", k)], writes=[("sq", k)])

            def nrm(e):
                last = None
                for k in range(8):
                    last = e.matmul(bank[6][:, :n], self.onesB[:], sq[:, k, :n], start=(k == 0), stop=(k == 7))
                return last
            P.op("tensor", nrm, reads=[("sq", k) for k in range(8)] + ["onesB"], writes=[("bank", 6)])
            P.op("scalar", lambda e: e.activation(rstd[:, :n], bank[6][:, :n], AF.Sqrt, bias=self.epsC[:], scale=1.0 / D),
                 reads=[("bank", 6), "epsC"], writes=["rstd"])
            P.op("vector", lambda e: e.reciprocal(rstd[:, :n], rstd[:, :n]), reads=["rstd"], writes=["rstd"])
            for k in range(8):
                P.op("vector", lambda e, k=k: e.scalar_tensor_tensor(
                    hT[:, k, :n], xT[:, k, :n], self.gcol[:, k:k + 1], rstd[:, :n], ALU.mult, ALU.mult),
                    reads=[("xT", k), "gcol", "rstd"], writes=[("hT", k)])
            for f in range(NF):
                bg = bank[f % 2]
                bu = bank[2 + f % 2]
                sil = self.sil[f % 2]

                def mmg(e, f=f, bg=bg):
                    last = None
                    for k in range(8):
                        last = e.matmul(bg[:, :n], self.wg[:, k, f * 128:(f + 1) * 128], hT[:, k, :n], start=(k == 0), stop=(k == 7))
                    return last

                def mmu(e, f=f, bu=bu):
                    last = None
                    for k in range(8):
                        last = e.matmul(bu[:, :n], self.wu[:, k, f * 128:(f + 1) * 128], hT[:, k, :n], start=(k == 0), stop=(k == 7))
                    return last
                hk = [("hT", k) for k in range(8)]
                P.op("tensor", mmg, reads=hk + [("wg", k) for k in range(8)], writes=[("bank", f % 2)])
                P.op("tensor", mmu, reads=hk + [("wu", k) for k in range(8)], writes=[("bank", 2 + f % 2)])
                P.op("scalar", lambda e, bg=bg, sil=sil: e.activation(sil[:, :n], bg[:, :n], AF.Silu),
                     reads=[("bank", f % 2)], writes=[("sil", f % 2)])
                P.op("vector", lambda e, f=f, bu=bu, sil=sil: e.tensor_tensor(aT[:, f, :n], sil[:, :n], bu[:, :n], ALU.mult),
                     reads=[("sil", f % 2), ("bank", 2 + f % 2)], writes=[("aT", f)])
            for m in range(8):
                bd = bank[4 + m % 2]

                def mmd(e, m=m, bd=bd):
                    last = None
                    for f in range(NF):
                        last = e.matmul(bd[:, :n], self.wd[:, f, m * 128:(m + 1) * 128], aT[:, f, :n], start=(f == 0), stop=(f == NF - 1))
                    return last
                P.op("tensor", mmd, reads=[("aT", f) for f in range(NF)] + [("wd", f) for f in range(NF)], writes=[("bank", 4 + m % 2)])
                P.op("vector", lambda e, m=m, bd=bd: e.scalar_tensor_tensor(
                    xT[:, m, :n], bd[:, :n], 0.5, xT[:, m, :n], ALU.mult, ALU.add),
                    reads=[("bank", 4 + m % 2), ("xT", m)], writes=[("xT", m)])
            if dst_T is not None:
                P.dma("sync", dst_T[:, :, gt0:gt0 + n].rearrange("k p n -> p k n"), xT[:, :, :n],
                      reads=[("xT", k) for k in range(8)], writes=[("x1T", ti)])
            if dst_tok is not None:
                dst = dst_tok[0] if kind == "p" else dst_tok[1]
                for s in range(nsub):
                    r = min(128, n - s * 128)
                    xo = self.xin[s % 2]
                    for k in range(8):
                        bk = bank[6 + (k // 4)]
                        P.op("tensor", lambda e, bk=bk, k=k, s=s, r=r: e.transpose(
                            bk[:r, (k % 4) * 128:(k % 4 + 1) * 128], xT[:, k, s * 128: s * 128 + r], self.identF[:, :]),
                            reads=[("xT", k), "identF"], writes=[("bank", 6 + k // 4)])
                    P.op("vector", lambda e, xo=xo, r=r: e.tensor_copy(xo[:r, 0:512], bank[6][:r, :]),
                         reads=[("bank", 6)], writes=[("xin", s % 2)])
                    P.op("scalar", lambda e, xo=xo, r=r: e.copy(xo[:r, 512:1024], bank[7][:r, :]),
                         reads=[("bank", 7)], writes=[("xin", s % 2)])
                    o0 = (t0 - self.own0) if kind == "p" else t0
                    P.dma("sync", dst[o0 + s * 128: o0 + s * 128 + r, :], xo[:r, :], reads=[("xin", s % 2)],
                          writes=[("yout", kind, t0, s)])


_CACHE = {}
WINS = (128, 512, 2048)


def make_in_maps(inp, seq=None):
    seq = SEQ if seq is None else seq
    ident = np.eye(128, dtype=np.float32)
    caches = [inp["cache_kv_w%d" % W][0].reshape(32, W, 512) for W in WINS]
    tvals = np.ascontiguousarray(np.tile(np.arange(1, ST + 1, dtype=np.float32)[None, :], (128, 1)))
    kk, qq = np.meshgrid(np.arange(128), np.arange(128), indexing="ij")
    mask_cur = np.ascontiguousarray(np.tile((kk <= qq).astype(np.float32), (1, 2)))
    mask_prev = np.ascontiguousarray(np.tile((kk >= qq).astype(np.float32), (1, 2)))
    ssm_mats = {nm: np.zeros((16, 128, 128), np.float32) for nm in ("Bm_re", "Bm_im", "Cm_re", "Cm_im")}
    for s_ in range(16):
        for gi in range(2):
            g = 2 * s_ + gi
            r0 = 32 * (s_ % 4) + 16 * gi
            for nm, src in (("Bm_re", "ssm_b_re"), ("Bm_im", "ssm_b_im")):
                ssm_mats[nm][s_, r0:r0 + 16, 64 * gi:64 * gi + 64] = inp[src][0][g].T
            for nm, src in (("Cm_re", "ssm_c_re"), ("Cm_im", "ssm_c_im")):
                ssm_mats[nm][s_, 64 * gi:64 * gi + 64, r0:r0 + 16] = inp[src][0][g].T
    in_maps = []
    for c in range(NCORES):
        m = {}
        b_, q_ = c // 4, c % 4
        xw = np.zeros((seq, D), np.float32)
        nreal = QTR * (q_ + 1)
        xw[seq - nreal:] = inp["x_prompt"][b_][:nreal]
        m["xp"] = xw
        m["mask_halo"] = mask_prev if q_ > 0 else np.zeros_like(mask_prev)
        m["xs"] = np.ascontiguousarray(inp["x_sample"][c * NS:(c + 1) * NS, 0, :])
        for nm in ("g_ffn1", "w1_gate", "w1_up", "w1_down", "g_ffn2", "w2_gate", "w2_up", "w2_down", "g_mix", "w_in"):
            m[nm] = np.ascontiguousarray(inp[nm][0])
        m["g_q"] = np.ascontiguousarray(inp["g_q"][0].reshape(1, 192))
        m["g_k"] = np.ascontiguousarray(inp["g_k"][0].reshape(1, 192))
        m["ident"] = ident
        m["ssm_a_re"] = np.ascontiguousarray(inp["ssm_a_re"][0].reshape(16, 128))
        m["ssm_a_im"] = np.ascontiguousarray(inp["ssm_a_im"][0].reshape(16, 128))
        m["ssm_ldt"] = np.ascontiguousarray(np.repeat(inp["ssm_log_dt"][0].reshape(16, 2), 64, axis=1))
        for nm in ("Bm_re", "Bm_im", "Cm_re", "Cm_im"):
            m[nm] = ssm_mats[nm]
        m["BmT_re"] = np.ascontiguousarray(ssm_mats["Bm_re"].transpose(0, 2, 1))
        m["BmT_im"] = np.ascontiguousarray(ssm_mats["Bm_im"].transpose(0, 2, 1))
        m["ssm_d"] = np.ascontiguousarray(inp["ssm_d"][0])
        m["w_glu"] = np.ascontiguousarray(inp["w_glu"][0])
        m["b_glu"] = np.ascontiguousarray(inp["b_glu"][0])
        m["tvals"] = tvals
        m["mask_cur"] = mask_cur
        m["mask_prev"] = mask_prev
        for nm in ("w_ssm_proj", "w_attn_proj", "w_o"):
            m[nm] = np.ascontiguousarray(inp[nm][0])
        m["st_re"] = np.ascontiguousarray(inp["state_ssm_re"][0][c * NS:(c + 1) * NS].reshape(NS, 2048))
        m["st_im"] = np.ascontiguousarray(inp["state_ssm_im"][0][c * NS:(c + 1) * NS].reshape(NS, 2048))
        for g, W in enumerate(WINS):
            m["c%d" % W] = np.ascontiguousarray(caches[g][c * NS:(c + 1) * NS])
        in_maps.append(m)
    return in_maps


def kernel(**inputs):
    inp = {k: np.asarray(v) for k, v in inputs.items()}
    kb = _CACHE.get("k")
    if kb is None:
        kb = K()
        kb.build()
        _CACHE["k"] = kb
    in_maps = make_in_maps(inp)
    res = run_bass_kernel_spmd(kb.nc, in_maps, core_ids=list(range(NCORES))).results
    y_p = np.stack([np.concatenate([res[4 * b + q]["yp"] for q in range(4)]) for b in range(2)])
    y_s = np.concatenate([res[c]["ys"] for c in range(NCORES)])[:, None, :]
    outs = [y_p, y_s]
    for W in WINS:
        outs.append(np.stack([res[4 * b + 3]["kvp%d" % W] for b in range(2)]).reshape(1, 2, W, 2, 4, 64))
    outs.append(np.stack([res[4 * b + 3]["stp_re"] for b in range(2)]).reshape(1, 2, 32, 64))
    outs.append(np.stack([res[4 * b + 3]["stp_im"] for b in range(2)]).reshape(1, 2, 32, 64))
    for W in WINS:
        outs.append(np.concatenate([res[c]["kvs%d" % W] for c in range(NCORES)]).reshape(1, 32, W, 2, 4, 64))
    outs.append(np.concatenate([res[c]["sts_re"] for c in range(NCORES)]).reshape(1, 32, 32, 64))
    outs.append(np.concatenate([res[c]["sts_im"] for c in range(NCORES)]).reshape(1, 32, 32, 64))
    return tuple(outs)
```

```python
import numpy as np
from contextlib import ExitStack
import concourse.bass as bass
import concourse.mybir as mybir
from concourse.bass_utils import run_bass_kernel_spmd

F32 = mybir.dt.float32
BF16 = mybir.dt.bfloat16
ALU = mybir.AluOpType
AF = mybir.ActivationFunctionType
AX = mybir.AxisListType

NCORES = 8
D = 1024
DFF = 2816
NF = DFF // 128
SEQ = 8192
NS = 4
ST = 512
QTR = 2048
EPS = 1e-6


class Prog:
    ENG = ["tensor", "vector", "scalar", "gpsimd", "sync"]

    def __init__(self, nc, stack):
        self.nc = nc
        self.stack = stack
        self.q = {e: [] for e in self.ENG}
        self.cnt = {e: 0 for e in self.ENG}
        self.esem = {e: stack.enter_context(nc.semaphore("es_" + e)) for e in self.ENG if e != "sync"}
        self.lastw = {}
        self.readers = {}
        self.waited = {e: {} for e in self.ENG}
        self.dpool = {}
        self.dpi = {}
        for e, n in (("sync", 12), ("gpsimd", 6), ("scalar", 4)):
            self.dpool[e] = [[stack.enter_context(nc.semaphore("ds_%s%d" % (e, i))), 0] for i in range(n)]
            self.dpi[e] = 0
        self.nops = 0

    def _need(self, eng, tk):
        sem, val = tk
        if val <= 0:
            return
        w = self.waited[eng]
        if w.get(id(sem), 0) >= val:
            return
        w[id(sem)] = val
        self.q[eng].append(lambda e, sem=sem, val=val: e.wait_ge(sem, val))

    def _deps(self, eng, reads, writes):
        for k in reads:
            t = self.lastw.get(k)
            if t is not None:
                self._need(eng, t)
        for k in writes:
            t = self.lastw.get(k)
            if t is not None:
                self._need(eng, t)
            for t in self.readers.get(k, ()):
                self._need(eng, t)

    def _record(self, tk, reads, writes):
        for k in reads:
            self.readers.setdefault(k, []).append(tk)
        for k in writes:
            self.lastw[k] = tk
            self.readers[k] = []

    def op(self, eng, fn, reads=(), writes=()):
        self._deps(eng, reads, writes)
        self.cnt[eng] += 1
        v = self.cnt[eng]
        sem = self.esem[eng]
        self.q[eng].append(lambda e, fn=fn, sem=sem: fn(e).then_inc(sem, 1))
        tk = (sem, v)
        if eng == "tensor":
            self.waited[eng][id(sem)] = v
        self._record(tk, reads, writes)
        self.nops += 1
        return tk

    def dma(self, eng, out, in_, reads=(), writes=(), **kw):
        pool = self.dpool[eng]
        i = self.dpi[eng]
        self.dpi[eng] = (i + 1) % len(pool)
        sem, cur = pool[i]
        self._deps(eng, reads, writes)
        self._need(eng, (sem, cur))
        pool[i][1] = cur + 16
        self.q[eng].append(lambda e, out=out, in_=in_, sem=sem, kw=kw: e.dma_start(out=out, in_=in_, **kw).then_inc(sem, 16))
        tk = (sem, cur + 16)
        self._record(tk, reads, writes)
        self.nops += 1
        return tk

    def barrier(self):
        for e in self.ENG:
            for pe in self.dpool:
                for sem, cur in self.dpool[pe]:
                    self._need(e, (sem, cur))
            for ce in self.esem:
                if ce != e:
                    self._need(e, (self.esem[ce], self.cnt[ce]))

    def flush(self):
        nc = self.nc
        q = self.q
        self.q = {e: [] for e in self.ENG}
        with nc.Block() as block:
            @block.tensor
            def _(e):
                for f in q["tensor"]:
                    f(e)

            @block.vector
            def _(e):
                for f in q["vector"]:
                    f(e)

            @block.scalar
            def _(e):
                for f in q["scalar"]:
                    f(e)

            @block.gpsimd
            def _(e):
                for f in q["gpsimd"]:
                    f(e)

            @block.sync
            def _(e):
                for f in q["sync"]:
                    f(e)

    def finish(self):
        self.barrier()
        self.flush()


def tiles_of(total):
    return [(t0, min(ST, total - t0)) for t0 in range(0, total, ST)]


class K:
    def __init__(self, debug=()):
        self.debug = set(debug)
        self.nc = bass.Bass("TRN2", target_bir_lowering=False)
        self.stack = ExitStack()
        self.P = Prog(self.nc, self.stack)
        self.ins = {}
        self.outs = {}
        self._uid = 0

    def din(self, name, shape, dt=F32):
        t = self.nc.dram_tensor(name, list(shape), dt, kind="ExternalInput").ap()
        self.ins[name] = t
        return t

    def dout(self, name, shape, dt=F32):
        t = self.nc.dram_tensor(name, list(shape), dt, kind="ExternalOutput").ap()
        self.outs[name] = t
        return t

    def dscr(self, name, shape, dt=F32):
        kind = "ExternalOutput" if name in self.debug else "Internal"
        t = self.nc.dram_tensor(name, list(shape), dt, kind=kind).ap()
        if name in self.debug:
            self.outs[name] = t
        return t

    def sb(self, name, shape, dt=F32):
        return self.stack.enter_context(self.nc.sbuf_tensor(name, list(shape), dt))

    def ps(self, name, shape, dt=F32):
        return self.stack.enter_context(self.nc.psum_tensor(name, list(shape), dt))

    def sbp(self, name, shape, dt=F32):
        self._uid += 1
        return self.ph.enter_context(self.nc.sbuf_tensor("%s_%d" % (name, self._uid), list(shape), dt))

    def build(self):
        nc, P = self.nc, self.P
        NT = SEQ + NS
        self.NT = NT
        xp = self.din("xp", [SEQ, D])
        xs = self.din("xs", [NS, D])
        w = {}
        for nm, shp in (("g_ffn1", [D]), ("w1_gate", [D, DFF]), ("w1_up", [D, DFF]), ("w1_down", [DFF, D]),
                        ("g_ffn2", [D]), ("w2_gate", [D, DFF]), ("w2_up", [D, DFF]), ("w2_down", [DFF, D]),
                        ("g_mix", [D]), ("w_in", [D, 4864]), ("g_q", [1, 192]), ("g_k", [1, 192]),
                        ("ssm_a_re", [16, 128]), ("ssm_a_im", [16, 128]), ("ssm_ldt", [16, 128]),
                        ("Bm_re", [16, 128, 128]), ("Bm_im", [16, 128, 128]), ("BmT_re", [16, 128, 128]), ("BmT_im", [16, 128, 128]), ("Cm_re", [16, 128, 128]), ("Cm_im", [16, 128, 128]),
                        ("ssm_d", [512]), ("w_glu", [512, 512]), ("b_glu", [512]),
                        ("mask_cur", [128, 256]), ("mask_prev", [128, 256]), ("mask_halo", [128, 256]), ("tvals", [128, ST]),
                        ("w_ssm_proj", [512, D]), ("w_attn_proj", [256, D]), ("w_o", [D, D])):
            w[nm] = self.din(nm, shp)
        self.w = w
        ident = self.din("ident", [128, 128])
        w["ident"] = ident
        self.cache = [self.din("c%d" % W, [NS, W, 512]) for W in (128, 512, 2048)]
        yp = self.dout("yp", [QTR, D])
        ys = self.dout("ys", [NS, D])
        self.kvp = [self.dout("kvp%d" % W, [W, 512]) for W in (128, 512, 2048)]
        self.kvs = [self.dout("kvs%d" % W, [NS, W, 512]) for W in (128, 512, 2048)]
        self.st_in = [self.din("st_re", [NS, 2048]), self.din("st_im", [NS, 2048])]
        self.st_out_p = [self.dout("stp_re", [16, 128]), self.dout("stp_im", [16, 128])]
        self.st_out_s = [self.dout("sts_re", [NS, 2048]), self.dout("sts_im", [NS, 2048])]
        self.yssT_s = self.dscr("yssT_s", [4, 128, NT], BF16)
        self.yatT_s = self.dscr("yatT_s", [2, 128, NT], BF16)
        self.utok_s = self.dscr("utok_s", [SEQ, 512], BF16)
        self.Etab_s = [self.dscr("Etab_s%d" % j, [16, 128, ST]) for j in range(2)]
        self.FEtab_s = [self.dscr("FEtab_s%d" % j, [16, 128, ST]) for j in range(2)]
        self.car = [self.sb("car_re", [128, 16], F32), self.sb("car_im", [128, 16], F32)]
        x1T = self.dscr("x1T", [8, 128, NT])
        self.x1T = x1T
        self.hT_s = self.dscr("hT_s", [8, 128, NT], BF16)
        self.uT_s = self.dscr("uT_s", [4, 128, NT], BF16)
        self.kv_s = [self.dscr("kv_s%d" % g, [NT, 512]) for g in range(3)]
        self.q_s = [self.dscr("q_s%d" % g, [NT, 256]) for g in range(3)]

        self.identF = self.sb("identF", [128, 128], F32)
        P.dma("sync", self.identF[:], ident[:, :], writes=["identF"])
        self.onesB = self.sb("onesB", [128, 128], BF16)
        P.op("vector", lambda e: e.memset(self.onesB[:], 1.0), writes=["onesB"])
        self.epsC = self.sb("epsC", [128, 1], F32)
        P.op("vector", lambda e: e.memset(self.epsC[:], EPS), writes=["epsC"])
        self.bank = [self.ps("bank%d" % i, [128, 512], F32) for i in range(8)]
        self.srcs = [("p", t0, n) for (t0, n) in tiles_of(SEQ)] + [("s", 0, NS)]
        self.own0 = SEQ - QTR
        self.own_tis = [ti for ti, (kind, t0, n) in enumerate(self.srcs) if kind == "s" or t0 >= self.own0]

        for g, W in enumerate((128, 512, 2048)):
            for b in range(NS):
                P.dma("sync", self.kvs[g][b, 0:W - 1, :], self.cache[g][b, 1:W, :], writes=[("kvs", g, b)])

        with ExitStack() as ph:
            self.ph = ph
            self.alloc_ffn()
            self.ffn_phase(1, w["g_ffn1"], w["w1_gate"], w["w1_up"], w["w1_down"], self.srcs,
                           src_tok=(xp, xs), src_T=None, dst_T=x1T, dst_tok=None)
            P.barrier()
            P.flush()
        with ExitStack() as ph:
            self.ph = ph
            self.phase_b1()
            P.barrier()
            P.flush()
        with ExitStack() as ph:
            self.ph = ph
            self.phase_b2p()
            P.barrier()
            P.flush()
        with ExitStack() as ph:
            self.ph = ph
            self.phase_b2()
            P.barrier()
            P.flush()
        with ExitStack() as ph:
            self.ph = ph
            self.phase_b3()
            P.barrier()
            P.flush()
        with ExitStack() as ph:
            self.ph = ph
            self.phase_b4()
            P.barrier()
            P.flush()
        with ExitStack() as ph:
            self.ph = ph
            self.alloc_ffn()
            self.ffn_phase(2, w["g_ffn2"], w["w2_gate"], w["w2_up"], w["w2_down"], self.srcs,
                           src_tok=None, src_T=x1T, dst_T=None, dst_tok=(yp, ys), only=self.own_tis)
            P.finish()
        return nc

    def alloc_ffn(self):
        self.wg = self.sbp("wg", [128, 8, DFF], BF16)
        self.wu = self.sbp("wu", [128, 8, DFF], BF16)
        self.wd = self.sbp("wd", [128, NF, D], BF16)
        self.gcol = self.sbp("gcol", [128, 8], F32)
        self.xin = [self.sbp("xin%d" % i, [128, D], F32) for i in range(2)]
        self.xT = self.sbp("xT", [128, 8, ST], F32)
        self.sq = self.sbp("sq", [128, 8, ST], BF16)
        self.hT = self.sbp("hT", [128, 8, ST], BF16)
        self.rstd = self.sbp("rstd", [128, ST], F32)
        self.aT = self.sbp("aT", [128, NF, ST], BF16)
        self.sil = [self.sbp("sil%d" % i, [128, ST], F32) for i in range(2)]

    def rmsnorm(self, n, nb=6):
        P = self.P
        xT, sq, hT, rstd, bank = self.xT, self.sq, self.hT, self.rstd, self.bank
        for k in range(8):
            P.op("scalar", lambda e, k=k: e.activation(sq[:, k, :n], xT[:, k, :n], AF.Square),
                 reads=[("xT", k)], writes=[("sq", k)])

        def nrm(e):
            last = None
            for k in range(8):
                last = e.matmul(bank[nb][:, :n], self.onesB[:], sq[:, k, :n], start=(k == 0), stop=(k == 7))
            return last
        P.op("tensor", nrm, reads=[("sq", k) for k in range(8)] + ["onesB"], writes=[("bank", nb)])
        P.op("scalar", lambda e: e.activation(rstd[:, :n], bank[nb][:, :n], AF.Sqrt, bias=self.epsC[:], scale=1.0 / D),
             reads=[("bank", nb), "epsC"], writes=["rstd"])
        P.op("vector", lambda e: e.reciprocal(rstd[:, :n], rstd[:, :n]), reads=["rstd"], writes=["rstd"])
        for k in range(8):
            P.op("vector", lambda e, k=k: e.scalar_tensor_tensor(
                hT[:, k, :n], xT[:, k, :n], self.gcol[:, k:k + 1], rstd[:, :n], ALU.mult, ALU.mult),
                reads=[("xT", k), "gcol", "rstd"], writes=[("hT", k)])

    def phase_b1(self):
        P, nc, w = self.P, self.nc, self.w
        NW = 2816
        self.winA = self.sbp("winA", [128, 8, NW], BF16)
        for k in range(8):
            P.dma("gpsimd", self.winA[:, k, :], w["w_in"][k * 128:(k + 1) * 128, 0:NW], writes=[("winA", k)], max_dma_last_dim=4096)
        self.gcol = self.sbp("gcol", [128, 8], F32)
        P.dma("sync", self.gcol[:], w["g_mix"].rearrange("(k p) -> p k", p=128), writes=["gcol"], allow_slow_non_contiguous=True)
        self.gqb = self.sbp("gqb", [128, 192], F32)
        self.gkb = self.sbp("gkb", [128, 192], F32)
        P.dma("sync", self.gqb[:], w["g_q"].partition_broadcast(128), writes=["gqb"])
        P.dma("sync", self.gkb[:], w["g_k"].partition_broadcast(128), writes=["gkb"])
        P.op("vector", lambda e: e.tensor_scalar(self.gqb[:], self.gqb[:], 0.125, None, ALU.mult), reads=["gqb"], writes=["gqb"])
        self.xT = self.sbp("xT", [128, 8, ST], F32)
        self.sq = self.sbp("sq", [128, 8, ST], BF16)
        self.hT = self.sbp("hT", [128, 8, ST], BF16)
        self.rstd = self.sbp("rstd", [128, ST], F32)
        self.uTt = self.sbp("uTt", [128, 4, ST], BF16)
        self.utkb = [self.sbp("utkb%d" % i, [128, 512], BF16) for i in range(2)]
        self.sqq = [self.sbp("sqq%d" % i, [128, 512], F32) for i in range(3)]
        self.ssum = [self.sbp("ssum%d" % i, [128, 8], F32) for i in range(3)]
        self.kvt = [self.sbp("kvt%d" % i, [128, 512], F32) for i in range(3)]
        self.qt = [self.sbp("qt%d" % i, [128, 256], F32) for i in range(3)]
        self.unit = 0
        for ti, (kind, t0, n) in enumerate(self.srcs):
            self._b1_tile(ti, kind, t0, n)
        allkv = lambda g: [("kv_s", g, ti) for ti in range(len(self.srcs))]
        for g, W in enumerate((128, 512, 2048)):
            P.dma("sync", self.kvp[g][:, :], self.kv_s[g][SEQ - W:SEQ, :], reads=allkv(g), writes=[("kvp", g)])
            for b in range(NS):
                P.dma("sync", self.kvs[g][b, W - 1:W, :], self.kv_s[g][SEQ + b:SEQ + b + 1, :], reads=allkv(g), writes=[("kvs", g, b)])

    def _b1_tile(self, ti, kind, t0, n):
        P = self.P
        xT, hT, bank, winA = self.xT, self.hT, self.bank, self.winA
        gt0 = t0 if kind == "p" else SEQ
        nsub = (n + 127) // 128
        P.dma("sync", xT[:, :, :n], self.x1T[:, :, gt0:gt0 + n].rearrange("k p n -> p k n"),
              reads=[("x1T", ti)], writes=[("xT", k) for k in range(8)])
        self.rmsnorm(n, nb=0)
        hk = [("hT", k) for k in range(8)]
        wk = [("winA", k) for k in range(8)]
        need_h = ti in self.own_tis
        need_kv = kind == "s" or t0 >= self.own0 - QTR
        if need_h:
            P.dma("sync", self.hT_s[:, :, gt0:gt0 + n].rearrange("k p n -> p k n"), hT[:, :, :n], reads=hk, writes=[("hT_s", ti)])
        if not need_h:
            for s in range(nsub):
                bu = bank[s % 2]

                def mmut(e, s=s, bu=bu):
                    last = None
                    for k in range(8):
                        last = e.matmul(bu[:, 0:512], hT[:, k, s * 128:(s + 1) * 128], winA[:, k, 0:512], start=(k == 0), stop=(k == 7))
                    return last
                P.op("tensor", mmut, reads=hk + wk, writes=[("bank", s % 2)])
                ub = self.utkb[s % 2]
                P.op("scalar", lambda e, bu=bu, ub=ub: e.copy(ub[:], bu[:, 0:512]), reads=[("bank", s % 2)], writes=[("utkb", s % 2)])
                P.dma("sync", self.utok_s[t0 + s * 128:t0 + (s + 1) * 128, :], ub[:], reads=[("utkb", s % 2)], writes=[("utok_s", ti)])
        for m in (range(4) if need_h else ()):
            bu = bank[m % 2]

            def mmu(e, m=m, bu=bu):
                last = None
                for k in range(8):
                    last = e.matmul(bu[:, :n], winA[:, k, m * 128:(m + 1) * 128], hT[:, k, :n], start=(k == 0), stop=(k == 7))
                return last
            P.op("tensor", mmu, reads=hk + wk, writes=[("bank", m % 2)])
            P.op("scalar", lambda e, m=m, bu=bu: e.copy(self.uTt[:, m, :n], bu[:, :n]), reads=[("bank", m % 2)], writes=[("uTt", m)])
        if need_h:
            P.dma("sync", self.uT_s[:, :, gt0:gt0 + n].rearrange("k p n -> p k n"), self.uTt[:, :, :n],
                  reads=[("uTt", m) for m in range(4)], writes=[("uT_s", ti)])
        if not need_kv:
            return
        for s in range(nsub):
            r = min(128, n - s * 128)
            for g in range(3):
                self._b1_qkv(ti, gt0 + s * 128, s, r, g)

    def _b1_qkv(self, ti, row0, s, r, g):
        P = self.P
        hT, bank, winA = self.hT, self.bank, self.winA
        u = self.unit
        self.unit += 1
        bA = bank[2 + (u % 3)]
        bB = bank[5 + (u % 3)]
        kA, kB = ("bank", 2 + u % 3), ("bank", 5 + u % 3)
        sqq, ssum, kvt, qt = self.sqq[u % 3], self.ssum[u % 3], self.kvt[u % 3], self.qt[u % 3]
        ks = lambda nm: (nm, u % 3)
        hk = [("hT", k) for k in range(8)]
        wk = [("winA", k) for k in range(8)]

        def mmkv(e):
            last = None
            for part, c0 in ((0, 1280 + 256 * g), (1, 2048 + 256 * g)):
                for k in range(8):
                    last = e.matmul(bA[:r, part * 256:(part + 1) * 256], hT[:, k, s * 128:s * 128 + r], winA[:, k, c0:c0 + 256],
                                    start=(k == 0), stop=(k == 7))
            return last

        def mmq(e):
            last = None
            c0 = 512 + 256 * g
            for k in range(8):
                last = e.matmul(bB[:r, 0:256], hT[:, k, s * 128:s * 128 + r], winA[:, k, c0:c0 + 256], start=(k == 0), stop=(k == 7))
            return last
        P.op("tensor", mmkv, reads=hk + wk, writes=[kA])
        P.op("tensor", mmq, reads=hk + wk, writes=[kB])
        P.op("scalar", lambda e: e.activation(sqq[:r, 0:256], bB[:r, 0:256], AF.Square), reads=[kB], writes=[ks("sqq")])
        P.op("scalar", lambda e: e.activation(sqq[:r, 256:512], bA[:r, 0:256], AF.Square), reads=[kA], writes=[ks("sqq")])
        P.op("vector", lambda e: e.tensor_reduce(ssum[:r, :], sqq[:r, :].rearrange("p (h d) -> p h d", d=64), AX.X, ALU.add),
             reads=[ks("sqq")], writes=[ks("ssum")])
        P.op("scalar", lambda e: e.activation(ssum[:r, :], ssum[:r, :], AF.Sqrt, bias=self.epsC[:r, :], scale=1.0 / 64),
             reads=[ks("ssum"), "epsC"], writes=[ks("ssum")])
        P.op("vector", lambda e: e.reciprocal(ssum[:r, :], ssum[:r, :]), reads=[ks("ssum")], writes=[ks("ssum")])
        v3 = lambda ap: ap.rearrange("p (h d) -> p h d", d=64)
        P.op("vector", lambda e: e.tensor_tensor(v3(kvt[:r, 0:256]), v3(bA[:r, 0:256]),
                                                 ssum[:r, 4:8].unsqueeze(2).broadcast_to([r, 4, 64]), ALU.mult),
             reads=[kA, ks("ssum")], writes=[ks("kvt")])
        P.op("vector", lambda e: e.tensor_tensor(v3(kvt[:r, 0:256]), v3(kvt[:r, 0:256]),
                                                 self.gkb[:r, g * 64:(g + 1) * 64].unsqueeze(1).broadcast_to([r, 4, 64]), ALU.mult),
             reads=[ks("kvt"), "gkb"], writes=[ks("kvt")])
        P.op("scalar", lambda e: e.copy(kvt[:r, 256:512], bA[:r, 256:512]), reads=[kA], writes=[ks("kvt")])
        P.op("vector", lambda e: e.tensor_tensor(v3(qt[:r, :]), v3(bB[:r, 0:256]),
                                                 ssum[:r, 0:4].unsqueeze(2).broadcast_to([r, 4, 64]), ALU.mult),
             reads=[kB, ks("ssum")], writes=[ks("qt")])
        P.op("vector", lambda e: e.tensor_tensor(v3(qt[:r, :]), v3(qt[:r, :]),
                                                 self.gqb[:r, g * 64:(g + 1) * 64].unsqueeze(1).broadcast_to([r, 4, 64]), ALU.mult),
             reads=[ks("qt"), "gqb"], writes=[ks("qt")])
        P.dma("sync", self.kv_s[g][row0:row0 + r, :], kvt[:r, :], reads=[ks("kvt")], writes=[("kv_s", g, ti)])
        P.dma("sync", self.q_s[g][row0:row0 + r, :], qt[:r, :], reads=[ks("qt")], writes=[("q_s", g, ti)])

    def _ssm_setup(self):
        P, nc, w = self.P, self.nc, self.w
        TS = 16
        T = {}

        def tl(nm, shape=(128, TS), dt=F32):
            T[nm] = self.sbp(nm, list(shape), dt)
            return T[nm]
        for nm in ("a_re", "a_im", "ldt"):
            tl(nm)
            P.dma("sync", T[nm][:], w["ssm_" + nm].rearrange("s q -> q s"), writes=[nm], allow_slow_non_contiguous=True)
        V = lambda fn, r, wr: P.op("vector", fn, reads=r, writes=wr)
        A = lambda fn, r, wr: P.op("scalar", fn, reads=r, writes=wr)
        tt = lambda o, a, b, op: V(lambda e: e.tensor_tensor(T[o][:], T[a][:], T[b][:], op), [a, b], [o])
        for nm in ("dt", "ar", "ai", "mag", "rs", "rc", "sn", "cs", "abr", "abi", "nabi", "sq1", "sq2", "inv", "em1", "t1", "t2", "f_re", "f_im"):
            tl(nm)
        A(lambda e: e.activation(T["dt"][:], T["ldt"][:], AF.Exp), ["ldt"], ["dt"])
        tt("ar", "a_re", "dt", ALU.mult)
        tt("ai", "a_im", "dt", ALU.mult)
        A(lambda e: e.activation(T["mag"][:], T["ar"][:], AF.Exp), ["ar"], ["mag"])
        PI = float(np.pi)
        tl("rtmp")
        T["rint"] = self.sbp("rint", [128, TS], mybir.dt.int32)
        tl("rmask")

        def reduce_generic(dst_ap, src_ap, off, tmp_ap, int_ap, mask_ap, kd):
            V(lambda e: e.tensor_scalar(dst_ap, src_ap, off, None, ALU.add), [kd, "ai", "mm0"], [kd])
            V(lambda e: e.tensor_scalar(tmp_ap, dst_ap, 1.0 / (2 * PI), None, ALU.mult), [kd], ["mm1"])
            V(lambda e: e.tensor_copy(int_ap, tmp_ap), ["mm1"], ["g_int"])
            V(lambda e: e.tensor_copy(tmp_ap, int_ap), ["g_int"], ["mm1"])
            V(lambda e: e.scalar_tensor_tensor(dst_ap, tmp_ap, -2 * PI, dst_ap, ALU.mult, ALU.add), ["mm1", kd], [kd])
            V(lambda e: e.tensor_scalar(mask_ap, dst_ap, PI, None, ALU.is_gt), [kd], ["mm2"])
            V(lambda e: e.scalar_tensor_tensor(dst_ap, mask_ap, -2 * PI, dst_ap, ALU.mult, ALU.add), ["mm2", kd], [kd])
            V(lambda e: e.tensor_scalar(mask_ap, dst_ap, -PI, None, ALU.is_lt), [kd], ["mm2"])
            V(lambda e: e.scalar_tensor_tensor(dst_ap, mask_ap, 2 * PI, dst_ap, ALU.mult, ALU.add), ["mm2", kd], [kd])
            V(lambda e: e.tensor_scalar(dst_ap, dst_ap, PI, -PI, ALU.min, ALU.max), [kd], [kd])

        def reduce_angle(dst, off):
            reduce_generic(T[dst][:], T["ai"][:], off, T["rtmp"][:], T["rint"][:], T["rmask"][:], dst)
        reduce_angle("rs", 0.0)
        reduce_angle("rc", 0.5 * PI)
        A(lambda e: e.activation(T["sn"][:], T["rs"][:], AF.Sin), ["rs"], ["sn"])
        A(lambda e: e.activation(T["cs"][:], T["rc"][:], AF.Sin), ["rc"], ["cs"])
        tt("abr", "mag", "cs", ALU.mult)
        tt("abi", "mag", "sn", ALU.mult)
        tt("sq1", "a_re", "a_re", ALU.mult)
        tt("sq2", "a_im", "a_im", ALU.mult)
        tt("inv", "sq1", "sq2", ALU.add)
        V(lambda e: e.reciprocal(T["inv"][:], T["inv"][:]), ["inv"], ["inv"])
        V(lambda e: e.tensor_scalar(T["em1"][:], T["abr"][:], -1.0, None, ALU.add), ["abr"], ["em1"])
        tt("t1", "em1", "a_re", ALU.mult)
        tt("t2", "abi", "a_im", ALU.mult)
        tt("t1", "t1", "t2", ALU.add)
        tt("f_re", "t1", "inv", ALU.mult)
        tt("t1", "abi", "a_re", ALU.mult)
        tt("t2", "em1", "a_im", ALU.mult)
        tt("t1", "t1", "t2", ALU.subtract)
        tt("f_im", "t1", "inv", ALU.mult)
        return T, tl, reduce_generic

    def phase_b2p(self):
        P, w, bank = self.P, self.w, self.bank
        TS, n = 16, ST
        T, tl, reduce_generic = self._ssm_setup()
        PI = float(np.pi)
        V = lambda fn, r, wr: P.op("vector", fn, reads=r, writes=wr)
        A = lambda fn, r, wr: P.op("scalar", fn, reads=r, writes=wr)
        G = lambda fn, r, wr: P.op("gpsimd", fn, reads=r, writes=wr)
        car = self.car
        V(lambda e: e.memset(car[0][:], 0.0), [], ["car"])
        V(lambda e: e.memset(car[1][:], 0.0), [], ["car"])
        tv, rev = tl("tvals", (128, ST)), tl("rev", (128, ST))
        P.dma("sync", tv[:], w["tvals"][:, :], writes=["tvals"])
        V(lambda e: e.tensor_scalar(rev[:], tv[:], -1.0, float(ST), ALU.mult, ALU.add), ["tvals"], ["rev"])
        Et = [tl("Et_re", (128, ST)), tl("Et_im", (128, ST))]
        FEt = [tl("FEt_re", (128, ST)), tl("FEt_im", (128, ST))]
        ang, g_tmp, g_msk = tl("angp", (128, ST)), tl("g_tmpp", (128, ST)), tl("g_mskp", (128, ST))
        g_int = self.sbp("g_intp", [128, ST], mybir.dt.int32)
        En = [tl("En_re"), tl("En_im")]
        Kc = [tl("Kc_re"), tl("Kc_im")]
        kt = tl("kt")
        Rt, Yr, Yi, W0, W1, t1 = [tl(nm, (128, ST)) for nm in ("Rt", "Yr", "Yi", "W0", "W1", "wt1")]
        Wt = [tl("Wt_re", (128, 4, TS, 128), BF16), tl("Wt_im", (128, 4, TS, 128), BF16)]
        BmT = [tl("BmT_re", (128, TS, 128), BF16), tl("BmT_im", (128, TS, 128), BF16)]
        for t_, nm in ((BmT[0], "BmT_re"), (BmT[1], "BmT_im")):
            P.dma("gpsimd", t_[:], w[nm].rearrange("s r c -> r s c"), writes=[nm])
        for s_ in range(TS):
            V(lambda e, s_=s_: e.tensor_scalar(ang[:], tv[:], T["ai"][:, s_:s_ + 1], None, ALU.mult), ["tvals", "ai"], ["mm0"])
            for j, off in ((1, 0.0), (0, 0.5 * PI)):
                reduce_generic(Et[j][:], ang[:], off, g_tmp[:], g_int[:], g_msk[:], ("Et", j))
                A(lambda e, j=j: e.activation(Et[j][:], Et[j][:], AF.Sin), [("Et", j)], [("Et", j)])
            for j in range(2):
                V(lambda e, j=j, s_=s_: e.tensor_copy(En[j][:, s_:s_ + 1], Et[j][:, ST - 1:ST]), [("Et", j)], ["En"])
            fr, fi = T["f_re"][:, s_:s_ + 1], T["f_im"][:, s_:s_ + 1]
            for j in range(2):
                P.dma("sync", self.Etab_s[j][s_], Et[j][:], reads=[("Et", j)], writes=[("Etab", j, s_)])
            V(lambda e, fi=fi: e.tensor_scalar(g_tmp[:], Et[1][:], fi, None, ALU.mult), [("Et", 1), "f_im"], ["mm1"])
            V(lambda e, fr=fr: e.scalar_tensor_tensor(FEt[0][:], Et[0][:], fr, g_tmp[:], ALU.mult, ALU.add), [("Et", 0), "f_re", "mm1"], [("FEt", 0)])
            V(lambda e, fr=fr: e.tensor_scalar(g_tmp[:], Et[1][:], fr, None, ALU.mult), [("Et", 1), "f_re"], ["mm1"])
            V(lambda e, fi=fi: e.scalar_tensor_tensor(FEt[1][:], Et[0][:], fi, g_tmp[:], ALU.mult, ALU.subtract), [("Et", 0), "f_im", "mm1"], [("FEt", 1)])
            for j in range(2):
                P.dma("sync", self.FEtab_s[j][s_], FEt[j][:], reads=[("FEt", j)], writes=[("FEtab", j, s_)])
            er, ei = En[0][:, s_:s_ + 1], En[1][:, s_:s_ + 1]
            kr, ki = Kc[0][:, s_:s_ + 1], Kc[1][:, s_:s_ + 1]
            V(lambda e, s_=s_, fi=fi, ei=ei: e.tensor_tensor(kt[:, 0:1], fi, ei, ALU.mult), ["f_im", "En"], ["kt"])
            V(lambda e, fr=fr, er=er, kr=kr: e.scalar_tensor_tensor(kr, er, fr, kt[:, 0:1], ALU.mult, ALU.subtract), ["f_re", "En", "kt"], ["Kc"])
            V(lambda e, s_=s_, fi=fi, er=er: e.tensor_tensor(kt[:, 0:1], fi, er, ALU.mult), ["f_im", "En"], ["kt"])
            V(lambda e, fr=fr, ei=ei, ki=ki: e.scalar_tensor_tensor(ki, ei, fr, kt[:, 0:1], ALU.mult, ALU.add), ["f_re", "En", "kt"], ["Kc"])
            V(lambda e, s_=s_: e.tensor_scalar(Rt[:], rev[:], T["ar"][:, s_:s_ + 1], None, ALU.mult), ["rev", "ar"], ["Rt"])
            A(lambda e: e.activation(Rt[:], Rt[:], AF.Exp), ["Rt"], ["Rt"])
            V(lambda e: e.tensor_tensor(Yr[:], Rt[:], Et[0][:], ALU.mult), ["Rt", ("Et", 0)], ["Yr"])
            G(lambda e: e.tensor_tensor(Yi[:], Rt[:], Et[1][:], ALU.mult), ["Rt", ("Et", 1)], ["Yi"])
            V(lambda e, ki=ki: e.tensor_scalar(t1[:], Yi[:], ki, None, ALU.mult), ["Yi", "Kc"], ["wt1"])
            V(lambda e, kr=kr: e.scalar_tensor_tensor(W0[:], Yr[:], kr, t1[:], ALU.mult, ALU.add), ["Yr", "Kc", "wt1"], ["W0"])
            V(lambda e, kr=kr: e.tensor_scalar(t1[:], Yi[:], kr, None, ALU.mult), ["Yi", "Kc"], ["wt1"])
            V(lambda e, ki=ki: e.scalar_tensor_tensor(W1[:], Yr[:], ki, t1[:], ALU.mult, ALU.subtract), ["Yr", "Kc", "wt1"], ["W1"])
            for j, (Wsrc, kW) in enumerate(((W0, "W0"), (W1, "W1"))):
                tb = bank[6 + j]
                for tc in range(4):
                    P.op("tensor", lambda e, tb=tb, tc=tc, Wsrc=Wsrc: e.transpose(tb[:, tc * 128:(tc + 1) * 128], Wsrc[:, tc * 128:(tc + 1) * 128], self.identF[:, :]),
                         reads=[kW, "identF"], writes=[("bank", 6 + j)])
                eng = "scalar" if j == 0 else "vector"
                P.op(eng, (lambda e, j=j, s_=s_, tb=tb: e.copy(Wt[j][:, :, s_, :], tb[:, :].rearrange("p (a n) -> p a n", a=4))) if j == 0 else
                     (lambda e, j=j, s_=s_, tb=tb: e.tensor_copy(Wt[j][:, :, s_, :], tb[:, :].rearrange("p (a n) -> p a n", a=4))),
                     reads=[("bank", 6 + j)], writes=["Wt"])
        rho_n, An = tl("rho_n"), [tl("An_re"), tl("An_im")]
        A(lambda e: e.activation(rho_n[:], T["ar"][:], AF.Exp, scale=float(ST)), ["ar"], ["rho_n"])
        for j in range(2):
            V(lambda e, j=j: e.tensor_tensor(An[j][:], rho_n[:], En[j][:], ALU.mult), ["rho_n", "En"], ["An"])
        utk = [tl("utk%d" % i, (128, 4, 512), BF16) for i in range(2)]
        junk = tl("junk", (128, 128))
        Dt = [[tl("D%d%d" % (j, k)) for k in range(2)] for j in range(2)]
        nr, ni, tt_ = tl("nr"), tl("ni"), tl("ttt")
        cnt = 0
        for ti, (kind, t0, n_) in enumerate(self.srcs):
            if kind != "p" or t0 >= self.own0:
                continue
            uk = utk[cnt % 2]
            kuk = ("utk", cnt % 2)
            cnt += 1
            P.dma("sync", uk[:], self.utok_s[t0:t0 + ST, :].rearrange("(a t) c -> t a c", t=128), reads=[("utok_s", ti)], writes=[kuk])
            for s_ in range(TS):
                c = s_ // 4
                bk = bank[s_ % 4]
                kb = ("bank", s_ % 4)

                def mmM(e, s_=s_, c=c, bk=bk, uk=uk):
                    last = None
                    for j in range(2):
                        for tc in range(4):
                            last = e.matmul(bk[:, j * 128:(j + 1) * 128], Wt[j][:, tc, s_, :], uk[:, tc, c * 128:(c + 1) * 128],
                                            start=(tc == 0), stop=(tc == 3))
                    return last
                P.op("tensor", mmM, reads=["Wt", kuk], writes=[kb])
                for j in range(2):
                    for k in range(2):
                        V(lambda e, j=j, k=k, s_=s_, bk=bk: e.scalar_tensor_tensor(junk[:], bk[:, j * 128:(j + 1) * 128], 1.0, BmT[k][:, s_, :],
                                                                                   ALU.mult, ALU.mult, accum_out=Dt[j][k][:, s_:s_ + 1]),
                          [kb, "BmT_re", "BmT_im", "junk"], ["junk", ("D", j, k)])
            dk = [("D", j, k) for j in range(2) for k in range(2)]
            V(lambda e: e.tensor_tensor(nr[:], An[0][:], car[0][:], ALU.mult), ["An", "car"], ["nr"])
            V(lambda e: e.tensor_tensor(tt_[:], An[1][:], car[1][:], ALU.mult), ["An", "car"], ["ttt"])
            V(lambda e: e.tensor_tensor(nr[:], nr[:], tt_[:], ALU.subtract), ["nr", "ttt"], ["nr"])
            V(lambda e: e.tensor_tensor(nr[:], nr[:], Dt[0][0][:], ALU.add), ["nr"] + dk, ["nr"])
            V(lambda e: e.tensor_tensor(nr[:], nr[:], Dt[1][1][:], ALU.subtract), ["nr"] + dk, ["nr"])
            V(lambda e: e.tensor_tensor(ni[:], An[0][:], car[1][:], ALU.mult), ["An", "car"], ["ni"])
            V(lambda e: e.tensor_tensor(tt_[:], An[1][:], car[0][:], ALU.mult), ["An", "car", "nr"], ["ttt"])
            V(lambda e: e.tensor_tensor(ni[:], ni[:], tt_[:], ALU.add), ["ni", "ttt"], ["ni"])
            V(lambda e: e.tensor_tensor(ni[:], ni[:], Dt[0][1][:], ALU.add), ["ni"] + dk, ["ni"])
            V(lambda e: e.tensor_tensor(ni[:], ni[:], Dt[1][0][:], ALU.add), ["ni"] + dk, ["ni"])
            V(lambda e: e.tensor_copy(car[0][:], nr[:]), ["nr", "car"], ["car"])
            V(lambda e: e.tensor_copy(car[1][:], ni[:]), ["ni", "car"], ["car"])

    def phase_b2(self):
        P, nc, w = self.P, self.nc, self.w
        TS = 16
        T, tl, reduce_generic = self._ssm_setup()
        PI = float(np.pi)
        V = lambda fn, r, wr: P.op("vector", fn, reads=r, writes=wr)
        A = lambda fn, r, wr: P.op("scalar", fn, reads=r, writes=wr)
        tv = tl("tvals", (128, ST))
        P.dma("sync", tv[:], w["tvals"][:, :], writes=["tvals"])
        E = [tl("E_re", (128, TS, ST)), tl("E_im", (128, TS, ST))]
        FE = [tl("FE_re", (128, TS, ST)), tl("FE_im", (128, TS, ST))]
        mm_ = [tl("mm%d" % i, (128, ST)) for i in range(4)]
        ang, g_tmp, g_msk = mm_[0], mm_[1], mm_[2]
        g_int = self.sbp("g_int", [128, ST], mybir.dt.int32)
        for s_ in range(TS):
            for j in range(2):
                P.dma("sync", E[j][:, s_, :], self.Etab_s[j][s_], writes=[("E", j, s_)])
                P.dma("sync", FE[j][:, s_, :], self.FEtab_s[j][s_], writes=["FE"])
        zs = [tl("zs0", (128, ST)), tl("zs1", (128, ST))]
        NL = 1
        PR, PIm, NPI = tl("PR", (128, NL, TS)), tl("PIm", (128, NL, TS)), tl("NPI", (128, NL, TS))
        V(lambda e: e.tensor_copy(PR[:, 0, :], T["abr"][:]), ["abr"], ["PR"])
        V(lambda e: e.tensor_copy(PIm[:, 0, :], T["abi"][:]), ["abi"], ["PIm"])
        for k in range(1, NL):
            V(lambda e, k=k: e.tensor_tensor(T["t1"][:], PR[:, k - 1, :], PR[:, k - 1, :], ALU.mult), ["PR"], ["t1"])
            V(lambda e, k=k: e.tensor_tensor(T["t2"][:], PIm[:, k - 1, :], PIm[:, k - 1, :], ALU.mult), ["PIm"], ["t2"])
            V(lambda e, k=k: e.tensor_tensor(PIm[:, k, :], PR[:, k - 1, :], PIm[:, k - 1, :], ALU.mult), ["PR", "PIm"], ["PIm"])
            V(lambda e, k=k: e.tensor_scalar(PIm[:, k, :], PIm[:, k, :], 2.0, None, ALU.mult), ["PIm"], ["PIm"])
            V(lambda e, k=k: e.tensor_tensor(PR[:, k, :], T["t1"][:], T["t2"][:], ALU.subtract), ["t1", "t2"], ["PR"])
        V(lambda e: e.tensor_scalar(NPI[:], PIm[:], -1.0, None, ALU.mult), ["PIm"], ["NPI"])
        Bm = [tl("Bm_re", (128, TS, 128), BF16), tl("Bm_im", (128, TS, 128), BF16)]
        Cm = [tl("Cm_re", (128, TS, 128), BF16), tl("Cm_im", (128, TS, 128), BF16)]
        for t_, nm in ((Bm[0], "Bm_re"), (Bm[1], "Bm_im"), (Cm[0], "Cm_re"), (Cm[1], "Cm_im")):
            P.dma("gpsimd", t_[:], w[nm].rearrange("s r c -> r s c"), writes=[nm])
        wglu = tl("wglu", (128, 4, 512), BF16)
        P.dma("gpsimd", wglu[:], w["w_glu"].rearrange("(k p) f -> p k f", p=128), writes=["wglu"])
        dcol, bcol = tl("dcol", (128, 4)), tl("bcol", (128, 4))
        P.dma("sync", dcol[:], w["ssm_d"].rearrange("(c p) -> p c", p=128), writes=["dcol"], allow_slow_non_contiguous=True)
        P.dma("sync", bcol[:], w["b_glu"].rearrange("(c p) -> p c", p=128), writes=["bcol"], allow_slow_non_contiguous=True)
        car = self.car
        h0s = [tl("h0s_re", (128, NS, TS)), tl("h0s_im", (128, NS, TS))]
        for j in range(2):
            for b in range(NS):
                P.dma("sync", h0s[j][:, b, :], self.st_in[j][b].rearrange("(s q) -> q s", q=128), writes=["h0s"], allow_slow_non_contiguous=True)
        sts = [tl("sts_re", (128, NS, TS)), tl("sts_im", (128, NS, TS))]
        uT = tl("uT", (128, 4, ST), BF16)
        pp = [[tl("pp%d%d" % (i, j), (128, ST)) for j in range(2)] for i in range(2)]
        tmp = [mm_[2], mm_[3]]
        sbr, sbi = tl("sbr", (128, ST), BF16), tl("sbi", (128, ST), BF16)
        yraw, x2, tg = tl("yraw", (128, ST)), tl("x2", (128, ST)), tl("tg", (128, ST))
        ygf, ygb, yss = tl("ygf", (128, 4, ST)), tl("ygb", (128, 4, ST), BF16), tl("yss", (128, 4, ST), BF16)
        bank = self.bank
        self._b2 = dict(E=E, FE=FE, zs=zs, mm=mm_, T=T, PR=PR, PIm=PIm, NPI=NPI, Bm=Bm, Cm=Cm, wglu=wglu, dcol=dcol, bcol=bcol, car=car, h0s=h0s, sts=sts,
                        uT=uT, pp=pp, tmp=tmp, sbr=sbr, sbi=sbi, yraw=yraw, x2=x2, tg=tg, ygf=ygf, ygb=ygb, yss=yss)
        for ti, (kind, t0, n) in enumerate(self.srcs):
            if kind == "s" or t0 >= self.own0:
                self._b2_tile(ti, kind, t0, n)
        P.dma("sync", self.st_out_p[0].rearrange("s q -> q s"), car[0][:], reads=["car"], writes=["stp0"], allow_slow_non_contiguous=True)
        P.dma("sync", self.st_out_p[1].rearrange("s q -> q s"), car[1][:], reads=["car"], writes=["stp1"], allow_slow_non_contiguous=True)
        for j in range(2):
            for b in range(NS):
                P.dma("sync", self.st_out_s[j][b].rearrange("(s q) -> q s", q=128), sts[j][:, b, :], reads=["sts"], writes=[("stso", j, b)], allow_slow_non_contiguous=True)

    def _b2_tile(self, ti, kind, t0, n):
        P = self.P
        B = self._b2
        T, PR, PIm, NPI, Bm, Cm, car = B["T"], B["PR"], B["PIm"], B["NPI"], B["Bm"], B["Cm"], B["car"]
        uT, pp, tmp, sbr, sbi = B["uT"], B["pp"], B["tmp"], B["sbr"], B["sbi"]
        bank = self.bank
        gt0 = t0 if kind == "p" else SEQ
        V = lambda fn, r, wr: P.op("vector", fn, reads=r, writes=wr)
        is_own = kind == "s" or t0 >= self.own0
        P.dma("sync", uT[:, :, :n], self.uT_s[:, :, gt0:gt0 + n].rearrange("k p n -> p k n"), reads=[("uT_s", ti)], writes=["uT"])
        for c in range(4):
            for s4 in range(4):
                s = 4 * c + s4
                zb = (bank[0], bank[1]) if s % 2 == 0 else (bank[4], bank[5])
                zk = (("bank", 0), ("bank", 1)) if s % 2 == 0 else (("bank", 4), ("bank", 5))
                for j in range(2):
                    P.op("tensor", lambda e, j=j, s=s, c=c, zb=zb: e.matmul(zb[j][:, :n], Bm[j][:, s, :], uT[:, c, :n], start=True, stop=True),
                         reads=["uT", "Bm_re", "Bm_im"], writes=[zk[j]])
                fre, fim = T["f_re"][:, s:s + 1], T["f_im"][:, s:s + 1]
                cur = pp[0]
                if kind == "p":
                    E, FE, zs, mm = B["E"], B["FE"], B["zs"], B["mm"]
                    G = lambda fn, r, wr: P.op("gpsimd", fn, reads=r, writes=wr)
                    V(lambda e, s=s, zb=zb: e.tensor_tensor(mm[0][:, :n], FE[0][:, s, :n], zb[0][:, :n], ALU.mult), ["FE", zk[0]], ["mm0"])
                    V(lambda e, s=s, zb=zb: e.tensor_tensor(mm[1][:, :n], FE[1][:, s, :n], zb[1][:, :n], ALU.mult), ["FE", zk[1]], ["mm1"])
                    G(lambda e: e.tensor_tensor(mm[0][:, :n], mm[0][:, :n], mm[1][:, :n], ALU.subtract), ["mm0", "mm1"], ["mm0"])
                    V(lambda e, s=s, zb=zb: e.tensor_tensor(mm[2][:, :n], FE[0][:, s, :n], zb[1][:, :n], ALU.mult), ["FE", zk[1]], ["mm2"])
                    V(lambda e, s=s, zb=zb: e.tensor_tensor(mm[3][:, :n], FE[1][:, s, :n], zb[0][:, :n], ALU.mult), ["FE", zk[0]], ["mm3"])
                    G(lambda e: e.tensor_tensor(mm[2][:, :n], mm[2][:, :n], mm[3][:, :n], ALU.add), ["mm2", "mm3"], ["mm2"])
                    rho = T["mag"][:, s:s + 1].broadcast_to([128, n])
                    V(lambda e, s=s, rho=rho: e.tensor_tensor_scan(pp[1][0][:, :n], rho, mm[0][:, :n], car[0][:, s:s + 1], ALU.mult, ALU.add),
                      ["mm0", "car", "mag"], ["pp10"])
                    V(lambda e, s=s, rho=rho: e.tensor_tensor_scan(pp[1][1][:, :n], rho, mm[2][:, :n], car[1][:, s:s + 1], ALU.mult, ALU.add),
                      ["mm2", "car", "mag"], ["pp11"])
                    wr_, wi_ = pp[1][0], pp[1][1]
                    if not is_own:
                        cl = slice(n - 1, n)
                        V(lambda e, s=s: e.tensor_tensor(mm[0][:, 0:1], wi_[:, cl], E[1][:, s, cl], ALU.mult), [("E", 1, s), "pp11"], ["mm0"])
                        V(lambda e, s=s: e.scalar_tensor_tensor(car[0][:, s:s + 1], wr_[:, cl], E[0][:, s, cl], mm[0][:, 0:1], ALU.mult, ALU.subtract),
                          [("E", 0, s), "pp10", "mm0"], ["car"])
                        V(lambda e, s=s: e.tensor_tensor(mm[2][:, 0:1], wr_[:, cl], E[1][:, s, cl], ALU.mult), [("E", 1, s), "pp10"], ["mm2"])
                        V(lambda e, s=s: e.scalar_tensor_tensor(car[1][:, s:s + 1], wi_[:, cl], E[0][:, s, cl], mm[2][:, 0:1], ALU.mult, ALU.add),
                          [("E", 0, s), "pp11", "mm2"], ["car"])
                        continue
                    G(lambda e, s=s: e.tensor_tensor(mm[0][:, :n], E[0][:, s, :n], wr_[:, :n], ALU.mult), [("E", 0, s), "pp10"], ["mm0"])
                    G(lambda e, s=s: e.tensor_tensor(mm[1][:, :n], E[1][:, s, :n], wi_[:, :n], ALU.mult), [("E", 1, s), "pp11"], ["mm1"])
                    G(lambda e: e.tensor_tensor(cur[0][:, :n], mm[0][:, :n], mm[1][:, :n], ALU.subtract), ["mm0", "mm1"], ["pp00"])
                    V(lambda e, s=s: e.tensor_tensor(mm[2][:, :n], E[0][:, s, :n], wi_[:, :n], ALU.mult), [("E", 0, s), "pp11"], ["mm2"])
                    V(lambda e, s=s: e.tensor_tensor(mm[3][:, :n], E[1][:, s, :n], wr_[:, :n], ALU.mult), [("E", 1, s), "pp10"], ["mm3"])
                    V(lambda e: e.tensor_tensor(cur[1][:, :n], mm[2][:, :n], mm[3][:, :n], ALU.add), ["mm2", "mm3"], ["pp01"])
                    ci = 0
                else:
                    V(lambda e, zb=zb, fim=fim: e.tensor_scalar(tmp[0][:, :n], zb[1][:, :n], fim, None, ALU.mult), [zk[1], "f_im"], ["mm2"])
                    V(lambda e, zb=zb, fre=fre, cur=cur: e.scalar_tensor_tensor(cur[0][:, :n], zb[0][:, :n], fre, tmp[0][:, :n], ALU.mult, ALU.subtract),
                      [zk[0], "f_re", "mm2"], ["pp00"])
                    V(lambda e, zb=zb, fim=fim: e.tensor_scalar(tmp[1][:, :n], zb[0][:, :n], fim, None, ALU.mult), [zk[0], "f_im"], ["mm3"])
                    V(lambda e, zb=zb, fre=fre, cur=cur: e.scalar_tensor_tensor(cur[1][:, :n], zb[1][:, :n], fre, tmp[1][:, :n], ALU.mult, ALU.add),
                      [zk[1], "f_re", "mm3"], ["pp01"])
                    a0, b0, nb0 = PR[:, 0, s:s + 1], PIm[:, 0, s:s + 1], NPI[:, 0, s:s + 1]
                    hr, hi = B["h0s"][0][:, :, s], B["h0s"][1][:, :, s]
                    w0 = n
                    V(lambda e, cur=cur, hr=hr, a0=a0: e.scalar_tensor_tensor(cur[0][:, :w0], hr, a0, cur[0][:, :w0], ALU.mult, ALU.add), ["pp00", "h0s", "PR"], ["pp00"])
                    V(lambda e, cur=cur, hi=hi, nb0=nb0: e.scalar_tensor_tensor(cur[0][:, :w0], hi, nb0, cur[0][:, :w0], ALU.mult, ALU.add), ["pp00", "h0s", "NPI"], ["pp00"])
                    V(lambda e, cur=cur, hi=hi, a0=a0: e.scalar_tensor_tensor(cur[1][:, :w0], hi, a0, cur[1][:, :w0], ALU.mult, ALU.add), ["pp01", "h0s", "PR"], ["pp01"])
                    V(lambda e, cur=cur, hr=hr, b0=b0: e.scalar_tensor_tensor(cur[1][:, :w0], hr, b0, cur[1][:, :w0], ALU.mult, ALU.add), ["pp01", "h0s", "PIm"], ["pp01"])
                    ci = 0
                X = pp[ci]
                kx = ["pp%d0" % ci, "pp%d1" % ci]
                if kind == "p":
                    P.op("scalar", lambda e, X=X, s=s: e.copy(car[0][:, s:s + 1], X[0][:, n - 1:n]), reads=[kx[0]], writes=["car"])
                    P.op("scalar", lambda e, X=X, s=s: e.copy(car[1][:, s:s + 1], X[1][:, n - 1:n]), reads=[kx[1]], writes=["car"])
                else:
                    P.op("scalar", lambda e, X=X, s=s: e.copy(B["sts"][0][:, :, s], X[0][:, :n]), reads=[kx[0]], writes=["sts"])
                    P.op("scalar", lambda e, X=X, s=s: e.copy(B["sts"][1][:, :, s], X[1][:, :n]), reads=[kx[1]], writes=["sts"])
                P.op("scalar", lambda e, X=X: e.copy(sbr[:, :n], X[0][:, :n]), reads=[kx[0]], writes=["sbr"])
                P.op("scalar", lambda e, X=X: e.mul(sbi[:, :n], X[1][:, :n], -1.0), reads=[kx[1]], writes=["sbi"])

                def mmy(e, s=s, s4=s4):
                    e.matmul(bank[7][:, :n], Cm[0][:, s, :], sbr[:, :n], start=(s4 == 0), stop=False)
                    return e.matmul(bank[7][:, :n], Cm[1][:, s, :], sbi[:, :n], start=False, stop=(s4 == 3))
                P.op("tensor", mmy, reads=["sbr", "sbi", "Cm_re", "Cm_im"], writes=[("bank", 7)])
            if not is_own:
                continue
            yraw, x2, tg, ygf, ygb = B["yraw"], B["x2"], B["tg"], B["ygf"], B["ygb"]
            V(lambda e, c=c: e.scalar_tensor_tensor(yraw[:, :n], uT[:, c, :n], B["dcol"][:, c:c + 1], bank[7][:, :n], ALU.mult, ALU.add),
              ["uT", "dcol", ("bank", 7)], ["yraw"])
            P.op("scalar", lambda e: e.activation(x2[:, :n], yraw[:, :n], AF.Square), reads=["yraw"], writes=["x2"])
            V(lambda e: e.tensor_scalar(x2[:, :n], x2[:, :n], 0.044715, 1.0, ALU.mult, ALU.add), ["x2"], ["x2"])
            V(lambda e: e.tensor_tensor(tg[:, :n], x2[:, :n], yraw[:, :n], ALU.mult), ["x2", "yraw"], ["tg"])
            P.op("scalar", lambda e: e.activation(tg[:, :n], tg[:, :n], AF.Sigmoid, scale=1.5957691216057308), reads=["tg"], writes=["tg"])
            V(lambda e, c=c: e.tensor_tensor(ygf[:, c, :n], yraw[:, :n], tg[:, :n], ALU.mult), ["tg", "yraw"], [("ygf", c)])
            P.op("gpsimd", lambda e, c=c: e.tensor_copy(ygb[:, c, :n], ygf[:, c, :n]), reads=[("ygf", c)], writes=[("ygb", c)])
        if not is_own:
            return
        wglu, yss = B["wglu"], B["yss"]
        for m in range(4):
            bg = bank[2 + m % 2]

            def mmg(e, m=m, bg=bg):
                last = None
                for k in range(4):
                    last = e.matmul(bg[:, :n], wglu[:, k, m * 128:(m + 1) * 128], ygb[:, k, :n], start=(k == 0), stop=(k == 3))
                return last
            P.op("tensor", mmg, reads=[("ygb", k) for k in range(4)] + ["wglu"], writes=[("bank", 2 + m % 2)])
            P.op("scalar", lambda e, m=m, bg=bg: e.activation(tg[:, :n], bg[:, :n], AF.Sigmoid, bias=B["bcol"][:, m:m + 1]),
                 reads=[("bank", 2 + m % 2), "bcol"], writes=["tg"])
            V(lambda e, m=m: e.tensor_tensor(yss[:, m, :n], ygf[:, m, :n], tg[:, :n], ALU.mult), ["tg", ("ygf", m)], [("yss", m)])
        P.dma("sync", self.yssT_s[:, :, gt0:gt0 + n].rearrange("k p n -> p k n"), yss[:, :, :n],
              reads=[("yss", m) for m in range(4)], writes=[("yssT_s", ti)])

    def phase_b3(self):
        P, w = self.P, self.w
        tl = self.sbp
        A = {}
        A["mask"] = [tl("mask_cur", [128, 256], BF16), tl("mask_prev", [128, 256], BF16), tl("mask_halo", [128, 256], BF16)]
        P.dma("gpsimd", A["mask"][0][:], w["mask_cur"][:, :], writes=["mask"])
        P.dma("gpsimd", A["mask"][1][:], w["mask_prev"][:, :], writes=["mask"])
        P.dma("gpsimd", A["mask"][2][:], w["mask_halo"][:, :], writes=["mask"])
        A["identB"] = tl("identB", [128, 128], BF16)
        P.dma("gpsimd", A["identB"][:], w["ident"][:, :], writes=["identB"])
        A["kv"] = [tl("kvA%d" % i, [128, 512], F32) for i in range(2)]
        A["qf"] = [tl("qf%d" % i, [128, 256], F32) for i in range(2)]
        A["kT"] = [tl("kT%d" % i, [128, 2, 128], BF16) for i in range(3)]
        A["Vz"] = [tl("Vz%d" % i, [128, 4, 128], BF16) for i in range(3)]
        A["qTz"] = [[tl("qTz%d%d" % (i, hh), [128, 2, 128], BF16) for hh in range(2)] for i in range(2)]
        A["onesZ"] = [tl("onesZ%d" % hh, [128, 128], BF16) for hh in range(2)]
        for i in range(3):
            P.op("vector", lambda e, i=i: e.memset(A["Vz"][i][:], 0.0), writes=[("Vb", i)])
        for i in range(2):
            for hh in range(2):
                P.op("vector", lambda e, i=i, hh=hh: e.memset(A["qTz"][i][hh][:], 0.0), writes=[("qT", i)])
        for hh in range(2):
            P.op("vector", lambda e, hh=hh: e.memset(A["onesZ"][hh][:], 0.0), writes=["onesZ"])
            P.op("vector", lambda e, hh=hh: e.memset(A["onesZ"][hh][:, 64 * hh:64 * hh + 64], 1.0), reads=["onesZ"], writes=["onesZ"])
        A["Pe"] = [tl("Pe%d" % i, [128, 512], BF16) for i in range(2)]
        A["Pm"] = [tl("Pm%d" % i, [128, 512], BF16) for i in range(2)]
        BLK = 2048
        A["accN"] = tl("accN", [128, 2, BLK], F32)
        A["accD"] = tl("accD", [128, 2, BLK], F32)
        A["yat"] = tl("yat", [128, 2, BLK], BF16)
        self._b3 = A
        self.pi = 0
        self.ui = 0
        V = lambda fn, r, wr: P.op("vector", fn, reads=r, writes=wr)
        groups = ((128, 1), (512, 4), (2048, 16))
        nblk = SEQ // BLK
        for bb in range(nblk - 1, nblk):
            V(lambda e: e.memset(A["accN"][:], 0.0), [], ["accN"])
            V(lambda e: e.memset(A["accD"][:], 0.0), [], ["accD"])
            for g, (W, dl) in enumerate(groups):
                span = 128 * dl
                for r in range(dl):
                    prev = None
                    for bk in range(BLK // span):
                        base = bb * BLK + bk * span
                        if prev is None and base >= span:
                            prev = self._b3_prep(g, base - span, r, dl, 128)
                        cur = self._b3_prep(g, base, r, dl, 128)
                        qT = self._b3_q(g, base, r, dl, 128)
                        pm = 2 if (base - span) < self.own0 else 1
                        ksets = [(cur, 128, 0)] + ([(prev, 128, pm)] if prev is not None else [])
                        c0 = bk * span + r
                        views = lambda acc, pr, c0=c0, dl=dl, span=span: acc[:, pr, c0:c0 + 127 * dl + 1:dl] if dl > 1 else acc[:, pr, c0:c0 + 128]
                        self._b3_unit(qT, 128, ksets, views)
                        prev = cur
            self._b3_norm(bb * BLK, BLK, A["accN"], A["accD"], A["yat"])
        V(lambda e: e.memset(A["accN"][:], 0.0), [], ["accN"])
        V(lambda e: e.memset(A["accD"][:], 0.0), [], ["accD"])
        for b in range(NS):
            for g, (W, dl) in enumerate(groups):
                cset = self._b3_prep(g, None, 0, dl, 128, cache=(g, b))
                sset = self._b3_prep(g, SEQ + b, 0, 1, 1)
                qT = self._b3_q(g, SEQ + b, 0, 1, 1)
                views = lambda acc, pr, b=b: acc[:, pr, b:b + 1]
                self._b3_unit(qT, 1, [(cset, 128, None), (sset, 1, None)], views)
        self._b3_norm(SEQ, NS, A["accN"], A["accD"], A["yat"])

    def _b3_prep(self, g, base, r, dl, nk, cache=None):
        P, A, bank = self.P, self._b3, self.bank
        i = self.pi % 3
        j = self.pi % 2
        self.pi += 1
        kv, kT, Vz = A["kv"][j], A["kT"][i], A["Vz"][i]
        if cache is not None:
            cg, b = cache
            W = (128, 512, 2048)[cg]
            src = self.cache[cg][b, :, :].rearrange("(m d) f -> d m f", d=dl)[0]
            rd = []
        elif dl > 1:
            src = self.kv_s[g][base:base + 128 * dl, :].rearrange("(m d) f -> d m f", d=dl)[r]
            rd = [("kv_s", g, ti) for ti in range(len(self.srcs))]
        else:
            src = self.kv_s[g][base:base + nk, :]
            rd = [("kv_s", g, ti) for ti in range(len(self.srcs))]
        P.dma("sync", kv[:nk, :], src, reads=rd, writes=[("kvA", j)])
        tb = bank[6 + j]
        for pr in range(2):
            P.op("tensor", lambda e, pr=pr: e.transpose(tb[:, pr * 128:pr * 128 + nk], kv[:nk, pr * 128:(pr + 1) * 128], self.identF[:nk, :nk]),
                 reads=[("kvA", j), "identF"], writes=[("bank", 6 + j)])
        P.op("scalar", lambda e: e.copy(kT[:, :, :nk], tb[:, 0:256].rearrange("p (a n) -> p a n", a=2)[:, :, :nk]),
             reads=[("bank", 6 + j)], writes=[("kT", i)])
        v4 = kv[:nk, 256:512].rearrange("p (h d) -> p h d", d=64)
        P.op("gpsimd", lambda e: e.tensor_copy(Vz[:nk, 0::2, 0:64], v4[:, 0::2, :]), reads=[("kvA", j)], writes=[("Vb", i)])
        P.op("gpsimd", lambda e: e.tensor_copy(Vz[:nk, 1::2, 64:128], v4[:, 1::2, :]), reads=[("kvA", j)], writes=[("Vb", i)])
        return i

    def _b3_q(self, g, base, r, dl, nq):
        P, A, bank = self.P, self._b3, self.bank
        j = self.ui % 2
        qf, qTz = A["qf"][j], A["qTz"][j]
        if dl > 1:
            src = self.q_s[g][base:base + 128 * dl, :].rearrange("(m d) f -> d m f", d=dl)[r]
        else:
            src = self.q_s[g][base:base + nq, :]
        P.dma("sync", qf[:nq, :], src, reads=[("q_s", g, ti) for ti in range(len(self.srcs))], writes=[("qf", j)])
        tb = bank[6 + j]
        for pr in range(2):
            P.op("tensor", lambda e, pr=pr: e.transpose(tb[:, 256 + pr * 128:256 + pr * 128 + nq], qf[:nq, pr * 128:(pr + 1) * 128], self.identF[:nq, :nq]),
                 reads=[("qf", j), "identF"], writes=[("bank", 6 + j)])
        P.op("vector", lambda e: e.tensor_copy(qTz[0][0:64, :, :nq], tb[0:64, 256:512].rearrange("p (a n) -> p a n", a=2)[:, :, :nq]),
             reads=[("bank", 6 + j)], writes=[("qT", j)])
        P.op("vector", lambda e: e.tensor_copy(qTz[1][64:128, :, :nq], tb[64:128, 256:512].rearrange("p (a n) -> p a n", a=2)[:, :, :nq]),
             reads=[("bank", 6 + j)], writes=[("qT", j)])
        return j

    def _b3_unit(self, qi, nq, ksets, views):
        P, A, bank = self.P, self._b3, self.bank
        qTz = A["qTz"][qi]
        V = lambda fn, r, wr: P.op("vector", fn, reads=r, writes=wr)
        for pr in range(2):
            u = self.ui
            self.ui += 1
            j = u % 2
            bS, bN, bD = bank[0 + j], bank[2 + j], bank[4 + j]
            kS, kN, kD = ("bank", j), ("bank", 2 + j), ("bank", 4 + j)
            Pe, Pm = A["Pe"][j], A["Pm"][j]

            def mms(e, pr=pr):
                last = None
                for si, (ki, nk, mk) in enumerate(ksets):
                    kT = A["kT"][ki]
                    for hh in range(2):
                        slot = si * 2 + hh
                        last = e.matmul(bS[:nk, slot * nq:(slot + 1) * nq], kT[:, pr, :nk],
                                        qTz[hh][:, pr, :nq], start=True, stop=True)
                return last
            P.op("tensor", mms, reads=[("kT", ki) for ki, _, _ in ksets] + [("qT", qi)], writes=[kS])
            for si, (ki, nk, mk) in enumerate(ksets):
                lo, hi = si * 2 * nq, (si + 1) * 2 * nq
                P.op("scalar", lambda e, nk=nk, lo=lo, hi=hi: e.activation(Pe[:nk, lo:hi], bS[:nk, lo:hi], AF.Exp),
                     reads=[kS], writes=[("Pe", j, si)])
                if mk is not None:
                    P.op("gpsimd", lambda e, nk=nk, lo=lo, hi=hi, mk=mk: e.tensor_tensor(Pm[:nk, lo:hi], Pe[:nk, lo:hi], A["mask"][mk][:nk, :], ALU.mult),
                         reads=[("Pe", j, si), "mask"], writes=[("Pm", j, si)])
                else:
                    P.op("gpsimd", lambda e, nk=nk, lo=lo, hi=hi: e.tensor_copy(Pm[:nk, lo:hi], Pe[:nk, lo:hi]),
                         reads=[("Pe", j, si)], writes=[("Pm", j, si)])

            def mmav(e, pr=pr):
                last = None
                ns = len(ksets)
                tot = 2 * ns
                for lhs_of, bO in ((lambda ki, nk, hh: A["Vz"][ki][:nk, 2 * pr + hh, :], bN), (lambda ki, nk, hh: A["onesZ"][hh][:nk, :], bD)):
                    c = 0
                    for hh in range(2):
                        for si, (ki, nk, mk) in enumerate(ksets):
                            slot = si * 2 + hh
                            last = e.matmul(bO[:, :nq], lhs_of(ki, nk, hh), Pm[:nk, slot * nq:(slot + 1) * nq],
                                            start=(c == 0), stop=(c == tot - 1))
                            c += 1
                return last
            P.op("tensor", mmav, reads=[("Vb", ki) for ki, _, _ in ksets] + [("Pm", j, si) for si in range(len(ksets))] + ["onesZ"],
                 writes=[kN, kD])
            vn, vd = views(A["accN"], pr), views(A["accD"], pr)
            V(lambda e, vn=vn: e.tensor_tensor(vn, vn, bN[:, :nq], ALU.add), [kN, "accN"], ["accN"])
            V(lambda e, vd=vd: e.tensor_tensor(vd, vd, bD[:, :nq], ALU.add), [kD, "accD"], ["accD"])

    def _b3_norm(self, col0, n, accN, accD, yat):
        P = self.P
        V = lambda fn, r, wr: P.op("vector", fn, reads=r, writes=wr)
        V(lambda e: e.reciprocal(accD[:, :, :n], accD[:, :, :n]), ["accD"], ["accD"])
        V(lambda e: e.tensor_tensor(yat[:, :, :n], accN[:, :, :n], accD[:, :, :n], ALU.mult), ["accN", "accD"], ["yat"])
        P.dma("sync", self.yatT_s[:, :, col0:col0 + n].rearrange("k p n -> p k n"), yat[:, :, :n], reads=["yat"], writes=[("yatT_s", col0)])

    def phase_b4(self):
        P, w = self.P, self.w
        tl = self.sbp
        wing = tl("wing", [128, 8, 2048], BF16)
        for k in range(8):
            P.dma("gpsimd", wing[:, k, :], w["w_in"][k * 128:(k + 1) * 128, 2816:4864], writes=[("wing", k)], max_dma_last_dim=4096)
        wsp, wap, wo = tl("wsp", [128, 4, D], BF16), tl("wap", [128, 2, D], BF16), tl("wo", [128, 8, D], BF16)
        P.dma("gpsimd", wsp[:], w["w_ssm_proj"].rearrange("(k p) f -> p k f", p=128), writes=["wsp"])
        P.dma("gpsimd", wap[:], w["w_attn_proj"].rearrange("(k p) f -> p k f", p=128), writes=["wap"])
        P.dma("gpsimd", wo[:], w["w_o"].rearrange("(k p) f -> p k f", p=128), writes=["wo"])
        B = dict(wing=wing, wsp=wsp, wap=wap, wo=wo,
                 hT=tl("hT4", [128, 8, ST], BF16), yss=tl("yss4", [128, 4, ST], BF16), yat=tl("yat4", [128, 2, ST], BF16),
                 xT=tl("xT4", [128, 8, ST], F32), mixed=tl("mixed", [128, 8, ST], BF16),
                 sg=[tl("sg%d" % i, [128, ST], F32) for i in range(2)], tm=[tl("tm%d" % i, [128, ST], F32) for i in range(2)])
        self._b4 = B
        for ti, (kind, t0, n) in enumerate(self.srcs):
            if ti in self.own_tis:
                self._b4_tile(ti, kind, t0, n)

    def _b4_tile(self, ti, kind, t0, n):
        P, B, bank = self.P, self._b4, self.bank
        gt0 = t0 if kind == "p" else SEQ
        hT, yss, yat, xT, mixed, sg, tm = B["hT"], B["yss"], B["yat"], B["xT"], B["mixed"], B["sg"], B["tm"]
        wing, wsp, wap, wo = B["wing"], B["wsp"], B["wap"], B["wo"]
        V = lambda fn, r, wr: P.op("vector", fn, reads=r, writes=wr)
        fm = lambda t: t[:, :, gt0:gt0 + n].rearrange("k p n -> p k n")
        P.dma("sync", hT[:, :, :n], fm(self.hT_s), writes=["hT4"])
        P.dma("sync", yss[:, :, :n], fm(self.yssT_s), writes=["yss4"])
        P.dma("sync", yat[:, :, :n], fm(self.yatT_s), writes=["yat4"])
        P.dma("sync", xT[:, :, :n], fm(self.x1T), reads=[("x1T", ti)], writes=["xT4"])
        wk = [("wing", k) for k in range(8)]
        for m in range(8):
            for br, (src, nk_, wp, key, coff) in enumerate(((yss, 4, wsp, "yss4", 0), (yat, 2, wap, "yat4", 1024))):
                bP, bG = bank[2 * br], bank[2 * br + 1]
                kP, kG = ("bank", 2 * br), ("bank", 2 * br + 1)

                def mmp(e, m=m, src=src, nk_=nk_, wp=wp, bP=bP):
                    last = None
                    for k in range(nk_):
                        last = e.matmul(bP[:, :n], wp[:, k, m * 128:(m + 1) * 128], src[:, k, :n], start=(k == 0), stop=(k == nk_ - 1))
                    return last

                def mmg(e, m=m, coff=coff, bG=bG):
                    last = None
                    for k in range(8):
                        last = e.matmul(bG[:, :n], wing[:, k, coff + m * 128:coff + (m + 1) * 128], hT[:, k, :n], start=(k == 0), stop=(k == 7))
                    return last
                P.op("tensor", mmp, reads=[key, "wsp", "wap"], writes=[kP])
                P.op("tensor", mmg, reads=["hT4"] + wk, writes=[kG])
                P.op("scalar", lambda e, br=br, bG=bG: e.activation(sg[br][:, :n], bG[:, :n], AF.Sigmoid), reads=[kG], writes=[("sg", br)])
                V(lambda e, br=br, bP=bP: e.tensor_tensor(tm[br][:, :n], sg[br][:, :n], bP[:, :n], ALU.mult), [("sg", br), kP], [("tm", br)])
            P.op("gpsimd", lambda e, m=m: e.tensor_tensor(mixed[:, m, :n], tm[0][:, :n], tm[1][:, :n], ALU.add),
                 reads=[("tm", 0), ("tm", 1)], writes=[("mixed", m)])
        for m in range(8):
            bo = bank[4 + m % 2]

            def mmo(e, m=m, bo=bo):
                last = None
                for k in range(8):
                    last = e.matmul(bo[:, :n], wo[:, k, m * 128:(m + 1) * 128], mixed[:, k, :n], start=(k == 0), stop=(k == 7))
                return last
            P.op("tensor", mmo, reads=[("mixed", k) for k in range(8)] + ["wo"], writes=[("bank", 4 + m % 2)])
            V(lambda e, m=m, bo=bo: e.tensor_tensor(xT[:, m, :n], xT[:, m, :n], bo[:, :n], ALU.add), [("bank", 4 + m % 2), "xT4"], ["xT4"])
        P.dma("sync", fm(self.x1T), xT[:, :, :n], reads=["xT4"], writes=[("x1T", ti)])

    def load_ffn_weights(self, tag, g, wgate, wup, wdown):
        P = self.P
        P.dma("sync", self.gcol[:], g.rearrange("(k p) -> p k", p=128), writes=["gcol"], allow_slow_non_contiguous=True)
        for k in range(8):
            P.dma("gpsimd", self.wg[:, k, :], wgate[k * 128:(k + 1) * 128, :], writes=[("wg", k)], max_dma_last_dim=4096)
            P.dma("gpsimd", self.wu[:, k, :], wup[k * 128:(k + 1) * 128, :], writes=[("wu", k)], max_dma_last_dim=4096)
        for f in range(NF):
            P.dma("gpsimd", self.wd[:, f, :], wdown[f * 128:(f + 1) * 128, :], writes=[("wd", f)], max_dma_last_dim=4096)

    def ffn_phase(self, tag, g, wgate, wup, wdown, srcs, src_tok, src_T, dst_T, dst_tok, only=None):
        P, nc = self.P, self.nc
        self.load_ffn_weights(tag, g, wgate, wup, wdown)
        xT, sq, hT, rstd, aT = self.xT, self.sq, self.hT, self.rstd, self.aT
        bank = self.bank
        for ti, (kind, t0, n) in enumerate(srcs):
            if only is not None and ti not in only:
                continue
            self._ffn_tile(ti, kind, t0, n, src_tok, src_T, dst_T, dst_tok)

    def _ffn_tile(self, ti, kind, t0, n, src_tok, src_T, dst_T, dst_tok):
        P, nc = self.P, self.nc
        xT, sq, hT, rstd, aT = self.xT, self.sq, self.hT, self.rstd, self.aT
        bank = self.bank
        if True:
            gt0 = t0 if kind == "p" else SEQ
            nsub = (n + 127) // 128
            if src_tok is not None:
                src = src_tok[0] if kind == "p" else src_tok[1]
                for s in range(nsub):
                    r = min(128, n - s * 128)
                    xin = self.xin[s % 2]
                    P.dma("sync", xin[:r, :], src[t0 + s * 128: t0 + s * 128 + r, :], writes=[("xin", s % 2)])
                    for k in range(8):
                        b = bank[k]
                        P.op("tensor", lambda e, b=b, xin=xin, k=k, s=s, r=r: e.transpose(
                            b[:, s * 128: s * 128 + r], xin[:r, k * 128:(k + 1) * 128], self.identF[:r, :r]),
                            reads=[("xin", s % 2), "identF"], writes=[("bank", k)])
                for k in range(8):
                    P.op("scalar" if k % 2 else "vector",
                         (lambda e, k=k: e.copy(xT[:, k, :n], bank[k][:, :n])) if k % 2 else
                         (lambda e, k=k: e.tensor_copy(xT[:, k, :n], bank[k][:, :n])),
                         reads=[("bank", k)], writes=[("xT", k)])
            else:
                P.dma("sync", xT[:, :, :n], src_T[:, :, gt0:gt0 + n].rearrange("k p n -> p k n"),
                      writes=[("xT", k) for k in range(8)])
            for k in range(8):
                P.op("scalar", lambda e, k=k: e.activation(sq[:, k, :n], xT[:, k, :n], AF.Square),
                     reads=[("xT", k)], writes=[("sq", k)])

            def nrm(e):
                last = None
                for k in range(8):
                    last = e.matmul(bank[6][:, :n], self.onesB[:], sq[:, k, :n], start=(k == 0), stop=(k == 7))
                return last
            P.op("tensor", nrm, reads=[("sq", k) for k in range(8)] + ["onesB"], writes=[("bank", 6)])
            P.op("scalar", lambda e: e.activation(rstd[:, :n], bank[6][:, :n], AF.Sqrt, bias=self.epsC[:], scale=1.0 / D),
                 reads=[("bank", 6), "epsC"], writes=["rstd"])
            P.op("vector", lambda e: e.reciprocal(rstd[:, :n], rstd[:, :n]), reads=["rstd"], writes=["rstd"])
            for k in range(8):
                P.op("vector", lambda e, k=k: e.scalar_tensor_tensor(
                    hT[:, k, :n], xT[:, k, :n], self.gcol[:, k:k + 1], rstd[:, :n], ALU.mult, ALU.mult),
                    reads=[("xT", k), "gcol", "rstd"], writes=[("hT", k)])
            for f in range(NF):
                bg = bank[f % 2]
                bu = bank[2 + f % 2]
                sil = self.sil[f % 2]

                def mmg(e, f=f, bg=bg):
                    last = None
                    for k in range(8):
                        last = e.matmul(bg[:, :n], self.wg[:, k, f * 128:(f + 1) * 128], hT[:, k, :n], start=(k == 0), stop=(k == 7))
                    return last

                def mmu(e, f=f, bu=bu):
                    last = None
                    for k in range(8):
                        last = e.matmul(bu[:, :n], self.wu[:, k, f * 128:(f + 1) * 128], hT[:, k, :n], start=(k == 0), stop=(k == 7))
                    return last
                hk = [("hT", k) for k in range(8)]
                P.op("tensor", mmg, reads=hk + [("wg", k) for k in range(8)], writes=[("bank", f % 2)])
                P.op("tensor", mmu, reads=hk + [("wu", k) for k in range(8)], writes=[("bank", 2 + f % 2)])
                P.op("scalar", lambda e, bg=bg, sil=sil: e.activation(sil[:, :n], bg[:, :n], AF.Silu),
                     reads=[("bank", f % 2)], writes=[("sil", f % 2)])
                P.op("vector", lambda e, f=f, bu=bu, sil=sil: e.tensor_tensor(aT[:, f, :n], sil[:, :n], bu[:, :n], ALU.mult),
                     reads=[("sil", f % 2), ("bank", 2 + f % 2)], writes=[("aT", f)])
            for m in range(8):
                bd = bank[4 + m % 2]

                def mmd(e, m=m, bd=bd):
                    last = None
                    for f in range(NF):
                        last = e.matmul(bd[:, :n], self.wd[:, f, m * 128:(m + 1) * 128], aT[:, f, :n], start=(f == 0), stop=(f == NF - 1))
                    return last
                P.op("tensor", mmd, reads=[("aT", f) for f in range(NF)] + [("wd", f) for f in range(NF)], writes=[("bank", 4 + m % 2)])
                P.op("vector", lambda e, m=m, bd=bd: e.scalar_tensor_tensor(
                    xT[:, m, :n], bd[:, :n], 0.5, xT[:, m, :n], ALU.mult, ALU.add),
                    reads=[("bank", 4 + m % 2), ("xT", m)], writes=[("xT", m)])
            if dst_T is not None:
                P.dma("sync", dst_T[:, :, gt0:gt0 + n].rearrange("k p n -> p k n"), xT[:, :, :n],
                      reads=[("xT", k) for k in range(8)], writes=[("x1T", ti)])
            if dst_tok is not None:
                dst = dst_tok[0] if kind == "p" else dst_tok[1]
                for s in range(nsub):
                    r = min(128, n - s * 128)
                    xo = self.xin[s % 2]
                    for k in range(8):
                        bk = bank[6 + (k // 4)]
                        P.op("tensor", lambda e, bk=bk, k=k, s=s, r=r: e.transpose(
                            bk[:r, (k % 4) * 128:(k % 4 + 1) * 128], xT[:, k, s * 128: s * 128 + r], self.identF[:, :]),
                            reads=[("xT", k), "identF"], writes=[("bank", 6 + k // 4)])
                    P.op("vector", lambda e, xo=xo, r=r: e.tensor_copy(xo[:r, 0:512], bank[6][:r, :]),
                         reads=[("bank", 6)], writes=[("xin", s % 2)])
                    P.op("scalar", lambda e, xo=xo, r=r: e.copy(xo[:r, 512:1024], bank[7][:r, :]),
                         reads=[("bank", 7)], writes=[("xin", s % 2)])
                    o0 = (t0 - self.own0) if kind == "p" else t0
                    P.dma("sync", dst[o0 + s * 128: o0 + s * 128 + r, :], xo[:r, :], reads=[("xin", s % 2)],
                          writes=[("yout", kind, t0, s)])


_CACHE = {}
WINS = (128, 512, 2048)


def make_in_maps(inp, seq=None):
    seq = SEQ if seq is None else seq
    ident = np.eye(128, dtype=np.float32)
    caches = [inp["cache_kv_w%d" % W][0].reshape(32, W, 512) for W in WINS]
    tvals = np.ascontiguousarray(np.tile(np.arange(1, ST + 1, dtype=np.float32)[None, :], (128, 1)))
    kk, qq = np.meshgrid(np.arange(128), np.arange(128), indexing="ij")
    mask_cur = np.ascontiguousarray(np.tile((kk <= qq).astype(np.float32), (1, 2)))
    mask_prev = np.ascontiguousarray(np.tile((kk >= qq).astype(np.float32), (1, 2)))
    ssm_mats = {nm: np.zeros((16, 128, 128), np.float32) for nm in ("Bm_re", "Bm_im", "Cm_re", "Cm_im")}
    for s_ in range(16):
        for gi in range(2):
            g = 2 * s_ + gi
            r0 = 32 * (s_ % 4) + 16 * gi
            for nm, src in (("Bm_re", "ssm_b_re"), ("Bm_im", "ssm_b_im")):
                ssm_mats[nm][s_, r0:r0 + 16, 64 * gi:64 * gi + 64] = inp[src][0][g].T
            for nm, src in (("Cm_re", "ssm_c_re"), ("Cm_im", "ssm_c_im")):
                ssm_mats[nm][s_, 64 * gi:64 * gi + 64, r0:r0 + 16] = inp[src][0][g].T
    in_maps = []
    for c in range(NCORES):
        m = {}
        b_, q_ = c // 4, c % 4
        xw = np.zeros((seq, D), np.float32)
        nreal = QTR * (q_ + 1)
        xw[seq - nreal:] = inp["x_prompt"][b_][:nreal]
        m["xp"] = xw
        m["mask_halo"] = mask_prev if q_ > 0 else np.zeros_like(mask_prev)
        m["xs"] = np.ascontiguousarray(inp["x_sample"][c * NS:(c + 1) * NS, 0, :])
        for nm in ("g_ffn1", "w1_gate", "w1_up", "w1_down", "g_ffn2", "w2_gate", "w2_up", "w2_down", "g_mix", "w_in"):
            m[nm] = np.ascontiguousarray(inp[nm][0])
        m["g_q"] = np.ascontiguousarray(inp["g_q"][0].reshape(1, 192))
        m["g_k"] = np.ascontiguousarray(inp["g_k"][0].reshape(1, 192))
        m["ident"] = ident
        m["ssm_a_re"] = np.ascontiguousarray(inp["ssm_a_re"][0].reshape(16, 128))
        m["ssm_a_im"] = np.ascontiguousarray(inp["ssm_a_im"][0].reshape(16, 128))
        m["ssm_ldt"] = np.ascontiguousarray(np.repeat(inp["ssm_log_dt"][0].reshape(16, 2), 64, axis=1))
        for nm in ("Bm_re", "Bm_im", "Cm_re", "Cm_im"):
            m[nm] = ssm_mats[nm]
        m["BmT_re"] = np.ascontiguousarray(ssm_mats["Bm_re"].transpose(0, 2, 1))
        m["BmT_im"] = np.ascontiguousarray(ssm_mats["Bm_im"].transpose(0, 2, 1))
        m["ssm_d"] = np.ascontiguousarray(inp["ssm_d"][0])
        m["w_glu"] = np.ascontiguousarray(inp["w_glu"][0])
        m["b_glu"] = np.ascontiguousarray(inp["b_glu"][0])
        m["tvals"] = tvals
        m["mask_cur"] = mask_cur
        m["mask_prev"] = mask_prev
        for nm in ("w_ssm_proj", "w_attn_proj", "w_o"):
            m[nm] = np.ascontiguousarray(inp[nm][0])
        m["st_re"] = np.ascontiguousarray(inp["state_ssm_re"][0][c * NS:(c + 1) * NS].reshape(NS, 2048))
        m["st_im"] = np.ascontiguousarray(inp["state_ssm_im"][0][c * NS:(c + 1) * NS].reshape(NS, 2048))
        for g, W in enumerate(WINS):
            m["c%d" % W] = np.ascontiguousarray(caches[g][c * NS:(c + 1) * NS])
        in_maps.append(m)
    return in_maps


def kernel(**inputs):
    inp = {k: np.asarray(v) for k, v in inputs.items()}
    kb = _CACHE.get("k")
    if kb is None:
        kb = K()
        kb.build()
        _CACHE["k"] = kb
    in_maps = make_in_maps(inp)
    res = run_bass_kernel_spmd(kb.nc, in_maps, core_ids=list(range(NCORES))).results
    y_p = np.stack([np.concatenate([res[4 * b + q]["yp"] for q in range(4)]) for b in range(2)])
    y_s = np.concatenate([res[c]["ys"] for c in range(NCORES)])[:, None, :]
    outs = [y_p, y_s]
    for W in WINS:
        outs.append(np.stack([res[4 * b + 3]["kvp%d" % W] for b in range(2)]).reshape(1, 2, W, 2, 4, 64))
    outs.append(np.stack([res[4 * b + 3]["stp_re"] for b in range(2)]).reshape(1, 2, 32, 64))
    outs.append(np.stack([res[4 * b + 3]["stp_im"] for b in range(2)]).reshape(1, 2, 32, 64))
    for W in WINS:
        outs.append(np.concatenate([res[c]["kvs%d" % W] for c in range(NCORES)]).reshape(1, 32, W, 2, 4, 64))
    outs.append(np.concatenate([res[c]["sts_re"] for c in range(NCORES)]).reshape(1, 32, 32, 64))
    outs.append(np.concatenate([res[c]["sts_im"] for c in range(NCORES)]).reshape(1, 32, 32, 64))
    return tuple(outs)
```
